# Optimizing a Trainium2 kernel written in Bass

```python
import jax, jax.numpy as jnp
from jax import lax
import numpy as np

D_MODEL = 2048
BATCH = 4
SEQ = 8192
DEPTH = 2

MEM_LEN = 256
M_HEADS = 4
M_HEAD_DIM = D_MODEL // 8
M_WIDTH = M_HEADS * M_HEAD_DIM
M_CONV = 4
M_CHUNK = 64
N_HEADS = 16
N_KV_HEADS = 4
N_HEAD_DIM = D_MODEL // 32
N_WIDTH = N_HEADS * N_HEAD_DIM
N_KV_WIDTH = N_KV_HEADS * N_HEAD_DIM
CMP_LEN = 32
CMP_STRIDE = 16
CMP_HIDDEN = 2 * N_HEAD_DIM
SEL_BLOCK = 64
SEL_TOP = 16
WINDOW = 512
N_QBLOCK = 64
FORCE_SCORE = 1.0e4
C_HEADS = 4
C_HEAD_DIM = D_MODEL // 8
C_WIDTH = C_HEADS * C_HEAD_DIM
D_FF = -((-8 * D_MODEL) // (3 * 256)) * 256
ROPE_THETA = 10000.0
EPS = 1e-6
IN_SIZES = (2 * M_WIDTH, M_WIDTH, M_WIDTH, M_HEADS, M_HEADS,
            N_WIDTH, N_KV_WIDTH, N_KV_WIDTH, N_KV_WIDTH, N_KV_WIDTH, N_KV_WIDTH, N_KV_WIDTH, 3 * N_HEADS,
            C_WIDTH, D_MODEL, D_MODEL, D_MODEL)
D_IN = 4 * M_WIDTH + 2 * M_HEADS + N_WIDTH + 6 * N_KV_WIDTH + 3 * N_HEADS + C_WIDTH + 3 * D_MODEL

kernel_name = 'hybrid_mlstm_nsa_memory_block'


def rms_norm(x, g):
    xf = x.astype(jnp.float32)
    y = xf * lax.rsqrt(jnp.mean(xf * xf, axis=-1, keepdims=True) + EPS)
    return (y * g.astype(jnp.float32)).astype(x.dtype)


def rope(x, pos):
    hd = x.shape[-1]
    half = hd // 2
    inv = jnp.power(ROPE_THETA, -jnp.arange(half, dtype=jnp.float32) * 2.0 / hd)
    ang = pos.astype(jnp.float32)[:, None] * inv[None, :]
    cos = jnp.cos(ang)[:, None, :]
    sin = jnp.sin(ang)[:, None, :]
    xf = x.astype(jnp.float32)
    x1, x2 = xf[..., :half], xf[..., half:]
    return jnp.concatenate([x1 * cos - x2 * sin, x2 * cos + x1 * sin], axis=-1).astype(x.dtype)


def masked_softmax(s, mask):
    s = jnp.where(mask, s.astype(jnp.float32), -jnp.inf)
    m = jnp.max(s, axis=-1, keepdims=True)
    m = jnp.where(jnp.isfinite(m), m, 0.0)
    e = jnp.exp(s - m)
    d = jnp.sum(e, axis=-1, keepdims=True)
    return e / jnp.where(d > 0, d, 1.0)


def split_cols(z, sizes):
    offs = np.cumsum(np.array(sizes))[:-1].tolist()
    return jnp.split(z, offs, axis=-1)


def causal_dwconv(u, w):
    k_taps = w.shape[0]
    s_len = u.shape[1]
    up = jnp.pad(u, ((0, 0), (k_taps - 1, 0), (0, 0)))
    return sum(up[:, j:j + s_len] * w[j] for j in range(k_taps))


def mlstm_chunkwise(q, k, v, ig, lf):
    B, S, H, dk = q.shape
    dv = v.shape[-1]
    L = M_CHUNK
    nch = S // L

    def to_chunks(a):
        a = a.astype(jnp.float32).reshape((B, nch, L, H) + a.shape[3:])
        return jnp.moveaxis(a, (1, 3), (0, 2))

    xs = (to_chunks(q), to_chunks(k), to_chunks(v), to_chunks(ig), to_chunks(lf))
    tri = jnp.tril(jnp.ones((L, L), dtype=bool))

    def step(carry, inp):
        C, n, m = carry
        qc, kc, vc, ic, fc = inp
        b = jnp.cumsum(fc, axis=-1)
        log_d = jnp.where(tri, b[..., :, None] - b[..., None, :] + ic[..., None, :], -jnp.inf)
        log_inter = b + m[..., None]
        m_t = jnp.maximum(log_inter, jnp.max(log_d, axis=-1))
        w_intra = jnp.exp(log_d - m_t[..., None])
        w_inter = jnp.exp(log_inter - m_t)
        s = jnp.einsum('bhtd,bhsd->bhts', qc, kc) * w_intra
        num = jnp.einsum('bhts,bhsv->bhtv', s, vc) + w_inter[..., None] * jnp.einsum('bhtd,bhdv->bhtv', qc, C)
        den = jnp.sum(s, axis=-1) + w_inter * jnp.einsum('bhtd,bhd->bht', qc, n)
        h = num / jnp.maximum(jnp.abs(den), jnp.exp(-m_t))[..., None]
        b_last = b[..., -1]
        log_g = b_last[..., None] - b + ic
        m_new = jnp.maximum(b_last + m, jnp.max(log_g, axis=-1))
        decay = jnp.exp(b_last + m - m_new)
        wk = kc * jnp.exp(log_g - m_new[..., None])[..., None]
        C = decay[..., None, None] * C + jnp.einsum('bhsd,bhsv->bhdv', wk, vc)
        n = decay[..., None] * n + jnp.sum(wk, axis=2)
        return (C, n, m_new), h

    init = (jnp.zeros((B, H, dk, dv), jnp.float32), jnp.zeros((B, H, dk), jnp.float32),
            jnp.zeros((B, H), jnp.float32))
    _, hs = lax.scan(step, init, xs)
    return jnp.moveaxis(hs, (0, 2), (1, 3)).reshape(B, S, H, dv)


def nsa_attention(q, kc_raw, vc_raw, ks, vs, kw, vw, gates, gq, gkc, gks, gkw,
                  pe_k, w1_k, w2_k, pe_v, w1_v, w2_v):
    B, S = q.shape[:2]
    G, R, hd = N_KV_HEADS, N_HEADS // N_KV_HEADS, N_HEAD_DIM
    scale = hd ** -0.5
    pos = jnp.arange(S, dtype=jnp.int32)
    q = rope(rms_norm(q, gq), pos).reshape(B, S, G, R, hd).transpose(0, 2, 3, 1, 4)

    ns = S // CMP_STRIDE
    n_sub = CMP_LEN // CMP_STRIDE
    nc = ns - n_sub + 1

    def compress(u, pe, w1, w2):
        sub = u.reshape(B, ns, CMP_STRIDE, G, hd)
        blocks = jnp.concatenate([sub[:, j:nc + j] for j in range(n_sub)], axis=2)
        blocks = blocks + pe[None, None, :, None, :]
        flat = blocks.transpose(0, 1, 3, 2, 4).reshape(B, nc, G, CMP_LEN * hd)
        return jax.nn.silu(flat @ w1) @ w2

    cmp_end = jnp.arange(nc, dtype=jnp.int32) * CMP_STRIDE + CMP_LEN - 1
    kc = rope(rms_norm(compress(kc_raw, pe_k, w1_k, w2_k), gkc), cmp_end).transpose(0, 2, 1, 3)
    vc = compress(vc_raw, pe_v, w1_v, w2_v).transpose(0, 2, 1, 3)

    nsel = S // SEL_BLOCK
    top = min(SEL_TOP, nsel)
    ks_blk = rope(rms_norm(ks, gks), pos).transpose(0, 2, 1, 3).reshape(B, G, nsel, SEL_BLOCK, hd)
    vs_blk = vs.transpose(0, 2, 1, 3).reshape(B, G, nsel, SEL_BLOCK, hd)

    pad = ((0, 0), (0, 0), (WINDOW, 0), (0, 0))
    kw_pad = jnp.pad(rope(rms_norm(kw, gkw), pos).transpose(0, 2, 1, 3), pad)
    vw_pad = jnp.pad(vw.transpose(0, 2, 1, 3), pad)

    Q = N_QBLOCK
    blk_ids = jnp.arange(nsel, dtype=jnp.int32)
    bi = jnp.arange(B)[:, None, None, None]
    gi = jnp.arange(G)[None, :, None, None]
    lead = ((0, 0),) * 4

    def block(qb):
        t0 = qb * Q
        tpos = t0 + jnp.arange(Q, dtype=jnp.int32)
        qq = lax.dynamic_slice_in_dim(q, t0, Q, axis=3)
        gg = lax.dynamic_slice_in_dim(gates, t0, Q, axis=3)
        s_c = jnp.einsum('bgrqd,bgnd->bgrqn', qq, kc) * scale
        p_c = masked_softmax(s_c, cmp_end[None, :] <= tpos[:, None])
        o_c = jnp.einsum('bgrqn,bgnd->bgrqd', p_c.astype(vc.dtype), vc)
        p_pad = jnp.pad(p_c, lead + ((0, ns - nc),))
        p_sub = sum(jnp.pad(p_pad[..., :ns - j], lead + ((j, 0),)) for j in range(n_sub))
        imp = p_sub.reshape(B, G, R, Q, nsel, SEL_BLOCK // CMP_STRIDE).sum(axis=(2, 5))
        cur = tpos // SEL_BLOCK
        causal_b = blk_ids[None, :] <= cur[:, None]
        forced = (blk_ids[None, :] == 0) | (blk_ids[None, :] == cur[:, None]) | (blk_ids[None, :] == cur[:, None] - 1)
        score = jnp.where(forced, FORCE_SCORE, jnp.where(causal_b, imp, -1.0))
        _, idx = lax.top_k(score, top)
        k_sel = ks_blk[bi, gi, idx]
        v_sel = vs_blk[bi, gi, idx]
        s_s = jnp.einsum('bgrqd,bgqnkd->bgrqnk', qq, k_sel) * scale
        kpos = idx[..., None] * SEL_BLOCK + jnp.arange(SEL_BLOCK, dtype=jnp.int32)
        m_s = (kpos <= tpos[:, None, None])[:, :, None]
        p_s = masked_softmax(s_s.reshape(B, G, R, Q, top * SEL_BLOCK), m_s.reshape(B, G, 1, Q, top * SEL_BLOCK))
        o_s = jnp.einsum('bgrqm,bgqmd->bgrqd', p_s.astype(v_sel.dtype), v_sel.reshape(B, G, Q, top * SEL_BLOCK, hd))
        k_w = lax.dynamic_slice_in_dim(kw_pad, t0, WINDOW + Q, axis=2)
        v_w = lax.dynamic_slice_in_dim(vw_pad, t0, WINDOW + Q, axis=2)
        wpos = t0 - WINDOW + jnp.arange(WINDOW + Q, dtype=jnp.int32)
        m_w = (wpos[None, :] <= tpos[:, None]) & (wpos[None, :] > tpos[:, None] - WINDOW) & (wpos[None, :] >= 0)
        s_w = jnp.einsum('bgrqd,bgkd->bgrqk', qq, k_w) * scale
        p_w = masked_softmax(s_w, m_w)
        o_w = jnp.einsum('bgrqk,bgkd->bgrqd', p_w.astype(v_w.dtype), v_w)
        return gg[..., 0:1] * o_c + gg[..., 1:2] * o_s + gg[..., 2:3] * o_w

    out = lax.map(block, jnp.arange(S // Q))
    return out.transpose(1, 0, 4, 2, 3, 5).reshape(B, S, N_HEADS * hd)


def memory_cross_attention(c_q, mem_n, w_mem_k, w_mem_v, gq, gk):
    B, S = c_q.shape[:2]
    M = mem_n.shape[1]
    q = rms_norm(c_q.reshape(B, S, C_HEADS, C_HEAD_DIM), gq)
    k = rms_norm((mem_n @ w_mem_k).reshape(B, M, C_HEADS, C_HEAD_DIM), gk)
    v = (mem_n @ w_mem_v).reshape(B, M, C_HEADS, C_HEAD_DIM)
    s = jnp.einsum('bshd,bmhd->bhsm', q, k).astype(jnp.float32) * (C_HEAD_DIM ** -0.5)
    p = jax.nn.softmax(s, axis=-1).astype(v.dtype)
    return jnp.einsum('bhsm,bmhd->bshd', p, v).reshape(B, S, C_WIDTH)


def hybrid_layer(x, mem, norm_mix_g, norm_mem_g, norm_ffn_g, w_in, m_conv_w, m_i_bias, m_f_bias, m_norm_g,
                 n_q_norm_g, n_kc_norm_g, n_ks_norm_g, n_kw_norm_g, n_cmp_pe_k, n_cmp_w1_k, n_cmp_w2_k,
                 n_cmp_pe_v, n_cmp_w1_v, n_cmp_w2_v, c_q_norm_g, c_k_norm_g, w_mem_k, w_mem_v,
                 w_up_a, w_up_b, w_up_c, w_out, w_ffn_gate, w_ffn_up, w_ffn_down):
    B, S, _ = x.shape
    h = rms_norm(x, norm_mix_g)
    z = h @ w_in
    (m_qk, m_v, m_o, m_i, m_f, n_q, n_kc, n_vc, n_ks, n_vs, n_kw, n_vw, n_g,
     c_q, g_a, g_b, g_c) = split_cols(z, IN_SIZES)

    qk = jax.nn.silu(causal_dwconv(m_qk, m_conv_w))
    q_m = qk[..., :M_WIDTH].reshape(B, S, M_HEADS, M_HEAD_DIM)
    k_m = qk[..., M_WIDTH:].reshape(B, S, M_HEADS, M_HEAD_DIM) * (M_HEAD_DIM ** -0.5)
    v_m = m_v.reshape(B, S, M_HEADS, M_HEAD_DIM)
    ig = m_i.astype(jnp.float32) + m_i_bias.astype(jnp.float32)
    lf = jax.nn.log_sigmoid(m_f.astype(jnp.float32) + m_f_bias.astype(jnp.float32))
    h_m = mlstm_chunkwise(q_m, k_m, v_m, ig, lf).astype(x.dtype)
    y_a = jax.nn.sigmoid(m_o) * rms_norm(h_m, m_norm_g).reshape(B, S, M_WIDTH)

    G, R = N_KV_HEADS, N_HEADS // N_KV_HEADS
    kv = lambda u: u.reshape(B, S, G, N_HEAD_DIM)
    gates = jax.nn.sigmoid(n_g).reshape(B, S, G, R, 3).transpose(0, 2, 3, 1, 4)
    y_b = nsa_attention(n_q.reshape(B, S, N_HEADS, N_HEAD_DIM), kv(n_kc), kv(n_vc), kv(n_ks), kv(n_vs),
                        kv(n_kw), kv(n_vw), gates, n_q_norm_g, n_kc_norm_g, n_ks_norm_g, n_kw_norm_g,
                        n_cmp_pe_k, n_cmp_w1_k, n_cmp_w2_k, n_cmp_pe_v, n_cmp_w1_v, n_cmp_w2_v)

    y_c = memory_cross_attention(c_q, rms_norm(mem, norm_mem_g), w_mem_k, w_mem_v, c_q_norm_g, c_k_norm_g)

    mix = (jax.nn.sigmoid(g_a) * (y_a @ w_up_a) + jax.nn.sigmoid(g_b) * (y_b @ w_up_b)
           + jax.nn.sigmoid(g_c) * (y_c @ w_up_c))
    x = x + mix @ w_out

    h2 = rms_norm(x, norm_ffn_g)
    return x + (jax.nn.silu(h2 @ w_ffn_gate) * (h2 @ w_ffn_up)) @ w_ffn_down


def setup_inputs(seed: int = 0) -> dict:
    key = jax.random.key(seed)
    k = jax.random.split(key, 32)
    L, D = DEPTH, D_MODEL
    f32 = jnp.float32

    def dense(kk, shape, fan_in):
        return jax.random.normal(kk, shape, f32) * (fan_in ** -0.5)

    def gain(kk, shape):
        return 1.0 + 0.02 * jax.random.normal(kk, shape, f32)

    def small(kk, shape, s):
        return s * jax.random.normal(kk, shape, f32)

    cin = CMP_LEN * N_HEAD_DIM
    return {
        'x': jax.random.normal(k[0], (BATCH, SEQ, D), f32),
        'mem': jax.random.normal(k[1], (BATCH, MEM_LEN, D), f32),
        'norm_mix_g': gain(k[2], (L, D)),
        'norm_mem_g': gain(k[3], (L, D)),
        'norm_ffn_g': gain(k[4], (L, D)),
        'w_in': dense(k[5], (L, D, D_IN), D),
        'm_conv_w': dense(k[6], (L, M_CONV, 2 * M_WIDTH), M_CONV),
        'm_i_bias': small(k[7], (L, M_HEADS), 0.1),
        'm_f_bias': jnp.linspace(3.0, 6.0, M_HEADS, dtype=f32)[None, :] + small(k[8], (L, M_HEADS), 0.1),
        'm_norm_g': gain(k[9], (L, M_HEADS, M_HEAD_DIM)),
        'n_q_norm_g': gain(k[10], (L, N_HEAD_DIM)),
        'n_kc_norm_g': gain(k[11], (L, N_HEAD_DIM)),
        'n_ks_norm_g': gain(k[12], (L, N_HEAD_DIM)),
        'n_kw_norm_g': gain(k[13], (L, N_HEAD_DIM)),
        'n_cmp_pe_k': small(k[14], (L, CMP_LEN, N_HEAD_DIM), 0.1),
        'n_cmp_w1_k': dense(k[15], (L, cin, CMP_HIDDEN), cin),
        'n_cmp_w2_k': dense(k[16], (L, CMP_HIDDEN, N_HEAD_DIM), CMP_HIDDEN),
        'n_cmp_pe_v': small(k[17], (L, CMP_LEN, N_HEAD_DIM), 0.1),
        'n_cmp_w1_v': dense(k[18], (L, cin, CMP_HIDDEN), cin),
        'n_cmp_w2_v': dense(k[19], (L, CMP_HIDDEN, N_HEAD_DIM), CMP_HIDDEN),
        'c_q_norm_g': gain(k[20], (L, C_HEAD_DIM)),
        'c_k_norm_g': gain(k[21], (L, C_HEAD_DIM)),
        'w_mem_k': dense(k[22], (L, D, C_WIDTH), D),
        'w_mem_v': dense(k[23], (L, D, C_WIDTH), D),
        'w_up_a': dense(k[24], (L, M_WIDTH, D), M_WIDTH),
        'w_up_b': dense(k[25], (L, N_WIDTH, D), N_WIDTH),
        'w_up_c': dense(k[26], (L, C_WIDTH, D), C_WIDTH),
        'w_out': dense(k[27], (L, D, D), D),
        'w_ffn_gate': dense(k[28], (L, D, D_FF), D),
        'w_ffn_up': dense(k[29], (L, D, D_FF), D),
        'w_ffn_down': dense(k[30], (L, D_FF, D), D_FF),
    }


def reference(x, mem, norm_mix_g, norm_mem_g, norm_ffn_g, w_in, m_conv_w, m_i_bias, m_f_bias, m_norm_g,
              n_q_norm_g, n_kc_norm_g, n_ks_norm_g, n_kw_norm_g, n_cmp_pe_k, n_cmp_w1_k, n_cmp_w2_k,
              n_cmp_pe_v, n_cmp_w1_v, n_cmp_w2_v, c_q_norm_g, c_k_norm_g, w_mem_k, w_mem_v,
              w_up_a, w_up_b, w_up_c, w_out, w_ffn_gate, w_ffn_up, w_ffn_down):
    for l in range(DEPTH):
        x = hybrid_layer(x, mem, norm_mix_g[l], norm_mem_g[l], norm_ffn_g[l], w_in[l], m_conv_w[l],
                         m_i_bias[l], m_f_bias[l], m_norm_g[l], n_q_norm_g[l], n_kc_norm_g[l],
                         n_ks_norm_g[l], n_kw_norm_g[l], n_cmp_pe_k[l], n_cmp_w1_k[l], n_cmp_w2_k[l],
                         n_cmp_pe_v[l], n_cmp_w1_v[l], n_cmp_w2_v[l], c_q_norm_g[l], c_k_norm_g[l],
                         w_mem_k[l], w_mem_v[l], w_up_a[l], w_up_b[l], w_up_c[l], w_out[l],
                         w_ffn_gate[l], w_ffn_up[l], w_ffn_down[l])
    return x
```

```python
import contextlib
import numpy as np
import concourse.bass as bass
import concourse.mybir as mybir
from concourse.bass_utils import run_bass_kernel_spmd

F32 = mybir.dt.float32
AF = mybir.ActivationFunctionType
ALU = mybir.AluOpType

D = 2048
DFF = 5632
MEM = 256
EPS = 1e-6
NEG = -30000.0
ENGS = ("tensor", "vector", "scalar", "gpsimd", "sync")

_sizes = (2048, 1024, 1024, 4, 4, 1024, 256, 256, 256, 256, 256, 256, 48, 1024, 2048, 2048, 2048)
_names = ("m_qk", "m_v", "m_o", "m_i", "m_f", "n_q", "n_kc", "n_vc", "n_ks", "n_vs", "n_kw", "n_vw", "n_g",
          "c_q", "g_a", "g_b", "g_c")
_off = {}
_o = 0
for _n, _s in zip(_names, _sizes):
    _off[_n] = (_o, _s)
    _o += _s
D_IN = _o
FM_SEGS = ("m_qk", "m_o", "n_q", "n_kc", "n_vc", "n_ks", "n_kw", "c_q", "g_a", "g_b", "g_c", "n_g")
TM_SEGS = ("m_v", "n_vs", "n_vw", "m_i", "m_f")
SIG_SEGS = ("m_o", "g_a", "g_b", "g_c", "n_g")
FM_ROW = {}
_r = 0
for _n in FM_SEGS:
    FM_ROW[_n] = _r
    _r += _off[_n][1]
N_FM = _r
N_FM_BLK = (N_FM + 127) // 128
TM_COL = {}
_r = 0
for _n in TM_SEGS:
    TM_COL[_n] = _r
    _r += _off[_n][1]
N_TM = _r
N_TM_BLK = (N_TM + 127) // 128
FM_SIG_BLK = set()
for _n in SIG_SEGS:
    for _b in range(FM_ROW[_n] // 128, (FM_ROW[_n] + _off[_n][1] + 127) // 128):
        FM_SIG_BLK.add(_b)


class Res:
    __slots__ = ("name", "w", "r")

    def __init__(self, name=""):
        self.name = name
        self.w = {}
        self.r = {}


class Sched:
    def __init__(self, nc, n_dma_sems=28, strict_same=True):
        self.nc = nc
        self.es = contextlib.ExitStack()
        self.streams = {e: [] for e in ENGS}
        self.sem = {}
        self.cnt = {}
        self.known = {e: {} for e in ENGS}
        self.semobj = {}
        for e in ENGS:
            s = self.es.enter_context(nc.semaphore("s_" + e))
            self.sem[e] = "s_" + e
            self.semobj["s_" + e] = s
            self.cnt[e] = 0
        self.dma_pool = []
        for i in range(n_dma_sems):
            nm = "d%d" % i
            self.semobj[nm] = self.es.enter_context(nc.semaphore(nm))
            self.dma_pool.append([nm, 0])
        self.dma_rr = 0
        self.strict_same = strict_same
        self.n_ops = 0

    def sb(self, name, shape, dtype=F32):
        return self.es.enter_context(self.nc.sbuf_tensor(name, list(shape), dtype))

    def ps(self, name, shape, dtype=F32):
        return self.es.enter_context(self.nc.psum_tensor(name, list(shape), dtype))

    def _waits(self, eng, reads, writes, acc):
        need = {}
        own = self.sem[eng]

        def add(d, skip_own=False):
            for k, v in d.items():
                if skip_own and k == own:
                    continue
                if v > need.get(k, 0):
                    need[k] = v

        for r in reads:
            add(r.w)
        for w in writes:
            add(w.w, skip_own=acc)
            add(w.r)
        kn = self.known[eng]
        for k, v in need.items():
            if k == own and (eng == "tensor" or not self.strict_same):
                continue
            if kn.get(k, 0) >= v:
                continue
            kn[k] = v
            self.streams[eng].append(("w", k, v))

    def op(self, eng, fn, reads=(), writes=(), inc=True, acc=False):
        self._waits(eng, reads, writes, acc)
        own = self.sem[eng]
        if inc:
            self.cnt[eng] += 1
            tok = self.cnt[eng]
        else:
            tok = self.cnt[eng] + 1
        self.streams[eng].append(("o", fn, inc))
        for r in reads:
            if r.r.get(own, 0) < tok:
                r.r[own] = tok
        for w in writes:
            if acc:
                if w.w.get(own, 0) < tok:
                    w.w[own] = tok
            else:
                w.w = {own: tok}
                w.r = {}
        self.n_ops += 1

    def dma(self, q, out, in_, reads=(), writes=(), acc=False, **kw):
        ent = self.dma_pool[self.dma_rr]
        self.dma_rr = (self.dma_rr + 1) % len(self.dma_pool)
        nm = ent[0]
        kn = self.known[q]
        if ent[1] > 0 and kn.get(nm, 0) < ent[1]:
            kn[nm] = ent[1]
            self.streams[q].append(("w", nm, ent[1]))
        self._waits(q, reads, writes, acc)
        ent[1] += 16
        tok = ent[1]
        self.streams[q].append(("d", out, in_, nm, kw))
        for r in reads:
            if r.r.get(nm, 0) < tok:
                r.r[nm] = tok
        for w in writes:
            if acc:
                if w.w.get(nm, 0) < tok:
                    w.w[nm] = tok
            else:
                w.w = {nm: tok}
                w.r = {}
        self.n_ops += 1

    def finish(self, final_res):
        nc = self.nc
        need = {}
        for r in final_res:
            for k, v in r.w.items():
                need[k] = max(need.get(k, 0), v)
        for k, v in need.items():
            self.streams["sync"].append(("w", k, v))
        semobj = self.semobj
        streams = self.streams
        sem = self.sem

        def replay(name, e):
            own = semobj[sem[name]]
            for it in streams[name]:
                if it[0] == "w":
                    e.wait_ge(semobj[it[1]], it[2])
                elif it[0] == "o":
                    ins = it[1](e)
                    if it[2]:
                        ins.then_inc(own, 1)
                else:
                    e.dma_start(out=it[1], in_=it[2], **it[4]).then_inc(semobj[it[3]], 16)

        with nc.Block() as block:
            @block.sync
            def _(e):
                replay("sync", e)

            @block.tensor
            def _(e):
                replay("tensor", e)

            @block.vector
            def _(e):
                replay("vector", e)

            @block.scalar
            def _(e):
                replay("scalar", e)

            @block.gpsimd
            def _(e):
                replay("gpsimd", e)
        self.es.close()


def mm(S, out, lhsT, rhs, start, stop, reads, writes, inc=None):
    S.op("tensor", lambda e: e.matmul(out, lhsT=lhsT, rhs=rhs, start=start, stop=stop), reads, writes,
         inc=(stop if inc is None else inc), acc=(not start))


def tr(S, out, in_, ident, reads, writes):
    S.op("tensor", lambda e: e.transpose(out=out, in_=in_, identity=ident), reads, writes)


def act(S, out, in_, func, reads, writes, bias=None, scale=None, acc=False):
    kw = {}
    if bias is not None:
        kw["bias"] = bias
    if scale is not None:
        kw["scale"] = scale
    S.op("scalar", lambda e: e.activation(out=out, in_=in_, func=func, **kw), reads, writes, acc=acc)


def tt(S, eng, out, in0, in1, op, reads, writes, acc=False):
    S.op(eng, lambda e: e.tensor_tensor(out=out, in0=in0, in1=in1, op=op), reads, writes, acc=acc)


def stt(S, eng, out, in0, scalar, in1, op0, op1, reads, writes, acc=False):
    S.op(eng, lambda e: e.scalar_tensor_tensor(out=out, in0=in0, scalar=scalar, in1=in1, op0=op0, op1=op1),
         reads, writes, acc=acc)


def ts(S, eng, out, in0, s1, s2, op0, op1, reads, writes, acc=False):
    if s2 is None:
        S.op(eng, lambda e: e.tensor_scalar(out=out, in0=in0, scalar1=s1, scalar2=None, op0=op0), reads, writes, acc=acc)
    else:
        S.op(eng, lambda e: e.tensor_scalar(out=out, in0=in0, scalar1=s1, scalar2=s2, op0=op0, op1=op1),
             reads, writes, acc=acc)


def cp(S, eng, out, in_, reads, writes, acc=False):
    if eng == "scalar":
        S.op(eng, lambda e: e.copy(out=out, in_=in_), reads, writes, acc=acc)
    else:
        S.op(eng, lambda e: e.tensor_copy(out=out, in_=in_), reads, writes, acc=acc)


class Ring:
    def __init__(self, S, name, n, shape, psum=False):
        self.bufs = [(S.ps if psum else S.sb)("%s%d" % (name, i), shape) for i in range(n)]
        self.res = [Res("%s%d" % (name, i)) for i in range(n)]
        self.i = 0

    def next(self):
        b, r = self.bufs[self.i], self.res[self.i]
        self.i = (self.i + 1) % len(self.bufs)
        return b, r


class WStream:
    def __init__(self, S, nbuf=5):
        self.S = S
        self.bufs = [S.sb("wb%d" % i, [128, 16, 128]) for i in range(nbuf)]
        self.res = [Res("wb%d" % i) for i in range(nbuf)]
        self.q = []
        self.issued = 0
        self.consumed = 0

    def push(self, ap, kc):
        self.q.append((ap, kc))

    def get(self):
        nb = len(self.bufs)
        while self.issued < len(self.q) and self.issued < self.consumed + nb - 1:
            ap, kc = self.q[self.issued]
            b = self.issued % nb
            self.S.dma("sync", self.bufs[b][:, 0:kc, :], ap, writes=[self.res[b]])
            self.issued += 1
        b = self.consumed % nb
        kc = self.q[self.consumed][1]
        self.consumed += 1
        return self.bufs[b], self.res[b], kc


class Arena:
    def __init__(self, S, name, n):
        self.t = S.sb(name, [128, n])
        self.n = n
        self.off = 0
        self.live = []
        self.prev = {}

    def reset(self):
        for r in self.live:
            for d in (r.w, r.r):
                for k, v in d.items():
                    if v > self.prev.get(k, 0):
                        self.prev[k] = v
        self.live = []
        self.off = 0

    def alloc(self, shape, name="", p0=0, share=None):
        n = int(np.prod(shape[1:]))
        off = self.off if share is None else share
        assert off + n <= self.n, (name, off, n, self.n)
        v = self.t[p0:p0 + shape[0], off:off + n]
        if len(shape) == 3:
            v = v.rearrange("p (a b) -> p a b", a=shape[1])
        elif len(shape) == 4:
            v = v.rearrange("p (a b c) -> p a b c", a=shape[1], b=shape[2])
        if share is None:
            self.off += n
        r = Res(name)
        r.r = dict(self.prev)
        self.live.append(r)
        return v, r


class ZProxy:
    def __init__(self, prog, T):
        self.segs = []
        for n in FM_SEGS:
            rows = ((_off[n][1] + 127) // 128) * 128
            t = prog.nc.dram_tensor("z_" + n, [rows, T], F32, kind="Internal").ap()
            self.segs.append((FM_ROW[n], rows, t))

    def __getitem__(self, key):
        rs, cs = key
        for base, rows, t in self.segs:
            if base <= rs.start < base + rows:
                assert rs.stop <= base + rows
                return t[rs.start - base:rs.stop - base, cs]
        raise KeyError(key)


C_ONES, C_ID, C_TRI, C_BD64, C_RM = range(5)
LN16 = float(np.log(16.0))


class Prog:
    def __init__(self, T, L, dbg=()):
        self.T, self.L = T, L
        self.NT = T // 512
        self.NQT = T // 128
        self.NC = T // 16 - 1
        self.NCH = (self.NC + 127) // 128
        self.NCP = self.NCH * 128
        nc = bass.Bass("TRN2", target_bir_lowering=False)
        self.nc = nc
        self.S = Sched(nc)
        self.dbg = set(dbg)
        self.inputs = {}
        self.dres = {}
        S = self.S
        NQT, NCH, NCP = self.NQT, self.NCH, self.NCP
        self.xT = self.din("xT", [D, T])
        self.memT = self.din("memT", [D, MEM])
        self.w_in = self.din("w_in", [L, N_FM_BLK + N_TM_BLK, 128, 16, 128])
        self.gmix = self.din("gmix", [L, 128, 16])
        self.gffn = self.din("gffn", [L, 128, 16])
        self.gmem = self.din("gmem", [L, 128, 16])
        self.w_up = self.din("w_up", [L, 3, 16, 128, 8, 128])
        self.w_out = self.din("w_out", [L, 16, 128, 16, 128])
        self.w_fg = self.din("w_fg", [L, 44, 128, 16, 128])
        self.w_fu = self.din("w_fu", [L, 44, 128, 16, 128])
        self.w_fd = self.din("w_fd", [L, 16, 128, 44, 128])
        self.w_mk = self.din("w_mk", [L, 8, 128, 16, 128])
        self.w_mv = self.din("w_mv", [L, 8, 128, 16, 128])
        self.g_cq = self.din("g_cq", [L, 128, 2])
        self.g_ck = self.din("g_ck", [L, 128, 2])
        self.m_cw = self.din("m_cw", [L, 128, 16, 4])
        self.m_b8 = self.din("m_b8", [L, 1, 8])
        self.m_gn = self.din("m_gn", [L, 128, 8])
        self.g_n4 = self.din("g_n4", [L, 128, 4])
        self.n_w1 = self.din("n_w1", [L, 2, 64, 32, 128])
        self.n_pe = self.din("n_pe", [L, 2, 64, 32])
        self.n_w2 = self.din("n_w2", [L, 2, 128, 64])
        self.c128 = self.din("c128", [5, 128, 128])
        self.c_cos = self.din("c_cos", [128, T])
        self.c_sin = self.din("c_sin", [128, T])
        self.c_cosc = self.din("c_cosc", [64, NCP])
        self.c_sinc = self.din("c_sinc", [64, NCP])
        self.c_mimp = self.din("c_mimp", [128, NCH, 128])
        self.c_g = self.din("c_g", [128, T])
        self.c_sbias = self.din("c_sbias", [NQT, 128, 128])
        self.c_thr = self.din("c_thr", [128, NCH * NQT])
        self.c_qidx = self.din("c_qidx", [128, 512])
        self.c_negc = self.din("c_negc", [128, 512])
        self.c_nega = self.din("c_nega", [128, 512])
        self.zfm = ZProxy(self, T)
        self.dres["zfm"] = Res("zfm")
        self.ztm = self.dscr("ztm", [T, N_TM_BLK * 128])
        self.yT = [self.dscr("y%sT" % b, [1024, T], inp=("yin" in self.dbg)) for b in "abc"]
        self.qkc = self.dscr("qkc", [2048, T])
        self.nqr = self.dscr("nqr", [1024, T])
        self.nkr = self.dscr("nkr", [512, T])
        self.xbuf = [self.dscr("xs%d" % i, [D, T]) for i in range(2)]
        self.outT = nc.dram_tensor("outT", [D, T], F32, kind="ExternalOutput").ap()
        self.dres["outT"] = Res("outT")
        self.r_const = Res("const")
        self.cst = S.sb("cst", [128, 5, 128])
        S.dma("sync", self.cst[:], self.c128.rearrange("k p m -> p k m"), writes=[self.r_const])
        self.ones = self.cst[:, C_ONES, :]
        self.ident = self.cst[:, C_ID, :]
        self.tri = self.cst[:, C_TRI, :]
        self.bd64 = self.cst[:, C_BD64, :]
        self.rm = self.cst[:, C_RM, :]
        self.gm = S.sb("gm", [128, L, 16])
        self.gf = S.sb("gf", [128, L, 16])
        self.gme = S.sb("gme", [128, L, 16])
        self.gsm = S.sb("gsm", [128, L, 24])
        S.dma("sync", self.gm[:], self.gmix.rearrange("l p c -> p l c"), writes=[self.r_const], acc=True)
        S.dma("sync", self.gf[:], self.gffn.rearrange("l p c -> p l c"), writes=[self.r_const], acc=True)
        S.dma("sync", self.gme[:], self.gmem.rearrange("l p c -> p l c"), writes=[self.r_const], acc=True)
        S.dma("sync", self.gsm[:, :, 0:2], self.g_cq.rearrange("l p c -> p l c"), writes=[self.r_const], acc=True)
        S.dma("sync", self.gsm[:, :, 2:4], self.g_ck.rearrange("l p c -> p l c"), writes=[self.r_const], acc=True)
        S.dma("sync", self.gsm[:, :, 4:12], self.m_gn.rearrange("l p c -> p l c"), writes=[self.r_const], acc=True)
        S.dma("sync", self.gsm[:, :, 12:16], self.g_n4.rearrange("l p c -> p l c"), writes=[self.r_const], acc=True)
        for l in range(L):
            S.dma("sync", self.gsm[:, l, 16:24], self.m_b8[l].partition_broadcast(128), writes=[self.r_const], acc=True)
        self.cw = S.sb("cw", [128, L, 16, 4])
        S.dma("sync", self.cw[:], self.m_cw.rearrange("l p c j -> p l c j"), writes=[self.r_const], acc=True)
        self.ws = WStream(S, nbuf=5)
        self.psr = Ring(S, "pp", 4, [128, 512], psum=True)
        self.pa = [S.ps("pa%d" % i, [128, 512]) for i in range(4)]
        self.r_pa = [Res("pa%d" % i) for i in range(4)]
        self.obr = Ring(S, "ob", 4, [128, 512])
        self.sgr = Ring(S, "sg", 2, [128, 512])
        self.tmr = Ring(S, "tm", 2, [128, 512])
        self.AR0 = Arena(S, "ar0", 16 * 512)
        self.AR1 = Arena(S, "ar1", 16 * 512)
        self.AR2 = Arena(S, "ar2", 22 * 512)
        self.rs = S.sb("rs", [128, 512])
        self.r_rs = Res("rs")

    def din(self, name, shape):
        t = self.nc.dram_tensor(name, list(shape), F32, kind="ExternalInput").ap()
        self.inputs[name] = tuple(shape)
        return t

    def dscr(self, name, shape, inp=False):
        kind = "Internal"
        if inp:
            kind = "ExternalInput"
            self.inputs[name] = tuple(shape)
        elif name in self.dbg:
            kind = "ExternalOutput"
        t = self.nc.dram_tensor(name, list(shape), F32, kind=kind).ap()
        self.dres[name] = Res(name)
        return t

    def dense_bufs(self):
        for a in (self.AR0, self.AR1, self.AR2):
            a.reset()
        self.xt, self.r_xt = self.AR0.alloc([128, 16, 512], "xt")
        self.hb, self.r_hb = self.AR1.alloc([128, 16, 512], "hb")
        self.actb, self.r_actb = self.AR2.alloc([128, 22, 512], "actb")

    def rstd(self, out, r_out, ps, r_ps, n):
        S = self.S
        act(S, out, ps, AF.Ln, [r_ps], [r_out], bias=EPS, scale=1.0 / n)
        act(S, out, out, AF.Exp, [r_out], [r_out], scale=-0.5)

    def rmsnorm_tile(self, gsb, l):
        S = self.S
        xt, hb, rs = self.xt, self.hb, self.rs
        act(S, hb[:], xt[:], AF.Square, [self.r_xt], [self.r_hb])
        ps, rps = self.psr.next()
        for c in range(16):
            mm(S, ps[:], self.ones, hb[:, c, :], c == 0, c == 15, [self.r_const, self.r_hb], [rps])
        self.rstd(rs[:], self.r_rs, ps[:], rps, D)
        for c in range(16):
            if c % 3 != 2:
                stt(S, "vector", hb[:, c, :], xt[:, c, :], gsb[:, l, c:c + 1], rs[:], ALU.mult, ALU.mult,
                    [self.r_xt, self.r_rs, self.r_const], [self.r_hb], acc=(c > 0))
            else:
                ts(S, "gpsimd", hb[:, c, :], xt[:, c, :], gsb[:, l, c:c + 1], None, ALU.mult, None,
                   [self.r_xt, self.r_const], [self.r_hb], acc=True)
                tt(S, "gpsimd", hb[:, c, :], hb[:, c, :], rs[:], ALU.mult, [self.r_rs, self.r_hb], [self.r_hb], acc=True)

    def phase_A(self, l, xsrc, r_xsrc):
        S = self.S
        self.dense_bufs()
        nblk = N_FM_BLK + N_TM_BLK
        for tt_ in range(self.NT):
            t0 = tt_ * 512
            S.dma("sync", self.xt[:], xsrc[:, t0:t0 + 512].rearrange("(c p) t -> p c t", p=128),
                  reads=[r_xsrc], writes=[self.r_xt])
            for b in range(nblk):
                self.ws.push(self.w_in[l, b], 16)
            self.rmsnorm_tile(self.gm, l)
            hb = self.hb
            for b in range(nblk):
                wb, rwb, kc = self.ws.get()
                ps, rps = self.psr.next()
                ob, rob = self.obr.next()
                if b < N_FM_BLK:
                    for c in range(16):
                        mm(S, ps[:], wb[:, c, :], hb[:, c, :], c == 0, c == 15, [rwb, self.r_hb], [rps])
                    if b in FM_SIG_BLK:
                        act(S, ob[:], ps[:], AF.Sigmoid, [rps], [rob])
                    elif b % 2 == 0:
                        cp(S, "vector", ob[:], ps[:], [rps], [rob])
                    else:
                        cp(S, "scalar", ob[:], ps[:], [rps], [rob])
                    S.dma("gpsimd", self.zfm[b * 128:(b + 1) * 128, t0:t0 + 512], ob[:], reads=[rob],
                          writes=[self.dres["zfm"]], acc=True)
                else:
                    bt = b - N_FM_BLK
                    for j in range(4):
                        for c in range(16):
                            mm(S, ps[:, j * 128:(j + 1) * 128], hb[:, c, j * 128:(j + 1) * 128], wb[:, c, :],
                               c == 0, c == 15, [rwb, self.r_hb], [rps], inc=(c == 15 and j == 3))
                    cp(S, "vector", ob[:], ps[:], [rps], [rob])
                    S.dma("gpsimd", self.ztm[t0:t0 + 512, bt * 128:(bt + 1) * 128].rearrange("(j p) c -> p j c", p=128),
                          ob[:].rearrange("p (j c) -> p j c", j=4), reads=[rob], writes=[self.dres["ztm"]], acc=True)

    def phase_E(self, l, xsrc, r_xsrc, xdst, r_xdst):
        S = self.S
        self.dense_bufs()
        xt, hb = self.xt, self.hb
        actb = self.actb
        yb = actb
        for tt_ in range(self.NT):
            t0 = tt_ * 512
            S.dma("sync", xt[:], xsrc[:, t0:t0 + 512].rearrange("(c p) t -> p c t", p=128),
                  reads=[r_xsrc], writes=[self.r_xt])
            for br in range(3):
                for mb in range(16):
                    self.ws.push(self.w_up[l, br, mb], 8)
            for mb in range(16):
                self.ws.push(self.w_out[l, mb], 16)
            gname = ("g_a", "g_b", "g_c")
            for br in range(3):
                S.dma("sync", yb[:, 0:8, :], self.yT[br][:, t0:t0 + 512].rearrange("(c p) t -> p c t", p=128),
                      reads=[self.dres["y%sT" % "abc"[br]]], writes=[self.r_actb])
                for mb in range(16):
                    wb, rwb, kc = self.ws.get()
                    ps, rps = self.psr.next()
                    sg, rsg = self.sgr.next()
                    r0 = FM_ROW[gname[br]] + mb * 128
                    S.dma("sync", sg[:], self.zfm[r0:r0 + 128, t0:t0 + 512], reads=[self.dres["zfm"]], writes=[rsg])
                    for c in range(8):
                        mm(S, ps[:], wb[:, c, :], yb[:, c, :], c == 0, c == 7, [rwb, self.r_actb], [rps])
                    if br == 0:
                        tt(S, "vector", hb[:, mb, :], ps[:], sg[:], ALU.mult, [rps, rsg], [self.r_hb], acc=(mb > 0))
                    else:
                        tm, rtm = self.tmr.next()
                        tt(S, "vector", tm[:], ps[:], sg[:], ALU.mult, [rps, rsg], [rtm])
                        tt(S, "gpsimd", hb[:, mb, :], hb[:, mb, :], tm[:], ALU.add, [rtm, self.r_hb], [self.r_hb], acc=True)
            for mb in range(16):
                wb, rwb, kc = self.ws.get()
                ps, rps = self.psr.next()
                for c in range(16):
                    mm(S, ps[:], wb[:, c, :], hb[:, c, :], c == 0, c == 15, [rwb, self.r_hb], [rps])
                tt(S, "vector", xt[:, mb, :], xt[:, mb, :], ps[:], ALU.add, [rps, self.r_xt], [self.r_xt], acc=True)
            for half in range(2):
                for fb in range(22):
                    self.ws.push(self.w_fg[l, half * 22 + fb], 16)
                    self.ws.push(self.w_fu[l, half * 22 + fb], 16)
                for mb in range(16):
                    self.ws.push(self.w_fd[l, mb, :, half * 22:half * 22 + 16, :], 16)
                    self.ws.push(self.w_fd[l, mb, :, half * 22 + 16:half * 22 + 22, :], 6)
            self.rmsnorm_tile(self.gf, l)
            for half in range(2):
                for fb in range(22):
                    wg, rwg, _ = self.ws.get()
                    psg, rpsg = self.psr.next()
                    for c in range(16):
                        mm(S, psg[:], wg[:, c, :], hb[:, c, :], c == 0, c == 15, [rwg, self.r_hb], [rpsg])
                    wu, rwu, _ = self.ws.get()
                    psu, rpsu = self.psr.next()
                    for c in range(16):
                        mm(S, psu[:], wu[:, c, :], hb[:, c, :], c == 0, c == 15, [rwu, self.r_hb], [rpsu])
                    tm, rtm = self.tmr.next()
                    act(S, tm[:], psg[:], AF.Silu, [rpsg], [rtm])
                    tt(S, "vector", actb[:, fb, :], tm[:], psu[:], ALU.mult, [rtm, rpsu], [self.r_actb], acc=(fb > 0))
                for mb in range(16):
                    ps, rps = self.psr.next()
                    w1, rw1, _ = self.ws.get()
                    for c in range(16):
                        mm(S, ps[:], w1[:, c, :], actb[:, c, :], c == 0, False, [rw1, self.r_actb], [rps], inc=False)
                    w2, rw2, _ = self.ws.get()
                    for c in range(6):
                        mm(S, ps[:], w2[:, c, :], actb[:, 16 + c, :], False, c == 5, [rw2, self.r_actb], [rps])
                    tt(S, "vector", xt[:, mb, :], xt[:, mb, :], ps[:], ALU.add, [rps, self.r_xt], [self.r_xt], acc=True)
            S.dma("gpsimd", xdst[:, t0:t0 + 512].rearrange("(c p) t -> p c t", p=128), xt[:],
                  reads=[self.r_xt], writes=[r_xdst], acc=True)

    def phase_C(self, l):
        S = self.S
        for a in (self.AR1, self.AR2):
            a.reset()
        A1, A2 = self.AR1, self.AR2
        rc = self.r_const
        sqm, r_sqm = A1.alloc([128, 16, 256], "sqm")
        memn, r_memn = A2.alloc([128, 16, 256], "memn")
        kT, r_kT = A2.alloc([128, 8, 256], "kT")
        vv, r_vv = A2.alloc([128, 2, 1024], "vv")
        qb, r_qb = A2.alloc([128, 2, 512], "qb")
        sq, r_sq = A2.alloc([128, 2, 512], "sq")
        pt, r_pt = A2.alloc([128, 2, 512], "pt")
        rs, r_rs = self.rs, self.r_rs
        S.dma("sync", memn[:], self.memT.rearrange("(c p) m -> p c m", p=128), writes=[r_memn])
        for mb in range(8):
            self.ws.push(self.w_mk[l, mb], 16)
        for mb in range(8):
            self.ws.push(self.w_mv[l, mb], 16)
        act(S, sqm[:], memn[:], AF.Square, [r_memn], [r_sqm])
        ps, rps = self.psr.next()
        for c in range(16):
            mm(S, ps[:, 0:256], self.ones, sqm[:, c, :], c == 0, c == 15, [rc, r_sqm], [rps])
        self.rstd(rs[:, 0:256], r_rs, ps[:, 0:256], rps, D)
        for c in range(16):
            stt(S, "vector", memn[:, c, :], memn[:, c, :], self.gme[:, l, c:c + 1], rs[:, 0:256], ALU.mult, ALU.mult,
                [r_memn, r_rs, rc], [r_memn], acc=True)
        for mb in range(8):
            wb, rwb, _ = self.ws.get()
            ps, rps = self.psr.next()
            for c in range(16):
                mm(S, ps[:, 0:256], wb[:, c, :], memn[:, c, :], c == 0, c == 15, [rwb, r_memn], [rps])
            cp(S, "scalar", kT[:, mb, :], ps[:, 0:256], [rps], [r_kT], acc=(mb > 0))
        for hh in range(4):
            act(S, sq[:, :, 0:256], kT[:, 2 * hh:2 * hh + 2, :], AF.Square, [r_kT], [r_sq])
            ps, rps = self.psr.next()
            for j in range(2):
                mm(S, ps[:, 0:256], self.ones, sq[:, j, 0:256], j == 0, j == 1, [rc, r_sq], [rps])
            self.rstd(rs[:, 0:256], r_rs, ps[:, 0:256], rps, 256)
            for j in range(2):
                stt(S, "vector", kT[:, 2 * hh + j, :], kT[:, 2 * hh + j, :], self.gsm[:, l, 2 + j:3 + j], rs[:, 0:256],
                    ALU.mult, ALU.mult, [r_kT, r_rs, rc], [r_kT], acc=True)
        for mb in range(8):
            wb, rwb, _ = self.ws.get()
            ps, rps = self.psr.next()
            for mt in range(2):
                for c in range(16):
                    mm(S, ps[:, mt * 128:(mt + 1) * 128], memn[:, c, mt * 128:(mt + 1) * 128], wb[:, c, :],
                       c == 0, c == 15, [rwb, r_memn], [rps], inc=(c == 15 and mt == 1))
            cp(S, "vector", vv[:, :, mb * 128:(mb + 1) * 128], ps[:, 0:256].rearrange("p (m c) -> p m c", m=2),
               [rps], [r_vv], acc=(mb > 0))
        r0 = FM_ROW["c_q"]
        for tt_ in range(self.NT):
            t0 = tt_ * 512
            for hh in range(4):
                S.dma("sync", qb[:], self.zfm[r0 + hh * 256:r0 + (hh + 1) * 256, t0:t0 + 512].rearrange("(j p) t -> p j t", p=128),
                      reads=[self.dres["zfm"]], writes=[r_qb])
                act(S, sq[:], qb[:], AF.Square, [r_qb], [r_sq])
                ps, rps = self.psr.next()
                for j in range(2):
                    mm(S, ps[:], self.ones, sq[:, j, :], j == 0, j == 1, [rc, r_sq], [rps])
                self.rstd(rs[:], r_rs, ps[:], rps, 256)
                for j in range(2):
                    stt(S, "vector", qb[:, j, :], qb[:, j, :], self.gsm[:, l, j:j + 1], rs[:], ALU.mult, ALU.mult,
                        [r_qb, r_rs, rc], [r_qb], acc=True)
                for mt in range(2):
                    ps, rps = self.psr.next()
                    for j in range(2):
                        mm(S, ps[:], kT[:, 2 * hh + j, mt * 128:(mt + 1) * 128], qb[:, j, :], j == 0, j == 1, [r_kT, r_qb], [rps])
                    act(S, pt[:, mt, :], ps[:], AF.Exp, [rps], [r_pt], scale=1.0 / 16.0, acc=(mt > 0))
                psd, rpsd = self.psr.next()
                for mt in range(2):
                    mm(S, psd[:], self.ones, pt[:, mt, :], mt == 0, mt == 1, [rc, r_pt], [rpsd])
                tm, rtm = self.tmr.next()
                S.op("vector", lambda e, o=tm[:], i=psd[:]: e.reciprocal(out=o, in_=i), [rpsd], [rtm])
                for j in range(2):
                    ps, rps = self.psr.next()
                    for mt in range(2):
                        mm(S, ps[:], vv[:, mt, hh * 256 + j * 128:hh * 256 + (j + 1) * 128], pt[:, mt, :], mt == 0, mt == 1,
                           [r_vv, r_pt], [rps])
                    ob, rob = self.obr.next()
                    tt(S, "vector", ob[:], ps[:], tm[:], ALU.mult, [rps, rtm], [rob])
                    S.dma("gpsimd", self.yT[2][hh * 256 + j * 128:hh * 256 + (j + 1) * 128, t0:t0 + 512], ob[:],
                          reads=[rob], writes=[self.dres["ycT"]], acc=True)

    def phase_M(self, l):
        S = self.S
        T = self.T
        rc = self.r_const
        for a in (self.AR0, self.AR1, self.AR2):
            a.reset()
        A0, A1, A2 = self.AR0, self.AR1, self.AR2
        ub = [A0.alloc([128, 515], "u%d" % i) for i in range(3)]
        ab = [A0.alloc([128, 512], "a%d" % i) for i in range(3)]
        k = 0
        for tt_ in range(self.NT):
            t0 = tt_ * 512
            for c in range(16):
                (u, ru), (a, ra) = ub[k % 3], ab[k % 3]
                k += 1
                if tt_ == 0:
                    S.op("gpsimd", lambda e, o=u[:, 0:3]: e.memset(o, 0.0), [], [ru])
                    S.dma("sync", u[:, 3:515], self.zfm[c * 128:(c + 1) * 128, 0:512], reads=[self.dres["zfm"]], writes=[ru], acc=True)
                else:
                    S.dma("sync", u[:], self.zfm[c * 128:(c + 1) * 128, t0 - 3:t0 + 512], reads=[self.dres["zfm"]], writes=[ru])
                cwl = self.cw[:, l, c, :]
                ts(S, "vector", a[:], u[:, 0:512], cwl[:, 0:1], None, ALU.mult, None, [ru, rc], [ra])
                for j in range(1, 4):
                    stt(S, "vector", a[:], u[:, j:j + 512], cwl[:, j:j + 1], a[:], ALU.mult, ALU.add, [ru, ra, rc], [ra])
                act(S, a[:], a[:], AF.Silu, [ra], [ra])
                S.dma("gpsimd", self.qkc[c * 128:(c + 1) * 128, t0:t0 + 512], a[:], reads=[ra], writes=[self.dres["qkc"]], acc=True)
        for a in (A0, A1, A2):
            a.reset()
        qTb = [A0.alloc([128, 8, 128], "qT%d" % i) for i in range(2)]
        kTb = [A0.alloc([128, 8, 128], "kT%d" % i) for i in range(2)]
        sob = [A1.alloc([128, 8, 128], "so%d" % i) for i in range(2)]
        yab = [A1.alloc([128, 8, 128], "ya%d" % i) for i in range(2)]
        grb = [A1.alloc([128, 8], "gr%d" % i) for i in range(2)]
        Cst, r_C = A2.alloc([128, 4, 2, 384], "Cst")
        vxb = [A2.alloc([128, 4, 384], "vx%d" % i) for i in range(2)]
        tmp = []
        for i in range(2):
            d = {}
            for nm, shp in (("nb", [128, 128]), ("ebt", [128, 128]), ("Dt", [128, 128]), ("W0", [128, 128]), ("Wt", [128, 128]),
                            ("qtl", [128, 2, 128]), ("dm", [128, 128]), ("rd", [128, 128]), ("hh", [128, 2, 128]),
                            ("sqh", [128, 2, 128]), ("rr", [128, 128]), ("kt", [128, 256]), ("yt", [128, 128])):
                d[nm] = A2.alloc(shp, nm + str(i))
            tmp.append(d)
        gts = []
        for i in range(2):
            d = {}
            for nm, w in (("gz", 8), ("e1", 4), ("nlf", 4), ("imb", 4), ("bD", 4), ("gl", 4), ("gcol", 4), ("dcol", 4)):
                d[nm] = A1.alloc([128, w], nm + str(i))
            gts.append(d)
        S.op("vector", lambda e: e.memset(Cst[:], 0.0), [], [r_C])
        for i in range(2):
            S.op("gpsimd", lambda e, o=vxb[i][0][:, :, 256:384]: e.memset(o, 1.0), [], [vxb[i][1]])
        r_mo = FM_ROW["m_o"]
        hk = 0
        for ch in range(T // 128):
            t0 = ch * 128
            (qT, rq), (kT, rk), (so, rso), (ya, rya), (gr, rgr) = qTb[ch % 2], kTb[ch % 2], sob[ch % 2], yab[ch % 2], grb[ch % 2]
            vx, rvx = vxb[ch % 2]
            g = gts[ch % 2]
            S.dma("sync", qT[:], self.qkc[0:1024, t0:t0 + 128].rearrange("(c p) t -> p c t", p=128), reads=[self.dres["qkc"]], writes=[rq])
            S.dma("sync", kT[:], self.qkc[1024:2048, t0:t0 + 128].rearrange("(c p) t -> p c t", p=128), reads=[self.dres["qkc"]], writes=[rk])
            S.dma("sync", so[:], self.zfm[r_mo:r_mo + 1024, t0:t0 + 128].rearrange("(c p) t -> p c t", p=128), reads=[self.dres["zfm"]], writes=[rso])
            S.dma("sync", vx[:, :, 0:256], self.ztm[t0:t0 + 128, 0:1024].rearrange("p (h d) -> p h d", h=4),
                  reads=[self.dres["ztm"]], writes=[rvx], acc=True)
            S.dma("sync", gr[:], self.ztm[t0:t0 + 128, TM_COL["m_i"]:TM_COL["m_i"] + 8], reads=[self.dres["ztm"]], writes=[rgr])
            gz, rgz = g["gz"]; e1, re1 = g["e1"]; nlf, rnlf = g["nlf"]; imb, rimb = g["imb"]; bD, rbD = g["bD"]
            gl, rgl = g["gl"]; gcol, rgcol = g["gcol"]; dcol, rdcol = g["dcol"]
            tt(S, "vector", gz[:], gr[:], self.gsm[:, l, 16:24], ALU.add, [rgr, rc], [rgz])
            act(S, e1[:], gz[:, 4:8], AF.Exp, [rgz], [re1], scale=-1.0)
            act(S, nlf[:], e1[:], AF.Ln, [re1], [rnlf], bias=1.0)
            psg, rpsg = self.psr.next()
            mm(S, psg[:, 0:4], self.tri, nlf[:], True, True, [rc, rnlf], [rpsg])
            mm(S, psg[:, 4:8], self.ones, nlf[:], True, True, [rc, rnlf], [rpsg])
            tt(S, "vector", imb[:], gz[:, 0:4], psg[:, 0:4], ALU.add, [rgz, rpsg], [rimb])
            ts(S, "vector", bD[:], imb[:], -LN16, None, ALU.add, None, [rimb], [rbD])
            tt(S, "vector", gl[:], imb[:], psg[:, 4:8], ALU.subtract, [rimb, rpsg], [rgl])
            act(S, gcol[:], gl[:], AF.Exp, [rgl], [rgcol])
            act(S, dcol[:], psg[:, 4:8], AF.Exp, [rpsg], [rdcol], scale=-1.0)
            for h in range(4):
                tp = tmp[hk % 2]
                hk += 1
                nb, rnb = tp["nb"]; ebt, rebt = tp["ebt"]; Dt, rDt = tp["Dt"]; W0, rW0 = tp["W0"]; Wt, rWt = tp["Wt"]
                qtl, rqtl = tp["qtl"]; dm, rdm = tp["dm"]; rd, rrd = tp["rd"]; hh_, rhh = tp["hh"]; sqh, rsqh = tp["sqh"]
                rr, rrr = tp["rr"]; kt, rkt = tp["kt"]; yt, ryt = tp["yt"]
                cp(S, "vector", nb[:], nlf[:, h:h + 1].to_broadcast([128, 128]), [rnlf], [rnb])
                psA, rpsA = self.psr.next()
                mm(S, psA[:, 0:128], nb[:], self.tri, True, True, [rnb, rc], [rpsA])
                act(S, ebt[:], psA[:, 0:128], AF.Exp, [rpsA], [rebt], scale=-1.0, bias=-LN16)
                act(S, Dt[:], psA[:, 0:128], AF.Exp, [rpsA, rbD], [rDt], scale=-1.0, bias=bD[:, h:h + 1])
                for j in range(2):
                    mm(S, psA[:, 128:256], kT[:, 2 * h + j, :], qT[:, 2 * h + j, :], j == 0, j == 1, [rk, rq], [rpsA])
                tt(S, "gpsimd", W0[:], Dt[:], self.tri, ALU.mult, [rDt, rc], [rW0])
                tt(S, "vector", Wt[:], W0[:], psA[:, 128:256], ALU.mult, [rW0, rpsA], [rWt])
                for j in range(2):
                    tt(S, "gpsimd", qtl[:, j, :], qT[:, 2 * h + j, :], ebt[:], ALU.mult, [rq, rebt], [rqtl], acc=(j > 0))
                psN, rpsN = self.psr.next()
                for jv in range(3):
                    o = psN[:, jv * 128:(jv + 1) * 128]
                    mm(S, o, vx[:, h, jv * 128:(jv + 1) * 128], Wt[:], True, False, [rvx, rWt], [rpsN], inc=False)
                    for j in range(2):
                        mm(S, o, Cst[:, h, j, jv * 128:(jv + 1) * 128], qtl[:, j, :], False, j == 1, [r_C, rqtl], [rpsN],
                           inc=(j == 1 and jv == 2))
                act(S, dm[:], psN[:, 256:384], AF.Abs, [rpsN], [rdm])
                ts(S, "vector", dm[:], dm[:], 1.0, None, ALU.max, None, [rdm], [rdm])
                S.op("vector", lambda e, o=rd[:], i=dm[:]: e.reciprocal(out=o, in_=i), [rdm], [rrd])
                for jv in range(2):
                    tt(S, "vector", hh_[:, jv, :], psN[:, jv * 128:(jv + 1) * 128], rd[:], ALU.mult, [rpsN, rrd], [rhh], acc=(jv > 0))
                act(S, sqh[:], hh_[:], AF.Square, [rhh], [rsqh])
                psR, rpsR = self.psr.next()
                for jv in range(2):
                    mm(S, psR[:, 0:128], self.ones, sqh[:, jv, :], jv == 0, jv == 1, [rc, rsqh], [rpsR])
                self.rstd(rr[:], rrr, psR[:, 0:128], rpsR, 256)
                for jv in range(2):
                    stt(S, "vector", yt[:], hh_[:, jv, :], self.gsm[:, l, 4 + 2 * h + jv:5 + 2 * h + jv], rr[:], ALU.mult, ALU.mult,
                        [rhh, rrr, rc], [ryt])
                    tt(S, "gpsimd", ya[:, 2 * h + jv, :], yt[:], so[:, 2 * h + jv, :], ALU.mult, [ryt, rso], [rya],
                       acc=(h > 0 or jv > 0))
                for j in range(2):
                    S.op("tensor", lambda e, o=psR[:, 128 + j * 128:256 + j * 128], i=kT[:, 2 * h + j, :], idn=self.ident:
                         e.transpose(out=o, in_=i, identity=idn), [rk, rc], [rpsR], acc=True)
                ts(S, "vector", kt[:], psR[:, 128:384], gcol[:, h:h + 1], None, ALU.mult, None, [rpsR, rgcol], [rkt])
                for j in range(2):
                    psU, rpsU = self.psr.next()
                    mm(S, psU[:, 0:384], kt[:, j * 128:(j + 1) * 128], vx[:, h, :], True, True, [rkt, rvx], [rpsU])
                    stt(S, "vector", Cst[:, h, j, :], Cst[:, h, j, :], dcol[:, h:h + 1], psU[:, 0:384], ALU.mult, ALU.add,
                        [r_C, rdcol, rpsU], [r_C], acc=True)
            S.dma("gpsimd", self.yT[0][:, t0:t0 + 128].rearrange("(c p) t -> p c t", p=128), ya[:], reads=[rya],
                  writes=[self.dres["yaT"]], acc=True)

    def phase_N(self, l):
        S = self.S
        T, NQT, NCH, NCP, NC = self.T, self.NQT, self.NCH, self.NCP, self.NC
        rc = self.r_const
        A0, A1, A2 = self.AR0, self.AR1, self.AR2
        if not hasattr(self, "AR3"):
            self.AR3 = Arena(S, "ar3", 5120)
        A3 = self.AR3
        for a in (A0, A1, A2, A3):
            a.reset()
        zres = self.dres["zfm"]
        cosb = [A0.alloc([128, 512], "cos%d" % i) for i in range(2)]
        sinb = [A0.alloc([128, 512], "sin%d" % i) for i in range(2)]
        xin = [A0.alloc([128, 512], "xin%d" % i) for i in range(3)]
        sqb = [A0.alloc([128, 512], "sqb%d" % i) for i in range(2)]
        xnb = [A0.alloc([128, 512], "xnb%d" % i) for i in range(2)]
        t1b = [A0.alloc([128, 512], "t1b%d" % i) for i in range(2)]
        rrb = [A0.alloc([128, 512], "rrb%d" % i) for i in range(2)]
        items = [(FM_ROW["n_q"] + c * 128, 12, self.nqr[c * 128:(c + 1) * 128], "nqr") for c in range(8)]
        items += [(FM_ROW["n_ks"] + c * 128, 14, self.nkr[c * 128:(c + 1) * 128], "nkr") for c in range(2)]
        items += [(FM_ROW["n_kw"] + c * 128, 15, self.nkr[256 + c * 128:256 + (c + 1) * 128], "nkr") for c in range(2)]
        k = 0
        for tt_ in range(self.NT):
            t0 = tt_ * 512
            (cs, rcs), (sn, rsn) = cosb[tt_ % 2], sinb[tt_ % 2]
            S.dma("sync", cs[:], self.c_cos[:, t0:t0 + 512], writes=[rcs])
            S.dma("sync", sn[:], self.c_sin[:, t0:t0 + 512], writes=[rsn])
            for (row, gi, dst, dname) in items:
                (x, rx) = xin[k % 3]
                (sq, rsq), (xn, rxn), (t1, rt1), (rr, rrr) = sqb[k % 2], xnb[k % 2], t1b[k % 2], rrb[k % 2]
                k += 1
                S.dma("sync", x[:], self.zfm[row:row + 128, t0:t0 + 512], reads=[zres], writes=[rx])
                act(S, sq[:], x[:], AF.Square, [rx], [rsq])
                ps, rps = self.psr.next()
                mm(S, ps[:], self.bd64, sq[:], True, True, [rc, rsq], [rps])
                self.rstd(rr[:], rrr, ps[:], rps, 64)
                stt(S, "vector", xn[:], x[:], self.gsm[:, l, gi:gi + 1], rr[:], ALU.mult, ALU.mult, [rx, rrr, rc], [rxn])
                ps2, rps2 = self.psr.next()
                mm(S, ps2[:], self.rm, xn[:], True, True, [rc, rxn], [rps2])
                tt(S, "gpsimd", t1[:], xn[:], cs[:], ALU.mult, [rxn, rcs], [rt1])
                ob, rob = self.obr.next()
                tt(S, "vector", ob[:], ps2[:], sn[:], ALU.mult, [rps2, rsn], [rob])
                tt(S, "gpsimd", ob[:], ob[:], t1[:], ALU.add, [rob, rt1], [rob])
                S.dma("gpsimd", dst[:, t0:t0 + 512], ob[:], reads=[rob], writes=[self.dres[dname]], acc=True)
        for a in (A0, A1):
            a.reset()
        Gm, r_G = A2.alloc([128, T], "Gm")
        kcT, r_kcT = A2.alloc([64, 4, NCP], "kcT")
        vcs, r_vcs = A2.alloc([128, 4, NCH, 64], "vcs")
        S.dma("sync", Gm[:], self.c_g, writes=[r_G])
        w1, r_w1 = A0.alloc([64, 2, 32, 128], "w1")
        u, r_u = A1.alloc([64, T], "u")
        pe, r_pe = A3.alloc([64, 2, 32], "pe")
        w2, r_w2 = A3.alloc([128, 2, 64], "w2")
        bcol, r_bcol = A3.alloc([128, 2], "bcol")
        cosc, r_cosc = A3.alloc([64, NCP], "cosc")
        sinc, r_sinc = A3.alloc([64, NCP], "sinc")
        sqc, r_sqc = A3.alloc([64, NCP], "sqc")
        xnc, r_xnc = A3.alloc([64, NCP], "xnc")
        t1c, r_t1c = A3.alloc([64, NCP], "t1c")
        rrc, r_rrc = A3.alloc([64, NCP], "rrc")
        hs, r_hs = A3.alloc([128, NCP], "hs")
        S.dma("sync", w1[:], self.n_w1[l].rearrange("k d j m -> d k j m"), writes=[r_w1])
        S.dma("sync", pe[:], self.n_pe[l].rearrange("k d j -> d k j"), writes=[r_pe])
        S.dma("sync", w2[:], self.n_w2[l].rearrange("k p m -> p k m"), writes=[r_w2])
        S.dma("sync", cosc[:], self.c_cosc, writes=[r_cosc])
        S.dma("sync", sinc[:], self.c_sinc, writes=[r_sinc])
        S.op("vector", lambda e: e.memset(kcT[:], 0.0), [], [r_kcT])
        S.op("gpsimd", lambda e: e.memset(hs[:], 0.0), [], [r_hs])
        for kind in range(2):
            ps, rps = self.psr.next()
            for j in range(32):
                mm(S, ps[:, 0:1], w1[:, kind, j, :], pe[:, kind, j:j + 1], j == 0, j == 31, [r_w1, r_pe], [rps])
            cp(S, "vector", bcol[:, kind:kind + 1], ps[:, 0:1], [rps], [r_bcol], acc=(kind > 0))
        u3 = u.rearrange("p (n s) -> p n s", s=16)
        for g in range(4):
            for kind in range(2):
                row = FM_ROW["n_kc" if kind == 0 else "n_vc"] + g * 64
                S.dma("sync", u[:], self.zfm[row:row + 64, :], reads=[zres], writes=[r_u])
                ps, rps = self.psr.next()
                for j in range(32):
                    a, jj = (0, j) if j < 16 else (1, j - 16)
                    mm(S, ps[:, 0:NC], w1[:, kind, j, :], u3[:, a:a + NC, jj], j == 0, j == 31, [r_w1, r_u], [rps])
                act(S, hs[:, 0:NC], ps[:, 0:NC], AF.Silu, [rps, r_bcol], [r_hs], bias=bcol[:, kind:kind + 1], acc=True)
                if kind == 0:
                    ps2, rps2 = self.psr.next()
                    mm(S, ps2[0:64, 0:NC], w2[:, 0, :], hs[:, 0:NC], True, True, [r_w2, r_hs], [rps2])
                    act(S, sqc[:, 0:NC], ps2[0:64, 0:NC], AF.Square, [rps2], [r_sqc])
                    ps3, rps3 = self.psr.next()
                    mm(S, ps3[0:64, 0:NC], self.ones[0:64, 0:64], sqc[:, 0:NC], True, True, [rc, r_sqc], [rps3])
                    self.rstd(rrc[:, 0:NC], r_rrc, ps3[0:64, 0:NC], rps3, 64)
                    stt(S, "vector", xnc[:, 0:NC], ps2[0:64, 0:NC], self.gsm[0:64, l, 13:14], rrc[:, 0:NC], ALU.mult, ALU.mult,
                        [rps2, r_rrc, rc], [r_xnc])
                    ps4, rps4 = self.psr.next()
                    mm(S, ps4[0:64, 0:NC], self.rm[0:64, 0:64], xnc[:, 0:NC], True, True, [rc, r_xnc], [rps4])
                    tt(S, "gpsimd", t1c[:, 0:NC], xnc[:, 0:NC], cosc[:, 0:NC], ALU.mult, [r_xnc, r_cosc], [r_t1c])
                    tt(S, "vector", kcT[:, g, 0:NC], ps4[0:64, 0:NC], sinc[:, 0:NC], ALU.mult, [rps4, r_sinc], [r_kcT], acc=True)
                    tt(S, "gpsimd", kcT[:, g, 0:NC], kcT[:, g, 0:NC], t1c[:, 0:NC], ALU.add, [r_kcT, r_t1c], [r_kcT], acc=True)
                else:
                    for ch in range(NCH):
                        ps5, rps5 = self.psr.next()
                        mm(S, ps5[:, 0:64], hs[:, ch * 128:(ch + 1) * 128], w2[:, 1, :], True, True, [r_hs, r_w2], [rps5])
                        cp(S, "scalar", vcs[:, g, ch, :], ps5[:, 0:64], [rps5], [r_vcs], acc=True)
        for a in (A0, A1, A3):
            a.reset()
        kT2, r_kT2 = A0.alloc([128, T], "kT2")
        vs, r_vs = A1.alloc([128, NQT, 64], "vs")
        Et, r_Et = A1.alloc([128, NCH, 512], "Et")
        Q2b = [A1.alloc([128, 4, 128], "Q2%d" % i) for i in range(2)]
        gtb = [A3.alloc([64, 12, 128], "gt%d" % i) for i in range(1)]
        negc, r_negc = A3.alloc([128, 512], "negc")
        nega, r_nega = A3.alloc([128, 512], "nega")
        qidx, r_qidx = A3.alloc([128, 512], "qidx")
        thr, r_thr = A3.alloc([128, NCH * NQT], "thr")
        mimp, r_mimp = A3.alloc([128, NCH, 128], "mimp")
        vwb = [A3.alloc([128, 5, 64], "vw%d" % i) for i in range(1)]
        sbb = [A3.alloc([128, 128], "sb%d" % i) for i in range(2)]
        score, r_score = A3.alloc([128, 128], "score")
        sc2, r_sc2 = A3.alloc([128, 128], "sc2")
        selb, r_selb = A3.alloc([128, 128], "selb")
        rdT, r_rdT = A3.alloc([128, 4], "rdT")
        m8, r_m8 = A3.alloc([128, 16], "m8")
        selT, r_selT = self.rs, self.r_rs
        S.dma("sync", negc[:], self.c_negc, writes=[r_negc])
        S.dma("sync", nega[:], self.c_nega, writes=[r_nega])
        S.dma("sync", qidx[:], self.c_qidx, writes=[r_qidx])
        S.dma("sync", thr[:], self.c_thr, writes=[r_thr])
        S.dma("sync", mimp[:], self.c_mimp, writes=[r_mimp])
        ngrow = FM_ROW["n_g"]
        pa, rpa = self.pa, self.r_pa
        o64 = self.ones[:, 0:64]
        qk = 0
        for g in range(4):
            S.dma("sync", kT2[0:64, :], self.nkr[g * 64:(g + 1) * 64, :], reads=[self.dres["nkr"]], writes=[r_kT2])
            S.dma("sync", kT2[64:128, :], self.nkr[256 + g * 64:256 + (g + 1) * 64, :], reads=[self.dres["nkr"]], writes=[r_kT2], acc=True)
            c0 = TM_COL["n_vs"] + g * 64
            S.dma("sync", vs[:], self.ztm[:, c0:c0 + 64].rearrange("(k p) d -> p k d", p=128), reads=[self.dres["ztm"]], writes=[r_vs])
            cw0 = TM_COL["n_vw"] + g * 64
            for qt in range(NQT):
                t0 = qt * 128
                Q2, rQ = Q2b[qk % 2]
                vw, rvw = vwb[0]
                qk += 1
                gt, rgt = gtb[0]
                qsrc = self.nqr[g * 256:(g + 1) * 256, t0:t0 + 128].rearrange("(r d) q -> d r q", d=64)
                S.dma("sync", Q2[0:64], qsrc, reads=[self.dres["nqr"]], writes=[rQ])
                S.dma("sync", Q2[64:128], qsrc, reads=[self.dres["nqr"]], writes=[rQ], acc=True)
                S.dma("sync", gt[:], self.zfm[ngrow + g * 12:ngrow + (g + 1) * 12, t0:t0 + 128].partition_broadcast(64),
                      reads=[zres], writes=[rgt])
                sb, rsb = sbb[qk % 2]
                S.dma("sync", sb[:, 0:128], self.c_sbias[qt], writes=[rsb])
                k0 = max(0, qt - 4)
                nk = qt - k0 + 1
                S.dma("sync", vw[:, 0:nk, :], self.ztm[k0 * 128:(qt + 1) * 128, cw0:cw0 + 64].rearrange("(k p) d -> p k d", p=128),
                      reads=[self.dres["ztm"]], writes=[rvw])
                Qlo = Q2[0:64].rearrange("p r q -> p (r q)")
                Qhi = Q2[64:128].rearrange("p r q -> p (r q)")
                gt4 = gt.rearrange("p (r j) q -> p r j q", j=3)
                yacc, ryacc = self.tmr.next()

                def finalize(br, first):
                    num, rnum = (pa[0], rpa[0]) if br < 2 else (pa[2], rpa[2])
                    den, rden = (pa[1], rpa[1]) if br < 2 else (pa[3], rpa[3])
                    rcp, rrcp = self.sgr.next()
                    ts(S, "vector", rcp[0:64, :], den[0:64, :], 1e-30, None, ALU.max, None, [rden], [rrcp])
                    S.op("vector", lambda e, o=rcp[0:64, :]: e.reciprocal(out=o, in_=o), [rrcp], [rrcp])
                    tt(S, "gpsimd", rcp[0:64, :].rearrange("p (r q) -> p r q", r=4), rcp[0:64, :].rearrange("p (r q) -> p r q", r=4),
                       gt4[:, :, br, :], ALU.mult, [rrcp, rgt], [rrcp])
                    if first:
                        tt(S, "vector", yacc[0:64, :], num[0:64, :], rcp[0:64, :], ALU.mult, [rnum, rrcp], [ryacc])
                    else:
                        tt(S, "vector", rcp[0:64, :], num[0:64, :], rcp[0:64, :], ALU.mult, [rnum, rrcp], [rrcp])
                        tt(S, "gpsimd", yacc[0:64, :], yacc[0:64, :], rcp[0:64, :], ALU.add, [ryacc, rrcp], [ryacc])

                jmax = min((8 * qt + 6) // 128, NCH - 1)
                for j in range(jmax + 1):
                    ps, rps = self.psr.next()
                    mm(S, ps[:], kcT[:, g, j * 128:(j + 1) * 128], Qlo, True, True, [r_kcT, rQ], [rps])
                    act(S, Et[:, j, :], ps[:], AF.Exp, [rps], [r_Et], scale=0.125, acc=(j > 0))
                    if 16 * (128 * j + 127) + 31 > t0:
                        stt(S, "vector", Et[:, j, :], qidx[:], thr[:, j * NQT + qt:j * NQT + qt + 1], Et[:, j, :], ALU.is_ge, ALU.mult,
                            [r_qidx, r_thr, r_Et], [r_Et], acc=True)
                for j in range(jmax + 1):
                    mm(S, pa[0][0:64, :], vcs[:, g, j, :], Et[:, j, :], j == 0, j == jmax, [r_vcs, r_Et], [rpa[0]])
                for j in range(jmax + 1):
                    mm(S, pa[1][0:64, :], o64, Et[:, j, :], j == 0, j == jmax, [rc, r_Et], [rpa[1]])
                for r in range(4):
                    for j in range(jmax + 1):
                        mm(S, pa[2][:, r * 128:(r + 1) * 128], Et[:, j, r * 128:(r + 1) * 128], mimp[:, j, :], j == 0, j == jmax,
                           [r_Et, r_mimp], [rpa[2]], inc=(j == jmax and r == 3))
                for r in range(4):
                    for j in range(jmax + 1):
                        mm(S, pa[3][:, r:r + 1], Et[:, j, r * 128:(r + 1) * 128], self.ones[:, 0:1], j == 0, j == jmax,
                           [r_Et, rc], [rpa[3]], inc=(j == jmax and r == 3))
                ts(S, "vector", rdT[:], pa[3][:, 0:4], 1e-30, None, ALU.max, None, [rpa[3]], [r_rdT])
                S.op("vector", lambda e, o=rdT[:]: e.reciprocal(out=o, in_=o), [r_rdT], [r_rdT])
                stt(S, "vector", score[:], pa[2][:, 0:128], rdT[:, 0:1], sb[:, 0:128], ALU.mult, ALU.add, [rpa[2], r_rdT, rsb], [r_score])
                for r in range(1, 4):
                    stt(S, "vector", score[:], pa[2][:, r * 128:(r + 1) * 128], rdT[:, r:r + 1], score[:], ALU.mult, ALU.add,
                        [rpa[2], r_rdT, r_score], [r_score])
                S.op("vector", lambda e: e.max(out=m8[:, 0:8], in_=score[:]), [r_score], [r_m8])
                S.op("vector", lambda e: e.match_replace(out=sc2[:], in_to_replace=m8[:, 0:8], in_values=score[:], imm_value=-1e9),
                     [r_score, r_m8], [r_sc2])
                S.op("vector", lambda e: e.max(out=m8[:, 8:16], in_=sc2[:]), [r_sc2], [r_m8], acc=True)
                ts(S, "vector", selb[:], score[:], m8[:, 15:16], NEG, ALU.is_lt, ALU.mult, [r_score, r_m8], [r_selb])
                psT, rpsT = self.psr.next()
                tr(S, psT[:, 0:128], selb[:], self.ident, [r_selb, rc], [rpsT])
                for r in range(4):
                    cp(S, "scalar" if r % 2 else "vector", selT[:, r * 128:(r + 1) * 128], psT[:, 0:128], [rpsT], [r_selT], acc=(r > 0))
                finalize(0, True)
                for kt in range(qt + 1):
                    ps, rps = self.psr.next()
                    mm(S, ps[:], kT2[0:64, kt * 128:(kt + 1) * 128], Qlo, True, False, [r_kT2, rQ], [rps], inc=False)
                    mm(S, ps[:], Gm[:, kt * 128:(kt + 1) * 128], selT[:], False, kt != qt, [r_G, r_selT], [rps])
                    if kt == qt:
                        mm(S, ps[:], self.ident, negc[:], False, True, [rc, r_negc], [rps])
                    pt, rpt = self.obr.next()
                    act(S, pt[:], ps[:], AF.Exp, [rps], [rpt], scale=0.125)
                    mm(S, pa[0][0:64, :], vs[:, kt, :], pt[:], kt == 0, kt == qt, [r_vs, rpt], [rpa[0]])
                    mm(S, pa[1][0:64, :], o64, pt[:], kt == 0, kt == qt, [rc, rpt], [rpa[1]])
                finalize(1, False)
                for kt in range(k0, qt + 1):
                    ps, rps = self.psr.next()
                    last_c = (kt == qt)
                    last_a = (kt == qt - 4)
                    mm(S, ps[:], kT2[64:128, kt * 128:(kt + 1) * 128], Qhi, True, not (last_c or last_a), [r_kT2, rQ], [rps])
                    if last_c:
                        mm(S, ps[:], self.ident, negc[:], False, True, [rc, r_negc], [rps])
                    if last_a:
                        mm(S, ps[:], self.ident, nega[:], False, True, [rc, r_nega], [rps])
                    pt, rpt = self.obr.next()
                    act(S, pt[:], ps[:], AF.Exp, [rps], [rpt], scale=0.125)
                    mm(S, pa[2][0:64, :], vw[:, kt - k0, :], pt[:], kt == k0, kt == qt, [rvw, rpt], [rpa[2]])
                    mm(S, pa[3][0:64, :], o64, pt[:], kt == k0, kt == qt, [rc, rpt], [rpa[3]])
                finalize(2, False)
                S.dma("gpsimd", self.yT[1][g * 256:(g + 1) * 256, t0:t0 + 128].rearrange("(r d) q -> d r q", d=64),
                      yacc[0:64, :].rearrange("p (r q) -> p r q", r=4), reads=[ryacc], writes=[self.dres["ybT"]], acc=True)


def build(T, L, dbg=(), phases="ACMNE"):
    P = Prog(T, L, dbg)
    S = P.S
    src, rsrc = P.xT, Res("xT")
    for l in range(L):
        if l == L - 1:
            dst, rdst = P.outT, P.dres["outT"]
        else:
            dst, rdst = P.xbuf[l % 2], P.dres["xs%d" % (l % 2)]
        if "A" in phases:
            P.phase_A(l, src, rsrc)
        if "C" in phases:
            P.phase_C(l)
        if "M" in phases:
            P.phase_M(l)
        if "N" in phases:
            P.phase_N(l)
        if "E" in phases:
            P.phase_E(l, src, rsrc, dst, rdst)
        src, rsrc = dst, rdst
    fin = [P.dres["outT"]] + [P.dres[k] for k in P.dbg if k in P.dres]
    S.finish(fin)
    return P


def tile_w(W):
    K, M = W.shape
    return np.ascontiguousarray(W.reshape(K // 128, 128, M // 128, 128).transpose(2, 1, 0, 3))


def colmajor(g, L, nchunk):
    return np.ascontiguousarray(g.reshape(L, nchunk, 128).transpose(0, 2, 1))


def prep_weights(inp, L):
    out = {}
    f = np.float32
    w_in = inp["w_in"]
    cols = []
    for n in FM_SEGS:
        o, s = _off[n]
        cols.append(w_in[:, :, o:o + s])
    cols.append(np.zeros((L, D, N_FM_BLK * 128 - N_FM), f))
    for n in TM_SEGS:
        o, s = _off[n]
        cols.append(w_in[:, :, o:o + s])
    cols.append(np.zeros((L, D, N_TM_BLK * 128 - N_TM), f))
    wr = np.concatenate(cols, axis=2)
    out["w_in"] = np.stack([tile_w(wr[l]) for l in range(L)])
    out["gmix"] = colmajor(inp["norm_mix_g"], L, 16)
    out["gffn"] = colmajor(inp["norm_ffn_g"], L, 16)
    out["gmem"] = colmajor(inp["norm_mem_g"], L, 16)
    out["w_up"] = np.stack([np.stack([tile_w(inp[k][l]) for k in ("w_up_a", "w_up_b", "w_up_c")]) for l in range(L)])
    out["w_out"] = np.stack([tile_w(inp["w_out"][l]) for l in range(L)])
    out["w_fg"] = np.stack([tile_w(inp["w_ffn_gate"][l]) for l in range(L)])
    out["w_fu"] = np.stack([tile_w(inp["w_ffn_up"][l]) for l in range(L)])
    out["w_fd"] = np.stack([tile_w(inp["w_ffn_down"][l]) for l in range(L)])
    out["w_mk"] = np.stack([tile_w(inp["w_mem_k"][l]) for l in range(L)])
    out["w_mv"] = np.stack([tile_w(inp["w_mem_v"][l]) for l in range(L)])
    out["g_cq"] = colmajor(inp["c_q_norm_g"], L, 2)
    out["g_ck"] = colmajor(inp["c_k_norm_g"], L, 2)
    out["m_cw"] = np.ascontiguousarray(inp["m_conv_w"].reshape(L, 4, 16, 128).transpose(0, 3, 2, 1))
    out["m_b8"] = np.concatenate([inp["m_i_bias"], inp["m_f_bias"]], axis=1).reshape(L, 1, 8).astype(f)
    out["m_gn"] = colmajor(inp["m_norm_g"].reshape(L, 1024), L, 8)
    g4 = np.stack([inp["n_q_norm_g"], inp["n_kc_norm_g"], inp["n_ks_norm_g"], inp["n_kw_norm_g"]], axis=2)
    out["g_n4"] = np.ascontiguousarray(np.concatenate([g4, g4], axis=1))
    out["n_w1"] = np.stack([np.stack([inp[k][l].reshape(32, 64, 128).transpose(1, 0, 2) for k in ("n_cmp_w1_k", "n_cmp_w1_v")])
                            for l in range(L)])
    out["n_pe"] = np.stack([np.stack([inp[k][l].T for k in ("n_cmp_pe_k", "n_cmp_pe_v")]) for l in range(L)])
    out["n_w2"] = np.stack([np.stack([inp[k][l] for k in ("n_cmp_w2_k", "n_cmp_w2_v")]) for l in range(L)])
    return {k: np.ascontiguousarray(v, dtype=f) for k, v in out.items()}


def consts(T):
    f = np.float32
    NQT = T // 128
    NC = T // 16 - 1
    NCH = (NC + 127) // 128
    NCP = NCH * 128
    c = {}
    c128 = np.zeros((5, 128, 128), f)
    c128[C_ONES] = 1.0
    c128[C_ID] = np.eye(128, dtype=f)
    c128[C_TRI] = np.triu(np.ones((128, 128), f))
    bd = np.zeros((128, 128), f)
    bd[:64, :64] = 1.0
    bd[64:, 64:] = 1.0
    c128[C_BD64] = bd
    rm = np.zeros((128, 128), f)
    for hb in (0, 64):
        for m in range(64):
            if m < 32:
                rm[hb + m + 32, hb + m] = -1.0
            else:
                rm[hb + m - 32, hb + m] = 1.0
    c128[C_RM] = rm
    c["c128"] = c128
    inv = np.power(f(10000.0), -np.arange(32, dtype=f) * f(2.0) / f(64)).astype(f)
    pos = np.arange(T, dtype=f)
    ang = (pos[:, None] * inv[None, :]).astype(f)
    cs = np.cos(ang).astype(f).T
    sn = np.sin(ang).astype(f).T
    c["c_cos"] = np.ascontiguousarray(np.concatenate([cs, cs, cs, cs], axis=0))
    c["c_sin"] = np.ascontiguousarray(np.concatenate([sn, sn, sn, sn], axis=0))
    cend = (np.arange(NCP, dtype=f) * f(16) + f(31)).astype(f)
    angc = (cend[:, None] * inv[None, :]).astype(f)
    csc = np.cos(angc).astype(f).T
    snc = np.sin(angc).astype(f).T
    c["c_cosc"] = np.ascontiguousarray(np.concatenate([csc, csc], axis=0))
    c["c_sinc"] = np.ascontiguousarray(np.concatenate([snc, snc], axis=0))
    mimp = np.zeros((NCP, 128), f)
    for n in range(NC):
        mimp[n, n // 4] += 1.0
        mimp[n, (n + 1) // 4] += 1.0
    c["c_mimp"] = np.ascontiguousarray(mimp.reshape(NCH, 128, 128).transpose(1, 0, 2))
    g = np.zeros((128, T), f)
    for b in range(T // 64):
        g[b, b * 64:(b + 1) * 64] = 1.0
    c["c_g"] = g
    sb = np.zeros((NQT, 128, 128), f)
    blk = np.arange(128)
    for qt in range(NQT):
        for q in range(128):
            cur = (qt * 128 + q) // 64
            row = np.where(blk > cur, -1.0, 0.0)
            row[(blk == 0) | (blk == cur) | (blk == cur - 1)] = 1.0e4
            sb[qt, q] = row
    c["c_sbias"] = sb
    thr = np.zeros((128, NCH * NQT), f)
    p = np.arange(128)
    for j in range(NCH):
        for qt in range(NQT):
            thr[:, j * NQT + qt] = 16 * (128 * j + p) + 31 - 128 * qt
    c["c_thr"] = thr
    c["c_qidx"] = np.ascontiguousarray(np.tile(np.arange(128, dtype=f)[None, :], (128, 4)))
    kk = np.arange(128)[:, None]
    qq = np.arange(128)[None, :]
    c["c_negc"] = np.ascontiguousarray(np.tile(np.where(kk > qq, NEG, 0.0).astype(f), (1, 4)))
    c["c_nega"] = np.ascontiguousarray(np.tile(np.where(kk <= qq, NEG, 0.0).astype(f), (1, 4)))
    return c


_CACHE = {}


def kernel(**inputs):
    inp = {k: np.asarray(v) for k, v in inputs.items()}
    B, T, _ = inp["x"].shape
    L = inp["w_in"].shape[0]
    key = (T, L)
    if key not in _CACHE:
        _CACHE[key] = build(T, L)
    P = _CACHE[key]
    w = prep_weights(inp, L)
    w.update(consts(T))
    maps = []
    for b in range(B):
        m = {k: w[k] for k in P.inputs if k in w}
        m["xT"] = np.ascontiguousarray(inp["x"][b].T, dtype=np.float32)
        m["memT"] = np.ascontiguousarray(inp["mem"][b].T, dtype=np.float32)
        maps.append(m)
    res = run_bass_kernel_spmd(P.nc, maps, core_ids=list(range(B)))
    out = np.stack([np.asarray(res.results[b]["outT"]).T for b in range(B)])
    return np.ascontiguousarray(out, dtype=np.float32)
```

```python
import contextlib
import numpy as np
import concourse.bass as bass
import concourse.mybir as mybir
from concourse.bass_utils import run_bass_kernel_spmd

F32 = mybir.dt.float32
AF = mybir.ActivationFunctionType
ALU = mybir.AluOpType

D = 2048
DFF = 5632
MEM = 256
EPS = 1e-6
NEG = -30000.0
ENGS = ("tensor", "vector", "scalar", "gpsimd", "sync")

_sizes = (2048, 1024, 1024, 4, 4, 1024, 256, 256, 256, 256, 256, 256, 48, 1024, 2048, 2048, 2048)
_names = ("m_qk", "m_v", "m_o", "m_i", "m_f", "n_q", "n_kc", "n_vc", "n_ks", "n_vs", "n_kw", "n_vw", "n_g",
          "c_q", "g_a", "g_b", "g_c")
_off = {}
_o = 0
for _n, _s in zip(_names, _sizes):
    _off[_n] = (_o, _s)
    _o += _s
D_IN = _o
FM_SEGS = ("m_qk", "m_o", "n_q", "n_kc", "n_vc", "n_ks", "n_kw", "c_q", "g_a", "g_b", "g_c", "n_g")
TM_SEGS = ("m_v", "n_vs", "n_vw", "m_i", "m_f")
SIG_SEGS = ("m_o", "g_a", "g_b", "g_c", "n_g")
FM_ROW = {}
_r = 0
for _n in FM_SEGS:
    FM_ROW[_n] = _r
    _r += _off[_n][1]
N_FM = _r
N_FM_BLK = (N_FM + 127) // 128
TM_COL = {}
_r = 0
for _n in TM_SEGS:
    TM_COL[_n] = _r
    _r += _off[_n][1]
N_TM = _r
N_TM_BLK = (N_TM + 127) // 128
FM_SIG_BLK = set()
for _n in SIG_SEGS:
    for _b in range(FM_ROW[_n] // 128, (FM_ROW[_n] + _off[_n][1] + 127) // 128):
        FM_SIG_BLK.add(_b)


class Res:
    __slots__ = ("name", "w", "r")

    def __init__(self, name=""):
        self.name = name
        self.w = {}
        self.r = {}


class Sched:
    def __init__(self, nc, n_dma_sems=28, strict_same=True):
        self.nc = nc
        self.es = contextlib.ExitStack()
        self.streams = {e: [] for e in ENGS}
        self.sem = {}
        self.cnt = {}
        self.known = {e: {} for e in ENGS}
        self.semobj = {}
        for e in ENGS:
            s = self.es.enter_context(nc.semaphore("s_" + e))
            self.sem[e] = "s_" + e
            self.semobj["s_" + e] = s
            self.cnt[e] = 0
        self.dma_pool = []
        for i in range(n_dma_sems):
            nm = "d%d" % i
            self.semobj[nm] = self.es.enter_context(nc.semaphore(nm))
            self.dma_pool.append([nm, 0])
        self.dma_rr = 0
        self.strict_same = strict_same
        self.n_ops = 0

    def sb(self, name, shape, dtype=F32):
        return self.es.enter_context(self.nc.sbuf_tensor(name, list(shape), dtype))

    def ps(self, name, shape, dtype=F32):
        return self.es.enter_context(self.nc.psum_tensor(name, list(shape), dtype))

    def _waits(self, eng, reads, writes, acc):
        need = {}
        own = self.sem[eng]

        def add(d, skip_own=False):
            for k, v in d.items():
                if skip_own and k == own:
                    continue
                if v > need.get(k, 0):
                    need[k] = v

        for r in reads:
            add(r.w)
        for w in writes:
            add(w.w, skip_own=acc)
            add(w.r)
        kn = self.known[eng]
        for k, v in need.items():
            if k == own and (eng == "tensor" or not self.strict_same):
                continue
            if kn.get(k, 0) >= v:
                continue
            kn[k] = v
            self.streams[eng].append(("w", k, v))

    def op(self, eng, fn, reads=(), writes=(), inc=True, acc=False):
        self._waits(eng, reads, writes, acc)
        own = self.sem[eng]
        if inc:
            self.cnt[eng] += 1
            tok = self.cnt[eng]
        else:
            tok = self.cnt[eng] + 1
        self.streams[eng].append(("o", fn, inc))
        for r in reads:
            if r.r.get(own, 0) < tok:
                r.r[own] = tok
        for w in writes:
            if acc:
                if w.w.get(own, 0) < tok:
                    w.w[own] = tok
            else:
                w.w = {own: tok}
                w.r = {}
        self.n_ops += 1

    def dma(self, q, out, in_, reads=(), writes=(), acc=False, **kw):
        ent = self.dma_pool[self.dma_rr]
        self.dma_rr = (self.dma_rr + 1) % len(self.dma_pool)
        nm = ent[0]
        kn = self.known[q]
        if ent[1] > 0 and kn.get(nm, 0) < ent[1]:
            kn[nm] = ent[1]
            self.streams[q].append(("w", nm, ent[1]))
        self._waits(q, reads, writes, acc)
        ent[1] += 16
        tok = ent[1]
        self.streams[q].append(("d", out, in_, nm, kw))
        for r in reads:
            if r.r.get(nm, 0) < tok:
                r.r[nm] = tok
        for w in writes:
            if acc:
                if w.w.get(nm, 0) < tok:
                    w.w[nm] = tok
            else:
                w.w = {nm: tok}
                w.r = {}
        self.n_ops += 1

    def finish(self, final_res):
        nc = self.nc
        need = {}
        for r in final_res:
            for k, v in r.w.items():
                need[k] = max(need.get(k, 0), v)
        for k, v in need.items():
            self.streams["sync"].append(("w", k, v))
        semobj = self.semobj
        streams = self.streams
        sem = self.sem

        def replay(name, e):
            own = semobj[sem[name]]
            for it in streams[name]:
                if it[0] == "w":
                    e.wait_ge(semobj[it[1]], it[2])
                elif it[0] == "o":
                    ins = it[1](e)
                    if it[2]:
                        ins.then_inc(own, 1)
                else:
                    e.dma_start(out=it[1], in_=it[2], **it[4]).then_inc(semobj[it[3]], 16)

        with nc.Block() as block:
            @block.sync
            def _(e):
                replay("sync", e)

            @block.tensor
            def _(e):
                replay("tensor", e)

            @block.vector
            def _(e):
                replay("vector", e)

            @block.scalar
            def _(e):
                replay("scalar", e)

            @block.gpsimd
            def _(e):
                replay("gpsimd", e)
        self.es.close()


def mm(S, out, lhsT, rhs, start, stop, reads, writes, inc=None):
    S.op("tensor", lambda e: e.matmul(out, lhsT=lhsT, rhs=rhs, start=start, stop=stop), reads, writes,
         inc=(stop if inc is None else inc), acc=(not start))


def tr(S, out, in_, ident, reads, writes):
    S.op("tensor", lambda e: e.transpose(out=out, in_=in_, identity=ident), reads, writes)


def act(S, out, in_, func, reads, writes, bias=None, scale=None, acc=False):
    kw = {}
    if bias is not None:
        kw["bias"] = bias
    if scale is not None:
        kw["scale"] = scale
    S.op("scalar", lambda e: e.activation(out=out, in_=in_, func=func, **kw), reads, writes, acc=acc)


def tt(S, eng, out, in0, in1, op, reads, writes, acc=False):
    S.op(eng, lambda e: e.tensor_tensor(out=out, in0=in0, in1=in1, op=op), reads, writes, acc=acc)


def stt(S, eng, out, in0, scalar, in1, op0, op1, reads, writes, acc=False):
    S.op(eng, lambda e: e.scalar_tensor_tensor(out=out, in0=in0, scalar=scalar, in1=in1, op0=op0, op1=op1),
         reads, writes, acc=acc)


def ts(S, eng, out, in0, s1, s2, op0, op1, reads, writes, acc=False):
    if s2 is None:
        S.op(eng, lambda e: e.tensor_scalar(out=out, in0=in0, scalar1=s1, scalar2=None, op0=op0), reads, writes, acc=acc)
    else:
        S.op(eng, lambda e: e.tensor_scalar(out=out, in0=in0, scalar1=s1, scalar2=s2, op0=op0, op1=op1),
             reads, writes, acc=acc)


def cp(S, eng, out, in_, reads, writes, acc=False):
    if eng == "scalar":
        S.op(eng, lambda e: e.copy(out=out, in_=in_), reads, writes, acc=acc)
    else:
        S.op(eng, lambda e: e.tensor_copy(out=out, in_=in_), reads, writes, acc=acc)


class Ring:
    def __init__(self, S, name, n, shape, psum=False):
        self.bufs = [(S.ps if psum else S.sb)("%s%d" % (name, i), shape) for i in range(n)]
        self.res = [Res("%s%d" % (name, i)) for i in range(n)]
        self.i = 0

    def next(self):
        b, r = self.bufs[self.i], self.res[self.i]
        self.i = (self.i + 1) % len(self.bufs)
        return b, r


class WStream:
    def __init__(self, S, nbuf=5):
        self.S = S
        self.bufs = [S.sb("wb%d" % i, [128, 16, 128]) for i in range(nbuf)]
        self.res = [Res("wb%d" % i) for i in range(nbuf)]
        self.q = []
        self.issued = 0
        self.consumed = 0

    def push(self, ap, kc):
        self.q.append((ap, kc))

    def get(self):
        nb = len(self.bufs)
        while self.issued < len(self.q) and self.issued < self.consumed + nb - 1:
            ap, kc = self.q[self.issued]
            b = self.issued % nb
            self.S.dma("sync", self.bufs[b][:, 0:kc, :], ap, writes=[self.res[b]])
            self.issued += 1
        b = self.consumed % nb
        kc = self.q[self.consumed][1]
        self.consumed += 1
        return self.bufs[b], self.res[b], kc


class Arena:
    def __init__(self, S, name, n):
        self.t = S.sb(name, [128, n])
        self.n = n
        self.off = 0
        self.live = []
        self.prev = {}

    def reset(self):
        for r in self.live:
            for d in (r.w, r.r):
                for k, v in d.items():
                    if v > self.prev.get(k, 0):
                        self.prev[k] = v
        self.live = []
        self.off = 0

    def alloc(self, shape, name="", p0=0, share=None):
        n = int(np.prod(shape[1:]))
        off = self.off if share is None else share
        assert off + n <= self.n, (name, off, n, self.n)
        v = self.t[p0:p0 + shape[0], off:off + n]
        if len(shape) == 3:
            v = v.rearrange("p (a b) -> p a b", a=shape[1])
        elif len(shape) == 4:
            v = v.rearrange("p (a b c) -> p a b c", a=shape[1], b=shape[2])
        if share is None:
            self.off += n
        r = Res(name)
        r.r = dict(self.prev)
        self.live.append(r)
        return v, r


class ZProxy:
    def __init__(self, prog, T):
        self.segs = []
        for n in FM_SEGS:
            rows = ((_off[n][1] + 127) // 128) * 128
            t = prog.nc.dram_tensor("z_" + n, [rows, T], F32, kind="Internal").ap()
            self.segs.append((FM_ROW[n], rows, t))

    def __getitem__(self, key):
        rs, cs = key
        for base, rows, t in self.segs:
            if base <= rs.start < base + rows:
                assert rs.stop <= base + rows
                return t[rs.start - base:rs.stop - base, cs]
        raise KeyError(key)


C_ONES, C_ID, C_TRI, C_BD64, C_RM = range(5)
LN16 = float(np.log(16.0))


class Prog:
    def __init__(self, T, L, dbg=()):
        self.T, self.L = T, L
        self.NT = T // 512
        self.NQT = T // 128
        self.NC = T // 16 - 1
        self.NCH = (self.NC + 127) // 128
        self.NCP = self.NCH * 128
        nc = bass.Bass("TRN2", target_bir_lowering=False)
        self.nc = nc
        self.S = Sched(nc)
        self.dbg = set(dbg)
        self.inputs = {}
        self.dres = {}
        S = self.S
        NQT, NCH, NCP = self.NQT, self.NCH, self.NCP
        self.xT = self.din("xT", [D, T])
        self.memT = self.din("memT", [D, MEM])
        self.w_in = self.din("w_in", [L, N_FM_BLK + N_TM_BLK, 128, 16, 128])
        self.gmix = self.din("gmix", [L, 128, 16])
        self.gffn = self.din("gffn", [L, 128, 16])
        self.gmem = self.din("gmem", [L, 128, 16])
        self.w_up = self.din("w_up", [L, 3, 16, 128, 8, 128])
        self.w_out = self.din("w_out", [L, 16, 128, 16, 128])
        self.w_fg = self.din("w_fg", [L, 44, 128, 16, 128])
        self.w_fu = self.din("w_fu", [L, 44, 128, 16, 128])
        self.w_fd = self.din("w_fd", [L, 16, 128, 44, 128])
        self.w_mk = self.din("w_mk", [L, 8, 128, 16, 128])
        self.w_mv = self.din("w_mv", [L, 8, 128, 16, 128])
        self.g_cq = self.din("g_cq", [L, 128, 2])
        self.g_ck = self.din("g_ck", [L, 128, 2])
        self.m_cw = self.din("m_cw", [L, 128, 16, 4])
        self.m_b8 = self.din("m_b8", [L, 1, 8])
        self.m_gn = self.din("m_gn", [L, 128, 8])
        self.g_n4 = self.din("g_n4", [L, 128, 4])
        self.n_w1 = self.din("n_w1", [L, 2, 64, 32, 128])
        self.n_pe = self.din("n_pe", [L, 2, 64, 32])
        self.n_w2 = self.din("n_w2", [L, 2, 128, 64])
        self.c128 = self.din("c128", [5, 128, 128])
        self.c_cos = self.din("c_cos", [128, T])
        self.c_sin = self.din("c_sin", [128, T])
        self.c_cosc = self.din("c_cosc", [64, NCP])
        self.c_sinc = self.din("c_sinc", [64, NCP])
        self.c_mimp = self.din("c_mimp", [128, NCH, 128])
        self.c_g2 = self.din("c_g2", [64, T])
        self.c_sbias = self.din("c_sbias", [NQT, 128, 128])
        self.c_thr = self.din("c_thr", [128, NCH * NQT])
        self.c_qidx = self.din("c_qidx", [128, 512])
        self.c_negc = self.din("c_negc", [128, 512])
        self.c_nega = self.din("c_nega", [128, 512])
        self.zfm = ZProxy(self, T)
        self.dres["zfm"] = Res("zfm")
        self.ztm = self.dscr("ztm", [T, N_TM_BLK * 128])
        self.yT = [self.dscr("y%sT" % b, [1024, T], inp=("yin" in self.dbg)) for b in "abc"]
        self.qkc = self.dscr("qkc", [2048, T])
        self.nqr = self.dscr("nqr", [1024, T])
        self.nkr = self.dscr("nkr", [512, T])
        self.xbuf = [self.dscr("xs%d" % i, [D, T]) for i in range(2)]
        self.outT = nc.dram_tensor("outT", [D, T], F32, kind="ExternalOutput").ap()
        self.dres["outT"] = Res("outT")
        self.r_const = Res("const")
        self.cst = S.sb("cst", [128, 5, 128])
        S.dma("sync", self.cst[:], self.c128.rearrange("k p m -> p k m"), writes=[self.r_const])
        self.ones = self.cst[:, C_ONES, :]
        self.ident = self.cst[:, C_ID, :]
        self.tri = self.cst[:, C_TRI, :]
        self.bd64 = self.cst[:, C_BD64, :]
        self.rm = self.cst[:, C_RM, :]
        self.gm = S.sb("gm", [128, L, 16])
        self.gf = S.sb("gf", [128, L, 16])
        self.gme = S.sb("gme", [128, L, 16])
        self.gsm = S.sb("gsm", [128, L, 24])
        S.dma("sync", self.gm[:], self.gmix.rearrange("l p c -> p l c"), writes=[self.r_const], acc=True)
        S.dma("sync", self.gf[:], self.gffn.rearrange("l p c -> p l c"), writes=[self.r_const], acc=True)
        S.dma("sync", self.gme[:], self.gmem.rearrange("l p c -> p l c"), writes=[self.r_const], acc=True)
        S.dma("sync", self.gsm[:, :, 0:2], self.g_cq.rearrange("l p c -> p l c"), writes=[self.r_const], acc=True)
        S.dma("sync", self.gsm[:, :, 2:4], self.g_ck.rearrange("l p c -> p l c"), writes=[self.r_const], acc=True)
        S.dma("sync", self.gsm[:, :, 4:12], self.m_gn.rearrange("l p c -> p l c"), writes=[self.r_const], acc=True)
        S.dma("sync", self.gsm[:, :, 12:16], self.g_n4.rearrange("l p c -> p l c"), writes=[self.r_const], acc=True)
        for l in range(L):
            S.dma("sync", self.gsm[:, l, 16:24], self.m_b8[l].partition_broadcast(128), writes=[self.r_const], acc=True)
        self.cw = S.sb("cw", [128, L, 16, 4])
        S.dma("sync", self.cw[:], self.m_cw.rearrange("l p c j -> p l c j"), writes=[self.r_const], acc=True)
        self.ws = WStream(S, nbuf=4)
        self.psr = Ring(S, "pp", 4, [128, 512], psum=True)
        self.pa = [S.ps("pa%d" % i, [128, 512]) for i in range(4)]
        self.r_pa = [Res("pa%d" % i) for i in range(4)]
        self.obr = Ring(S, "ob", 4, [128, 512])
        self.sgr = Ring(S, "sg", 2, [128, 512])
        self.tmr = Ring(S, "tm", 2, [128, 512])
        self.AR0 = Arena(S, "ar0", 16 * 512)
        self.AR1 = Arena(S, "ar1", 16 * 512)
        self.AR2 = Arena(S, "ar2", 22 * 512)
        self.rs = S.sb("rs", [128, 512])
        self.r_rs = Res("rs")

    def din(self, name, shape):
        t = self.nc.dram_tensor(name, list(shape), F32, kind="ExternalInput").ap()
        self.inputs[name] = tuple(shape)
        return t

    def dscr(self, name, shape, inp=False):
        kind = "Internal"
        if inp:
            kind = "ExternalInput"
            self.inputs[name] = tuple(shape)
        elif name in self.dbg:
            kind = "ExternalOutput"
        t = self.nc.dram_tensor(name, list(shape), F32, kind=kind).ap()
        self.dres[name] = Res(name)
        return t

    def dense_bufs(self):
        for a in (self.AR0, self.AR1, self.AR2):
            a.reset()
        self.xt, self.r_xt = self.AR0.alloc([128, 16, 512], "xt")
        self.hb, self.r_hb = self.AR1.alloc([128, 16, 512], "hb")
        self.actb, self.r_actb = self.AR2.alloc([128, 22, 512], "actb")

    def rstd(self, out, r_out, ps, r_ps, n):
        S = self.S
        act(S, out, ps, AF.Ln, [r_ps], [r_out], bias=EPS, scale=1.0 / n)
        act(S, out, out, AF.Exp, [r_out], [r_out], scale=-0.5)

    def rmsnorm_tile(self, gsb, l):
        S = self.S
        xt, hb, rs = self.xt, self.hb, self.rs
        act(S, hb[:], xt[:], AF.Square, [self.r_xt], [self.r_hb])
        ps, rps = self.psr.next()
        for c in range(16):
            mm(S, ps[:], self.ones, hb[:, c, :], c == 0, c == 15, [self.r_const, self.r_hb], [rps])
        self.rstd(rs[:], self.r_rs, ps[:], rps, D)
        for c in range(16):
            if c % 3 != 2:
                stt(S, "vector", hb[:, c, :], xt[:, c, :], gsb[:, l, c:c + 1], rs[:], ALU.mult, ALU.mult,
                    [self.r_xt, self.r_rs, self.r_const], [self.r_hb], acc=(c > 0))
            else:
                ts(S, "gpsimd", hb[:, c, :], xt[:, c, :], gsb[:, l, c:c + 1], None, ALU.mult, None,
                   [self.r_xt, self.r_const], [self.r_hb], acc=True)
                tt(S, "gpsimd", hb[:, c, :], hb[:, c, :], rs[:], ALU.mult, [self.r_rs, self.r_hb], [self.r_hb], acc=True)

    def phase_A(self, l, xsrc, r_xsrc):
        S = self.S
        self.dense_bufs()
        nblk = N_FM_BLK + N_TM_BLK
        for tt_ in range(self.NT):
            t0 = tt_ * 512
            S.dma("sync", self.xt[:], xsrc[:, t0:t0 + 512].rearrange("(c p) t -> p c t", p=128),
                  reads=[r_xsrc], writes=[self.r_xt])
            for b in range(nblk):
                self.ws.push(self.w_in[l, b], 16)
            self.rmsnorm_tile(self.gm, l)
            hb = self.hb
            for b in range(nblk):
                wb, rwb, kc = self.ws.get()
                ps, rps = self.psr.next()
                ob, rob = self.obr.next()
                if b < N_FM_BLK:
                    for c in range(16):
                        mm(S, ps[:], wb[:, c, :], hb[:, c, :], c == 0, c == 15, [rwb, self.r_hb], [rps])
                    if b in FM_SIG_BLK:
                        act(S, ob[:], ps[:], AF.Sigmoid, [rps], [rob])
                    elif b % 2 == 0:
                        cp(S, "vector", ob[:], ps[:], [rps], [rob])
                    else:
                        cp(S, "scalar", ob[:], ps[:], [rps], [rob])
                    S.dma("sync", self.zfm[b * 128:(b + 1) * 128, t0:t0 + 512], ob[:], reads=[rob],
                          writes=[self.dres["zfm"]], acc=True)
                else:
                    bt = b - N_FM_BLK
                    for j in range(4):
                        for c in range(16):
                            mm(S, ps[:, j * 128:(j + 1) * 128], hb[:, c, j * 128:(j + 1) * 128], wb[:, c, :],
                               c == 0, c == 15, [rwb, self.r_hb], [rps], inc=(c == 15 and j == 3))
                    cp(S, "vector", ob[:], ps[:], [rps], [rob])
                    S.dma("sync", self.ztm[t0:t0 + 512, bt * 128:(bt + 1) * 128].rearrange("(j p) c -> p j c", p=128),
                          ob[:].rearrange("p (j c) -> p j c", j=4), reads=[rob], writes=[self.dres["ztm"]], acc=True)

    def phase_E(self, l, xsrc, r_xsrc, xdst, r_xdst):
        S = self.S
        self.dense_bufs()
        xt, hb = self.xt, self.hb
        actb = self.actb
        yb = actb
        for tt_ in range(self.NT):
            t0 = tt_ * 512
            S.dma("sync", xt[:], xsrc[:, t0:t0 + 512].rearrange("(c p) t -> p c t", p=128),
                  reads=[r_xsrc], writes=[self.r_xt])
            for br in range(3):
                for mb in range(16):
                    self.ws.push(self.w_up[l, br, mb], 8)
            for mb in range(16):
                self.ws.push(self.w_out[l, mb], 16)
            gname = ("g_a", "g_b", "g_c")
            for br in range(3):
                S.dma("sync", yb[:, 0:8, :], self.yT[br][:, t0:t0 + 512].rearrange("(c p) t -> p c t", p=128),
                      reads=[self.dres["y%sT" % "abc"[br]]], writes=[self.r_actb])
                for mb in range(16):
                    wb, rwb, kc = self.ws.get()
                    ps, rps = self.psr.next()
                    sg, rsg = self.sgr.next()
                    r0 = FM_ROW[gname[br]] + mb * 128
                    S.dma("sync", sg[:], self.zfm[r0:r0 + 128, t0:t0 + 512], reads=[self.dres["zfm"]], writes=[rsg])
                    for c in range(8):
                        mm(S, ps[:], wb[:, c, :], yb[:, c, :], c == 0, c == 7, [rwb, self.r_actb], [rps])
                    if br == 0:
                        tt(S, "vector", hb[:, mb, :], ps[:], sg[:], ALU.mult, [rps, rsg], [self.r_hb], acc=(mb > 0))
                    else:
                        tm, rtm = self.tmr.next()
                        tt(S, "vector", tm[:], ps[:], sg[:], ALU.mult, [rps, rsg], [rtm])
                        tt(S, "gpsimd", hb[:, mb, :], hb[:, mb, :], tm[:], ALU.add, [rtm, self.r_hb], [self.r_hb], acc=True)
            for mb in range(16):
                wb, rwb, kc = self.ws.get()
                ps, rps = self.psr.next()
                for c in range(16):
                    mm(S, ps[:], wb[:, c, :], hb[:, c, :], c == 0, c == 15, [rwb, self.r_hb], [rps])
                tt(S, "vector", xt[:, mb, :], xt[:, mb, :], ps[:], ALU.add, [rps, self.r_xt], [self.r_xt], acc=True)
            for half in range(2):
                for fb in range(22):
                    self.ws.push(self.w_fg[l, half * 22 + fb], 16)
                    self.ws.push(self.w_fu[l, half * 22 + fb], 16)
                for mb in range(16):
                    self.ws.push(self.w_fd[l, mb, :, half * 22:half * 22 + 16, :], 16)
                    self.ws.push(self.w_fd[l, mb, :, half * 22 + 16:half * 22 + 22, :], 6)
            self.rmsnorm_tile(self.gf, l)
            for half in range(2):
                for fb in range(22):
                    wg, rwg, _ = self.ws.get()
                    psg, rpsg = self.psr.next()
                    for c in range(16):
                        mm(S, psg[:], wg[:, c, :], hb[:, c, :], c == 0, c == 15, [rwg, self.r_hb], [rpsg])
                    wu, rwu, _ = self.ws.get()
                    psu, rpsu = self.psr.next()
                    for c in range(16):
                        mm(S, psu[:], wu[:, c, :], hb[:, c, :], c == 0, c == 15, [rwu, self.r_hb], [rpsu])
                    tm, rtm = self.tmr.next()
                    act(S, tm[:], psg[:], AF.Silu, [rpsg], [rtm])
                    tt(S, "vector", actb[:, fb, :], tm[:], psu[:], ALU.mult, [rtm, rpsu], [self.r_actb], acc=(fb > 0))
                for mb in range(16):
                    ps, rps = self.psr.next()
                    w1, rw1, _ = self.ws.get()
                    for c in range(16):
                        mm(S, ps[:], w1[:, c, :], actb[:, c, :], c == 0, False, [rw1, self.r_actb], [rps], inc=False)
                    w2, rw2, _ = self.ws.get()
                    for c in range(6):
                        mm(S, ps[:], w2[:, c, :], actb[:, 16 + c, :], False, c == 5, [rw2, self.r_actb], [rps])
                    tt(S, "vector", xt[:, mb, :], xt[:, mb, :], ps[:], ALU.add, [rps, self.r_xt], [self.r_xt], acc=True)
            S.dma("sync", xdst[:, t0:t0 + 512].rearrange("(c p) t -> p c t", p=128), xt[:],
                  reads=[self.r_xt], writes=[r_xdst], acc=True)

    def phase_C(self, l):
        S = self.S
        for a in (self.AR1, self.AR2):
            a.reset()
        A1, A2 = self.AR1, self.AR2
        rc = self.r_const
        sqm, r_sqm = A1.alloc([128, 16, 256], "sqm")
        memn, r_memn = A2.alloc([128, 16, 256], "memn")
        kT, r_kT = A2.alloc([128, 8, 256], "kT")
        vv, r_vv = A2.alloc([128, 2, 1024], "vv")
        qb, r_qb = A2.alloc([128, 2, 512], "qb")
        sq, r_sq = A2.alloc([128, 2, 512], "sq")
        pt, r_pt = A2.alloc([128, 2, 512], "pt")
        rs, r_rs = self.rs, self.r_rs
        S.dma("sync", memn[:], self.memT.rearrange("(c p) m -> p c m", p=128), writes=[r_memn])
        for mb in range(8):
            self.ws.push(self.w_mk[l, mb], 16)
        for mb in range(8):
            self.ws.push(self.w_mv[l, mb], 16)
        act(S, sqm[:], memn[:], AF.Square, [r_memn], [r_sqm])
        ps, rps = self.psr.next()
        for c in range(16):
            mm(S, ps[:, 0:256], self.ones, sqm[:, c, :], c == 0, c == 15, [rc, r_sqm], [rps])
        self.rstd(rs[:, 0:256], r_rs, ps[:, 0:256], rps, D)
        for c in range(16):
            stt(S, "vector", memn[:, c, :], memn[:, c, :], self.gme[:, l, c:c + 1], rs[:, 0:256], ALU.mult, ALU.mult,
                [r_memn, r_rs, rc], [r_memn], acc=True)
        for mb in range(8):
            wb, rwb, _ = self.ws.get()
            ps, rps = self.psr.next()
            for c in range(16):
                mm(S, ps[:, 0:256], wb[:, c, :], memn[:, c, :], c == 0, c == 15, [rwb, r_memn], [rps])
            cp(S, "scalar", kT[:, mb, :], ps[:, 0:256], [rps], [r_kT], acc=(mb > 0))
        for hh in range(4):
            act(S, sq[:, :, 0:256], kT[:, 2 * hh:2 * hh + 2, :], AF.Square, [r_kT], [r_sq])
            ps, rps = self.psr.next()
            for j in range(2):
                mm(S, ps[:, 0:256], self.ones, sq[:, j, 0:256], j == 0, j == 1, [rc, r_sq], [rps])
            self.rstd(rs[:, 0:256], r_rs, ps[:, 0:256], rps, 256)
            for j in range(2):
                stt(S, "vector", kT[:, 2 * hh + j, :], kT[:, 2 * hh + j, :], self.gsm[:, l, 2 + j:3 + j], rs[:, 0:256],
                    ALU.mult, ALU.mult, [r_kT, r_rs, rc], [r_kT], acc=True)
        for mb in range(8):
            wb, rwb, _ = self.ws.get()
            ps, rps = self.psr.next()
            for mt in range(2):
                for c in range(16):
                    mm(S, ps[:, mt * 128:(mt + 1) * 128], memn[:, c, mt * 128:(mt + 1) * 128], wb[:, c, :],
                       c == 0, c == 15, [rwb, r_memn], [rps], inc=(c == 15 and mt == 1))
            cp(S, "vector", vv[:, :, mb * 128:(mb + 1) * 128], ps[:, 0:256].rearrange("p (m c) -> p m c", m=2),
               [rps], [r_vv], acc=(mb > 0))
        r0 = FM_ROW["c_q"]
        for tt_ in range(self.NT):
            t0 = tt_ * 512
            for hh in range(4):
                S.dma("sync", qb[:], self.zfm[r0 + hh * 256:r0 + (hh + 1) * 256, t0:t0 + 512].rearrange("(j p) t -> p j t", p=128),
                      reads=[self.dres["zfm"]], writes=[r_qb])
                act(S, sq[:], qb[:], AF.Square, [r_qb], [r_sq])
                ps, rps = self.psr.next()
                for j in range(2):
                    mm(S, ps[:], self.ones, sq[:, j, :], j == 0, j == 1, [rc, r_sq], [rps])
                self.rstd(rs[:], r_rs, ps[:], rps, 256)
                for j in range(2):
                    stt(S, "vector", qb[:, j, :], qb[:, j, :], self.gsm[:, l, j:j + 1], rs[:], ALU.mult, ALU.mult,
                        [r_qb, r_rs, rc], [r_qb], acc=True)
                for mt in range(2):
                    ps, rps = self.psr.next()
                    for j in range(2):
                        mm(S, ps[:], kT[:, 2 * hh + j, mt * 128:(mt + 1) * 128], qb[:, j, :], j == 0, j == 1, [r_kT, r_qb], [rps])
                    act(S, pt[:, mt, :], ps[:], AF.Exp, [rps], [r_pt], scale=1.0 / 16.0, acc=(mt > 0))
                psd, rpsd = self.psr.next()
                for mt in range(2):
                    mm(S, psd[:], self.ones, pt[:, mt, :], mt == 0, mt == 1, [rc, r_pt], [rpsd])
                tm, rtm = self.tmr.next()
                S.op("vector", lambda e, o=tm[:], i=psd[:]: e.reciprocal(out=o, in_=i), [rpsd], [rtm])
                for j in range(2):
                    ps, rps = self.psr.next()
                    for mt in range(2):
                        mm(S, ps[:], vv[:, mt, hh * 256 + j * 128:hh * 256 + (j + 1) * 128], pt[:, mt, :], mt == 0, mt == 1,
                           [r_vv, r_pt], [rps])
                    ob, rob = self.obr.next()
                    tt(S, "vector", ob[:], ps[:], tm[:], ALU.mult, [rps, rtm], [rob])
                    S.dma("sync", self.yT[2][hh * 256 + j * 128:hh * 256 + (j + 1) * 128, t0:t0 + 512], ob[:],
                          reads=[rob], writes=[self.dres["ycT"]], acc=True)

    def phase_M(self, l):
        S = self.S
        T = self.T
        rc = self.r_const
        for a in (self.AR0, self.AR1, self.AR2):
            a.reset()
        A0, A1, A2 = self.AR0, self.AR1, self.AR2
        ub = [A0.alloc([128, 515], "u%d" % i) for i in range(3)]
        ab = [A0.alloc([128, 512], "a%d" % i) for i in range(3)]
        k = 0
        for tt_ in range(self.NT):
            t0 = tt_ * 512
            for c in range(16):
                (u, ru), (a, ra) = ub[k % 3], ab[k % 3]
                k += 1
                if tt_ == 0:
                    S.op("gpsimd", lambda e, o=u[:, 0:3]: e.memset(o, 0.0), [], [ru])
                    S.dma("sync", u[:, 3:515], self.zfm[c * 128:(c + 1) * 128, 0:512], reads=[self.dres["zfm"]], writes=[ru], acc=True)
                else:
                    S.dma("sync", u[:], self.zfm[c * 128:(c + 1) * 128, t0 - 3:t0 + 512], reads=[self.dres["zfm"]], writes=[ru])
                cwl = self.cw[:, l, c, :]
                ts(S, "vector", a[:], u[:, 0:512], cwl[:, 0:1], None, ALU.mult, None, [ru, rc], [ra])
                for j in range(1, 4):
                    stt(S, "vector", a[:], u[:, j:j + 512], cwl[:, j:j + 1], a[:], ALU.mult, ALU.add, [ru, ra, rc], [ra])
                act(S, a[:], a[:], AF.Silu, [ra], [ra])
                S.dma("sync", self.qkc[c * 128:(c + 1) * 128, t0:t0 + 512], a[:], reads=[ra], writes=[self.dres["qkc"]], acc=True)
        for a in (A0, A1, A2):
            a.reset()
        qTb = [A0.alloc([128, 8, 128], "qT%d" % i) for i in range(2)]
        kTb = [A0.alloc([128, 8, 128], "kT%d" % i) for i in range(2)]
        sob = [A1.alloc([128, 8, 128], "so%d" % i) for i in range(2)]
        yab = [A1.alloc([128, 8, 128], "ya%d" % i) for i in range(2)]
        grb = [A1.alloc([128, 8], "gr%d" % i) for i in range(2)]
        Cst, r_C = A2.alloc([128, 4, 2, 384], "Cst")
        vxb = [A2.alloc([128, 4, 384], "vx%d" % i) for i in range(2)]
        tmp = []
        for i in range(2):
            d = {}
            for nm, shp in (("nb", [128, 128]), ("ebt", [128, 128]), ("Dt", [128, 128]), ("W0", [128, 128]), ("Wt", [128, 128]),
                            ("qtl", [128, 2, 128]), ("dm", [128, 128]), ("rd", [128, 128]), ("hh", [128, 2, 128]),
                            ("sqh", [128, 2, 128]), ("rr", [128, 128]), ("kt", [128, 256]), ("yt", [128, 128])):
                d[nm] = A2.alloc(shp, nm + str(i))
            tmp.append(d)
        gts = []
        for i in range(2):
            d = {}
            for nm, w in (("gz", 8), ("e1", 4), ("nlf", 4), ("imb", 4), ("bD", 4), ("gl", 4), ("gcol", 4), ("dcol", 4)):
                d[nm] = A1.alloc([128, w], nm + str(i))
            gts.append(d)
        S.op("vector", lambda e: e.memset(Cst[:], 0.0), [], [r_C])
        for i in range(2):
            S.op("gpsimd", lambda e, o=vxb[i][0][:, :, 256:384]: e.memset(o, 1.0), [], [vxb[i][1]])
        r_mo = FM_ROW["m_o"]
        hk = 0
        for ch in range(T // 128):
            t0 = ch * 128
            (qT, rq), (kT, rk), (so, rso), (ya, rya), (gr, rgr) = qTb[ch % 2], kTb[ch % 2], sob[ch % 2], yab[ch % 2], grb[ch % 2]
            vx, rvx = vxb[ch % 2]
            g = gts[ch % 2]
            S.dma("sync", qT[:], self.qkc[0:1024, t0:t0 + 128].rearrange("(c p) t -> p c t", p=128), reads=[self.dres["qkc"]], writes=[rq])
            S.dma("sync", kT[:], self.qkc[1024:2048, t0:t0 + 128].rearrange("(c p) t -> p c t", p=128), reads=[self.dres["qkc"]], writes=[rk])
            S.dma("sync", so[:], self.zfm[r_mo:r_mo + 1024, t0:t0 + 128].rearrange("(c p) t -> p c t", p=128), reads=[self.dres["zfm"]], writes=[rso])
            S.dma("sync", vx[:, :, 0:256], self.ztm[t0:t0 + 128, 0:1024].rearrange("p (h d) -> p h d", h=4),
                  reads=[self.dres["ztm"]], writes=[rvx], acc=True)
            S.dma("sync", gr[:], self.ztm[t0:t0 + 128, TM_COL["m_i"]:TM_COL["m_i"] + 8], reads=[self.dres["ztm"]], writes=[rgr])
            gz, rgz = g["gz"]; e1, re1 = g["e1"]; nlf, rnlf = g["nlf"]; imb, rimb = g["imb"]; bD, rbD = g["bD"]
            gl, rgl = g["gl"]; gcol, rgcol = g["gcol"]; dcol, rdcol = g["dcol"]
            tt(S, "vector", gz[:], gr[:], self.gsm[:, l, 16:24], ALU.add, [rgr, rc], [rgz])
            act(S, e1[:], gz[:, 4:8], AF.Exp, [rgz], [re1], scale=-1.0)
            act(S, nlf[:], e1[:], AF.Ln, [re1], [rnlf], bias=1.0)
            psg, rpsg = self.psr.next()
            mm(S, psg[:, 0:4], self.tri, nlf[:], True, True, [rc, rnlf], [rpsg])
            mm(S, psg[:, 4:8], self.ones, nlf[:], True, True, [rc, rnlf], [rpsg])
            tt(S, "vector", imb[:], gz[:, 0:4], psg[:, 0:4], ALU.add, [rgz, rpsg], [rimb])
            ts(S, "vector", bD[:], imb[:], -LN16, None, ALU.add, None, [rimb], [rbD])
            tt(S, "vector", gl[:], imb[:], psg[:, 4:8], ALU.subtract, [rimb, rpsg], [rgl])
            act(S, gcol[:], gl[:], AF.Exp, [rgl], [rgcol])
            act(S, dcol[:], psg[:, 4:8], AF.Exp, [rpsg], [rdcol], scale=-1.0)
            for h in range(4):
                tp = tmp[hk % 2]
                hk += 1
                nb, rnb = tp["nb"]; ebt, rebt = tp["ebt"]; Dt, rDt = tp["Dt"]; W0, rW0 = tp["W0"]; Wt, rWt = tp["Wt"]
                qtl, rqtl = tp["qtl"]; dm, rdm = tp["dm"]; rd, rrd = tp["rd"]; hh_, rhh = tp["hh"]; sqh, rsqh = tp["sqh"]
                rr, rrr = tp["rr"]; kt, rkt = tp["kt"]; yt, ryt = tp["yt"]
                cp(S, "vector", nb[:], nlf[:, h:h + 1].to_broadcast([128, 128]), [rnlf], [rnb])
                psA, rpsA = self.psr.next()
                mm(S, psA[:, 0:128], nb[:], self.tri, True, True, [rnb, rc], [rpsA])
                act(S, ebt[:], psA[:, 0:128], AF.Exp, [rpsA], [rebt], scale=-1.0, bias=-LN16)
                act(S, Dt[:], psA[:, 0:128], AF.Exp, [rpsA, rbD], [rDt], scale=-1.0, bias=bD[:, h:h + 1])
                for j in range(2):
                    mm(S, psA[:, 128:256], kT[:, 2 * h + j, :], qT[:, 2 * h + j, :], j == 0, j == 1, [rk, rq], [rpsA])
                tt(S, "gpsimd", W0[:], Dt[:], self.tri, ALU.mult, [rDt, rc], [rW0])
                tt(S, "vector", Wt[:], W0[:], psA[:, 128:256], ALU.mult, [rW0, rpsA], [rWt])
                for j in range(2):
                    tt(S, "gpsimd", qtl[:, j, :], qT[:, 2 * h + j, :], ebt[:], ALU.mult, [rq, rebt], [rqtl], acc=(j > 0))
                psN, rpsN = self.psr.next()
                for jv in range(3):
                    o = psN[:, jv * 128:(jv + 1) * 128]
                    mm(S, o, vx[:, h, jv * 128:(jv + 1) * 128], Wt[:], True, False, [rvx, rWt], [rpsN], inc=False)
                    for j in range(2):
                        mm(S, o, Cst[:, h, j, jv * 128:(jv + 1) * 128], qtl[:, j, :], False, j == 1, [r_C, rqtl], [rpsN],
                           inc=(j == 1 and jv == 2))
                act(S, dm[:], psN[:, 256:384], AF.Abs, [rpsN], [rdm])
                ts(S, "vector", dm[:], dm[:], 1.0, None, ALU.max, None, [rdm], [rdm])
                S.op("vector", lambda e, o=rd[:], i=dm[:]: e.reciprocal(out=o, in_=i), [rdm], [rrd])
                for jv in range(2):
                    tt(S, "vector", hh_[:, jv, :], psN[:, jv * 128:(jv + 1) * 128], rd[:], ALU.mult, [rpsN, rrd], [rhh], acc=(jv > 0))
                act(S, sqh[:], hh_[:], AF.Square, [rhh], [rsqh])
                psR, rpsR = self.psr.next()
                for jv in range(2):
                    mm(S, psR[:, 0:128], self.ones, sqh[:, jv, :], jv == 0, jv == 1, [rc, rsqh], [rpsR])
                self.rstd(rr[:], rrr, psR[:, 0:128], rpsR, 256)
                for jv in range(2):
                    stt(S, "vector", yt[:], hh_[:, jv, :], self.gsm[:, l, 4 + 2 * h + jv:5 + 2 * h + jv], rr[:], ALU.mult, ALU.mult,
                        [rhh, rrr, rc], [ryt])
                    tt(S, "gpsimd", ya[:, 2 * h + jv, :], yt[:], so[:, 2 * h + jv, :], ALU.mult, [ryt, rso], [rya],
                       acc=(h > 0 or jv > 0))
                for j in range(2):
                    S.op("tensor", lambda e, o=psR[:, 128 + j * 128:256 + j * 128], i=kT[:, 2 * h + j, :], idn=self.ident:
                         e.transpose(out=o, in_=i, identity=idn), [rk, rc], [rpsR], acc=True)
                ts(S, "vector", kt[:], psR[:, 128:384], gcol[:, h:h + 1], None, ALU.mult, None, [rpsR, rgcol], [rkt])
                for j in range(2):
                    psU, rpsU = self.psr.next()
                    mm(S, psU[:, 0:384], kt[:, j * 128:(j + 1) * 128], vx[:, h, :], True, True, [rkt, rvx], [rpsU])
                    stt(S, "vector", Cst[:, h, j, :], Cst[:, h, j, :], dcol[:, h:h + 1], psU[:, 0:384], ALU.mult, ALU.add,
                        [r_C, rdcol, rpsU], [r_C], acc=True)
            S.dma("sync", self.yT[0][:, t0:t0 + 128].rearrange("(c p) t -> p c t", p=128), ya[:], reads=[rya],
                  writes=[self.dres["yaT"]], acc=True)

    def phase_N(self, l):
        S = self.S
        T, NQT, NCH, NCP, NC = self.T, self.NQT, self.NCH, self.NCP, self.NC
        rc = self.r_const
        A0, A1, A2 = self.AR0, self.AR1, self.AR2
        if not hasattr(self, "AR3"):
            self.AR3 = Arena(S, "ar3", 7168)
        A3 = self.AR3
        for a in (A0, A1, A2, A3):
            a.reset()
        zres = self.dres["zfm"]
        cosb = [A0.alloc([128, 512], "cos%d" % i) for i in range(2)]
        sinb = [A0.alloc([128, 512], "sin%d" % i) for i in range(2)]
        xin = [A0.alloc([128, 512], "xin%d" % i) for i in range(3)]
        sqb = [A0.alloc([128, 512], "sqb%d" % i) for i in range(2)]
        xnb = [A0.alloc([128, 512], "xnb%d" % i) for i in range(2)]
        t1b = [A0.alloc([128, 512], "t1b%d" % i) for i in range(2)]
        rrb = [A0.alloc([128, 512], "rrb%d" % i) for i in range(2)]
        items = [(FM_ROW["n_q"] + c * 128, 12, self.nqr[c * 128:(c + 1) * 128], "nqr") for c in range(8)]
        items += [(FM_ROW["n_ks"] + c * 128, 14, self.nkr[c * 128:(c + 1) * 128], "nkr") for c in range(2)]
        items += [(FM_ROW["n_kw"] + c * 128, 15, self.nkr[256 + c * 128:256 + (c + 1) * 128], "nkr") for c in range(2)]
        k = 0
        for tt_ in range(self.NT):
            t0 = tt_ * 512
            (cs, rcs), (sn, rsn) = cosb[tt_ % 2], sinb[tt_ % 2]
            S.dma("sync", cs[:], self.c_cos[:, t0:t0 + 512], writes=[rcs])
            S.dma("sync", sn[:], self.c_sin[:, t0:t0 + 512], writes=[rsn])
            for (row, gi, dst, dname) in items:
                (x, rx) = xin[k % 3]
                (sq, rsq), (xn, rxn), (t1, rt1), (rr, rrr) = sqb[k % 2], xnb[k % 2], t1b[k % 2], rrb[k % 2]
                k += 1
                S.dma("sync", x[:], self.zfm[row:row + 128, t0:t0 + 512], reads=[zres], writes=[rx])
                act(S, sq[:], x[:], AF.Square, [rx], [rsq])
                ps, rps = self.psr.next()
                mm(S, ps[:], self.bd64, sq[:], True, True, [rc, rsq], [rps])
                self.rstd(rr[:], rrr, ps[:], rps, 64)
                stt(S, "vector", xn[:], x[:], self.gsm[:, l, gi:gi + 1], rr[:], ALU.mult, ALU.mult, [rx, rrr, rc], [rxn])
                ps2, rps2 = self.psr.next()
                mm(S, ps2[:], self.rm, xn[:], True, True, [rc, rxn], [rps2])
                tt(S, "gpsimd", t1[:], xn[:], cs[:], ALU.mult, [rxn, rcs], [rt1])
                ob, rob = self.obr.next()
                tt(S, "vector", ob[:], ps2[:], sn[:], ALU.mult, [rps2, rsn], [rob])
                tt(S, "gpsimd", ob[:], ob[:], t1[:], ALU.add, [rob, rt1], [rob])
                S.dma("sync", dst[:, t0:t0 + 512], ob[:], reads=[rob], writes=[self.dres[dname]], acc=True)
        for a in (A0, A1):
            a.reset()
        kcT, r_kcT = A2.alloc([64, 4, NCP], "kcT")
        vcx, r_vcx = A2.alloc([128, 4, NCH, 128], "vcx")
        S.op("gpsimd", lambda e: e.memset(vcx[:], 1.0), [], [r_vcx])
        w1, r_w1 = A0.alloc([64, 2, 32, 128], "w1")
        u, r_u = A1.alloc([64, T], "u")
        pe, r_pe = A3.alloc([64, 2, 32], "pe")
        w2, r_w2 = A3.alloc([128, 2, 64], "w2")
        bcol, r_bcol = A3.alloc([128, 2], "bcol")
        cosc, r_cosc = A3.alloc([64, NCP], "cosc")
        sinc, r_sinc = A3.alloc([64, NCP], "sinc")
        sqc, r_sqc = A3.alloc([64, NCP], "sqc")
        xnc, r_xnc = A3.alloc([64, NCP], "xnc")
        t1c, r_t1c = A3.alloc([64, NCP], "t1c")
        rrc, r_rrc = A3.alloc([64, NCP], "rrc")
        hs, r_hs = A3.alloc([128, NCP], "hs")
        S.dma("sync", w1[:], self.n_w1[l].rearrange("k d j m -> d k j m"), writes=[r_w1])
        S.dma("sync", pe[:], self.n_pe[l].rearrange("k d j -> d k j"), writes=[r_pe])
        S.dma("sync", w2[:], self.n_w2[l].rearrange("k p m -> p k m"), writes=[r_w2])
        S.dma("sync", cosc[:], self.c_cosc, writes=[r_cosc])
        S.dma("sync", sinc[:], self.c_sinc, writes=[r_sinc])
        S.op("vector", lambda e: e.memset(kcT[:], 0.0), [], [r_kcT])
        S.op("gpsimd", lambda e: e.memset(hs[:], 0.0), [], [r_hs])
        for kind in range(2):
            ps, rps = self.psr.next()
            for j in range(32):
                mm(S, ps[:, 0:1], w1[:, kind, j, :], pe[:, kind, j:j + 1], j == 0, j == 31, [r_w1, r_pe], [rps])
            cp(S, "vector", bcol[:, kind:kind + 1], ps[:, 0:1], [rps], [r_bcol], acc=(kind > 0))
        u3 = u.rearrange("p (n s) -> p n s", s=16)
        for g in range(4):
            for kind in range(2):
                row = FM_ROW["n_kc" if kind == 0 else "n_vc"] + g * 64
                S.dma("sync", u[:], self.zfm[row:row + 64, :], reads=[zres], writes=[r_u])
                ps, rps = self.psr.next()
                for j in range(32):
                    a, jj = (0, j) if j < 16 else (1, j - 16)
                    mm(S, ps[:, 0:NC], w1[:, kind, j, :], u3[:, a:a + NC, jj], j == 0, j == 31, [r_w1, r_u], [rps])
                act(S, hs[:, 0:NC], ps[:, 0:NC], AF.Silu, [rps, r_bcol], [r_hs], bias=bcol[:, kind:kind + 1], acc=True)
                if kind == 0:
                    ps2, rps2 = self.psr.next()
                    mm(S, ps2[0:64, 0:NC], w2[:, 0, :], hs[:, 0:NC], True, True, [r_w2, r_hs], [rps2])
                    act(S, sqc[:, 0:NC], ps2[0:64, 0:NC], AF.Square, [rps2], [r_sqc])
                    ps3, rps3 = self.psr.next()
                    mm(S, ps3[0:64, 0:NC], self.ones[0:64, 0:64], sqc[:, 0:NC], True, True, [rc, r_sqc], [rps3])
                    self.rstd(rrc[:, 0:NC], r_rrc, ps3[0:64, 0:NC], rps3, 64)
                    stt(S, "vector", xnc[:, 0:NC], ps2[0:64, 0:NC], self.gsm[0:64, l, 13:14], rrc[:, 0:NC], ALU.mult, ALU.mult,
                        [rps2, r_rrc, rc], [r_xnc])
                    ps4, rps4 = self.psr.next()
                    mm(S, ps4[0:64, 0:NC], self.rm[0:64, 0:64], xnc[:, 0:NC], True, True, [rc, r_xnc], [rps4])
                    tt(S, "gpsimd", t1c[:, 0:NC], xnc[:, 0:NC], cosc[:, 0:NC], ALU.mult, [r_xnc, r_cosc], [r_t1c])
                    tt(S, "vector", kcT[:, g, 0:NC], ps4[0:64, 0:NC], sinc[:, 0:NC], ALU.mult, [rps4, r_sinc], [r_kcT], acc=True)
                    tt(S, "gpsimd", kcT[:, g, 0:NC], kcT[:, g, 0:NC], t1c[:, 0:NC], ALU.add, [r_kcT, r_t1c], [r_kcT], acc=True)
                else:
                    for ch in range(NCH):
                        ps5, rps5 = self.psr.next()
                        mm(S, ps5[:, 0:64], hs[:, ch * 128:(ch + 1) * 128], w2[:, 1, :], True, True, [r_hs, r_w2], [rps5])
                        cp(S, "scalar", vcx[:, g, ch, 0:64], ps5[:, 0:64], [rps5], [r_vcx], acc=True)
        for a in (A0, A1, A3):
            a.reset()
        NHB = (NQT + 31) // 32
        kT2, r_kT2 = A0.alloc([128, T], "kT2")
        vsx, r_vsx = A1.alloc([128, NQT, 128], "vsx")
        Et, r_Et = A2.alloc([128, NCH, 512], "Et")
        Qsb = [[A2.alloc([128, 4, 128], "Qs%d_%d" % (i, hb)) for hb in range(NHB)] for i in range(2)]
        vwx, r_vwx = A2.alloc([128, 5, 128], "vwx")
        rcpb = [A2.alloc([128, 512], "rcp%d" % i) for i in range(3)]
        off0 = A3.off
        gtb = [A3.alloc([64, 12, 128], "gt%d" % i, p0=64) for i in range(2)]
        kwb, r_kwb = A3.alloc([64, 5, 128], "kwb", p0=0, share=off0)
        negc, r_negc = A3.alloc([128, 512], "negc")
        nega, r_nega = A3.alloc([128, 512], "nega")
        qidx, r_qidx = A3.alloc([128, 512], "qidx")
        thr, r_thr = A3.alloc([128, NCH * NQT], "thr")
        mimp, r_mimp = A3.alloc([128, NCH, 128], "mimp")
        sbb = [A3.alloc([128, 128], "sb%d" % i) for i in range(2)]
        chs = []
        for i in range(2):
            d = {}
            for nm, w in (("score", 128), ("sc2", 128), ("selA", 128), ("selB", 128), ("rdT", 4), ("m8", 16)):
                d[nm] = A3.alloc([128, w], nm + str(i))
            chs.append(d)
        S.dma("sync", negc[:], self.c_negc, writes=[r_negc])
        S.dma("sync", nega[:], self.c_nega, writes=[r_nega])
        S.dma("sync", qidx[:], self.c_qidx, writes=[r_qidx])
        S.dma("sync", thr[:], self.c_thr, writes=[r_thr])
        S.dma("sync", mimp[:], self.c_mimp, writes=[r_mimp])
        S.dma("sync", kT2[64:128, :], self.c_g2, writes=[r_kT2])
        S.op("gpsimd", lambda e: e.memset(vsx[:], 1.0), [], [r_vsx])
        S.op("gpsimd", lambda e: e.memset(vwx[:], 1.0), [], [r_vwx])
        ngrow = FM_ROW["n_g"]
        pa, rpa = self.pa, self.r_pa
        idhi = self.ident[64:128, 64:128]
        iters = [(g, qt) for g in range(4) for qt in range(NQT)]
        n_it = len(iters)
        st = {}

        def group_loads(g):
            S.dma("sync", kT2[0:64, :], self.nkr[g * 64:(g + 1) * 64, :], reads=[self.dres["nkr"]], writes=[r_kT2], acc=True)
            c0 = TM_COL["n_vs"] + g * 64
            S.dma("sync", vsx[:, :, 0:64], self.ztm[:, c0:c0 + 64].rearrange("(k p) d -> p k d", p=128),
                  reads=[self.dres["ztm"]], writes=[r_vsx], acc=True)

        def load(i):
            g, qt = iters[i]
            t0 = qt * 128
            d = {"Qs": Qsb[i % 2], "sb": sbb[i % 2], "gt": gtb[i % 2], "ch": chs[i % 2], "yacc": self.tmr.next(),
                 "nhb": qt // 32 + 1}
            st[i] = d
            qsrc = self.nqr[g * 256:(g + 1) * 256, t0:t0 + 128].rearrange("(r d) q -> d r q", d=64)
            for hb in range(d["nhb"]):
                S.dma("sync", d["Qs"][hb][0][0:64], qsrc, reads=[self.dres["nqr"]], writes=[d["Qs"][hb][1]])
            gt, rgt = d["gt"]
            S.dma("sync", gt[:], self.zfm[ngrow + g * 12:ngrow + (g + 1) * 12, t0:t0 + 128].partition_broadcast(64),
                  reads=[zres], writes=[rgt])
            S.dma("sync", d["sb"][0][:], self.c_sbias[qt], writes=[d["sb"][1]])

        def finA(br, i, num, rnum):
            d = st[i]
            gt, rgt = d["gt"]
            gt4 = gt.rearrange("p (r j) q -> p r j q", j=3)
            rcp, rrcp = rcpb[br]
            hi = rcp[64:128, :]
            ts(S, "vector", hi, num[64:128, :], 1e-30, None, ALU.max, None, [rnum], [rrcp])
            S.op("vector", lambda e, o=hi: e.reciprocal(out=o, in_=o), [rrcp], [rrcp])
            tt(S, "gpsimd", hi.rearrange("p (r q) -> p r q", r=4), hi.rearrange("p (r q) -> p r q", r=4),
               gt4[:, :, br, :], ALU.mult, [rrcp, rgt], [rrcp])

        def finB(br, i, num, rnum, first):
            d = st[i]
            yacc, ryacc = d["yacc"]
            rcp, rrcp = rcpb[br]
            psf, rpsf = self.psr.next()
            mm(S, psf[0:64, :], idhi, rcp[64:128, :], True, True, [rc, rrcp], [rpsf])
            act(S, rcp[0:64, :], psf[0:64, :], AF.Copy, [rpsf], [rrcp], acc=True)
            if first:
                tt(S, "vector", yacc[0:64, :], num[0:64, :], rcp[0:64, :], ALU.mult, [rnum, rrcp], [ryacc])
            else:
                tt(S, "vector", rcp[0:64, :], num[0:64, :], rcp[0:64, :], ALU.mult, [rnum, rrcp], [rrcp])
                tt(S, "gpsimd", yacc[0:64, :], yacc[0:64, :], rcp[0:64, :], ALU.add, [ryacc, rrcp], [ryacc])

        def store(i):
            g, qt = iters[i]
            t0 = qt * 128
            yacc, ryacc = st[i]["yacc"]
            S.dma("sync", self.yT[1][g * 256:(g + 1) * 256, t0:t0 + 128].rearrange("(r d) q -> d r q", d=64),
                  yacc[0:64, :].rearrange("p (r q) -> p r q", r=4), reads=[ryacc], writes=[self.dres["ybT"]], acc=True)
            del st[i]

        def cmpA(i):
            g, qt = iters[i]
            t0 = qt * 128
            d = st[i]
            Q0, rQ0 = d["Qs"][0]
            Qlo = Q0[0:64].rearrange("p r q -> p (r q)")
            sb, rsb = d["sb"]
            c = d["ch"]
            score, r_score = c["score"]; sc2, r_sc2 = c["sc2"]; selA, r_selA = c["selA"]; selB, r_selB = c["selB"]
            rdT, r_rdT = c["rdT"]; m8, r_m8 = c["m8"]
            jmax = min((8 * qt + 6) // 128, NCH - 1)
            for j in range(jmax + 1):
                ps, rps = self.psr.next()
                mm(S, ps[:], kcT[:, g, j * 128:(j + 1) * 128], Qlo, True, True, [r_kcT, rQ0], [rps])
                act(S, Et[:, j, :], ps[:], AF.Exp, [rps], [r_Et], scale=0.125, acc=(j > 0))
                if 16 * (128 * j + 127) + 31 > t0:
                    stt(S, "vector", Et[:, j, :], qidx[:], thr[:, j * NQT + qt:j * NQT + qt + 1], Et[:, j, :], ALU.is_ge, ALU.mult,
                        [r_qidx, r_thr, r_Et], [r_Et], acc=True)
            for j in range(jmax + 1):
                mm(S, pa[0][:], vcx[:, g, j, :], Et[:, j, :], j == 0, j == jmax, [r_vcx, r_Et], [rpa[0]])
            for r in range(4):
                for j in range(jmax + 1):
                    mm(S, pa[1][:, r * 128:(r + 1) * 128], Et[:, j, r * 128:(r + 1) * 128], mimp[:, j, :], j == 0, j == jmax,
                       [r_Et, r_mimp], [rpa[1]], inc=(j == jmax and r == 3))
            psd, rpsd = self.psr.next()
            for r in range(4):
                for j in range(jmax + 1):
                    mm(S, psd[:, r:r + 1], Et[:, j, r * 128:(r + 1) * 128], self.ones[:, 0:1], j == 0, j == jmax,
                       [r_Et, rc], [rpsd], inc=(j == jmax and r == 3))
            ts(S, "vector", rdT[:], psd[:, 0:4], 1e-30, None, ALU.max, None, [rpsd], [r_rdT])
            S.op("vector", lambda e, o=rdT[:]: e.reciprocal(out=o, in_=o), [r_rdT], [r_rdT])
            stt(S, "vector", score[:], pa[1][:, 0:128], rdT[:, 0:1], sb[:], ALU.mult, ALU.add, [rpa[1], r_rdT, rsb], [r_score])
            for r in range(1, 4):
                stt(S, "vector", score[:], pa[1][:, r * 128:(r + 1) * 128], rdT[:, r:r + 1], score[:], ALU.mult, ALU.add,
                    [rpa[1], r_rdT, r_score], [r_score])
            S.op("vector", lambda e, o=m8[:, 0:8], s_=score[:]: e.max(out=o, in_=s_), [r_score], [r_m8])
            S.op("vector", lambda e, o=sc2[:], m=m8[:, 0:8], s_=score[:]: e.match_replace(out=o, in_to_replace=m, in_values=s_, imm_value=-1e9),
                 [r_score, r_m8], [r_sc2])
            S.op("vector", lambda e, o=m8[:, 8:16], s_=sc2[:]: e.max(out=o, in_=s_), [r_sc2], [r_m8], acc=True)
            ts(S, "vector", selA[:], score[:], m8[:, 15:16], NEG, ALU.is_lt, ALU.mult, [r_score, r_m8], [r_selA])
            ts(S, "gpsimd", selB[:, 64:128], score[:, 0:64], m8[:, 15:16], NEG, ALU.is_lt, ALU.mult, [r_score, r_m8], [r_selB])
            ts(S, "gpsimd", selB[:, 0:64], score[:, 64:128], m8[:, 15:16], NEG, ALU.is_lt, ALU.mult, [r_score, r_m8], [r_selB], acc=True)
            finA(0, i, pa[0], rpa[0])

        def trans(i):
            d = st[i]
            c = d["ch"]
            selA, r_selA = c["selA"]; selB, r_selB = c["selB"]
            for hb in range(d["nhb"]):
                psT, rpsT = self.psr.next()
                src, rsrc_ = (selB, r_selB) if hb == 0 else (selA, r_selA)
                tr(S, psT[:, 0:128], src[:], self.ident, [rsrc_, rc], [rpsT])
                Qh, rQh = d["Qs"][hb]
                for r in range(4):
                    cp(S, "scalar" if r % 2 else "vector", Qh[64:128, r, :], psT[64:128, 0:128], [rpsT], [rQh], acc=True)

        def win(i):
            g, qt = iters[i]
            d = st[i]
            Q0, rQ0 = d["Qs"][0]
            Qlo = Q0[0:64].rearrange("p r q -> p (r q)")
            cw0 = TM_COL["n_vw"] + g * 64
            k0 = max(0, qt - 4)
            nk = qt - k0 + 1
            S.dma("sync", vwx[:, 0:nk, 0:64], self.ztm[k0 * 128:(qt + 1) * 128, cw0:cw0 + 64].rearrange("(k p) d -> p k d", p=128),
                  reads=[self.dres["ztm"]], writes=[r_vwx], acc=True)
            S.dma("sync", kwb[:, 0:nk, :], self.nkr[256 + g * 64:256 + (g + 1) * 64, k0 * 128:(qt + 1) * 128].rearrange("d (k p) -> d k p", p=128),
                  reads=[self.dres["nkr"]], writes=[r_kwb])
            for kt in range(k0, qt + 1):
                ps, rps = self.psr.next()
                last_c = (kt == qt)
                last_a = (kt == qt - 4)
                mm(S, ps[:], kwb[:, kt - k0, :], Qlo, True, not (last_c or last_a), [r_kwb, rQ0], [rps])
                if last_c:
                    mm(S, ps[:], self.ident, negc[:], False, True, [rc, r_negc], [rps])
                if last_a:
                    mm(S, ps[:], self.ident, nega[:], False, True, [rc, r_nega], [rps])
                pt, rpt = self.obr.next()
                act(S, pt[:], ps[:], AF.Exp, [rps], [rpt], scale=0.125)
                mm(S, pa[3][:], vwx[:, kt - k0, :], pt[:], kt == k0, kt == qt, [r_vwx, rpt], [rpa[3]])

        def sel(i):
            g, qt = iters[i]
            d = st[i]
            for kt in range(qt + 1):
                Qh, rQh = d["Qs"][kt // 32]
                ps, rps = self.psr.next()
                mm(S, ps[:], kT2[:, kt * 128:(kt + 1) * 128], Qh[:].rearrange("p r q -> p (r q)"), True, kt != qt, [r_kT2, rQh], [rps])
                if kt == qt:
                    mm(S, ps[:], self.ident, negc[:], False, True, [rc, r_negc], [rps])
                pt, rpt = self.obr.next()
                act(S, pt[:], ps[:], AF.Exp, [rps], [rpt], scale=0.125)
                mm(S, pa[2][:], vsx[:, kt, :], pt[:], kt == 0, kt == qt, [r_vsx, rpt], [rpa[2]])

        load(0)
        cmpA(0)
        finB(0, 0, pa[0], rpa[0], True)
        for i in range(n_it):
            g, qt = iters[i]
            if qt == 0:
                group_loads(g)
            win(i)
            finA(2, i, pa[3], rpa[3])
            if i > 0:
                finB(1, i - 1, pa[2], rpa[2], False)
                store(i - 1)
            trans(i)
            if i + 1 < n_it:
                load(i + 1)
                cmpA(i + 1)
            finB(2, i, pa[3], rpa[3], False)
            sel(i)
            if i + 1 < n_it:
                finB(0, i + 1, pa[0], rpa[0], True)
            finA(1, i, pa[2], rpa[2])
        finB(1, n_it - 1, pa[2], rpa[2], False)
        store(n_it - 1)


def build(T, L, dbg=(), phases="ACMNE"):
    P = Prog(T, L, dbg)
    S = P.S
    src, rsrc = P.xT, Res("xT")
    for l in range(L):
        if l == L - 1:
            dst, rdst = P.outT, P.dres["outT"]
        else:
            dst, rdst = P.xbuf[l % 2], P.dres["xs%d" % (l % 2)]
        if "A" in phases:
            P.phase_A(l, src, rsrc)
        if "C" in phases:
            P.phase_C(l)
        if "M" in phases:
            P.phase_M(l)
        if "N" in phases:
            P.phase_N(l)
        if "E" in phases:
            P.phase_E(l, src, rsrc, dst, rdst)
        src, rsrc = dst, rdst
    fin = [P.dres["outT"]] + [P.dres[k] for k in P.dbg if k in P.dres]
    S.finish(fin)
    return P


def tile_w(W):
    K, M = W.shape
    return np.ascontiguousarray(W.reshape(K // 128, 128, M // 128, 128).transpose(2, 1, 0, 3))


def colmajor(g, L, nchunk):
    return np.ascontiguousarray(g.reshape(L, nchunk, 128).transpose(0, 2, 1))


def prep_weights(inp, L):
    out = {}
    f = np.float32
    w_in = inp["w_in"]
    cols = []
    for n in FM_SEGS:
        o, s = _off[n]
        cols.append(w_in[:, :, o:o + s])
    cols.append(np.zeros((L, D, N_FM_BLK * 128 - N_FM), f))
    for n in TM_SEGS:
        o, s = _off[n]
        cols.append(w_in[:, :, o:o + s])
    cols.append(np.zeros((L, D, N_TM_BLK * 128 - N_TM), f))
    wr = np.concatenate(cols, axis=2)
    out["w_in"] = np.stack([tile_w(wr[l]) for l in range(L)])
    out["gmix"] = colmajor(inp["norm_mix_g"], L, 16)
    out["gffn"] = colmajor(inp["norm_ffn_g"], L, 16)
    out["gmem"] = colmajor(inp["norm_mem_g"], L, 16)
    out["w_up"] = np.stack([np.stack([tile_w(inp[k][l]) for k in ("w_up_a", "w_up_b", "w_up_c")]) for l in range(L)])
    out["w_out"] = np.stack([tile_w(inp["w_out"][l]) for l in range(L)])
    out["w_fg"] = np.stack([tile_w(inp["w_ffn_gate"][l]) for l in range(L)])
    out["w_fu"] = np.stack([tile_w(inp["w_ffn_up"][l]) for l in range(L)])
    out["w_fd"] = np.stack([tile_w(inp["w_ffn_down"][l]) for l in range(L)])
    out["w_mk"] = np.stack([tile_w(inp["w_mem_k"][l]) for l in range(L)])
    out["w_mv"] = np.stack([tile_w(inp["w_mem_v"][l]) for l in range(L)])
    out["g_cq"] = colmajor(inp["c_q_norm_g"], L, 2)
    out["g_ck"] = colmajor(inp["c_k_norm_g"], L, 2)
    out["m_cw"] = np.ascontiguousarray(inp["m_conv_w"].reshape(L, 4, 16, 128).transpose(0, 3, 2, 1))
    out["m_b8"] = np.concatenate([inp["m_i_bias"], inp["m_f_bias"]], axis=1).reshape(L, 1, 8).astype(f)
    out["m_gn"] = colmajor(inp["m_norm_g"].reshape(L, 1024), L, 8)
    g4 = np.stack([inp["n_q_norm_g"], inp["n_kc_norm_g"], inp["n_ks_norm_g"], inp["n_kw_norm_g"]], axis=2)
    out["g_n4"] = np.ascontiguousarray(np.concatenate([g4, g4], axis=1))
    out["n_w1"] = np.stack([np.stack([inp[k][l].reshape(32, 64, 128).transpose(1, 0, 2) for k in ("n_cmp_w1_k", "n_cmp_w1_v")])
                            for l in range(L)])
    out["n_pe"] = np.stack([np.stack([inp[k][l].T for k in ("n_cmp_pe_k", "n_cmp_pe_v")]) for l in range(L)])
    out["n_w2"] = np.stack([np.stack([inp[k][l] for k in ("n_cmp_w2_k", "n_cmp_w2_v")]) for l in range(L)])
    return {k: np.ascontiguousarray(v, dtype=f) for k, v in out.items()}


def consts(T):
    f = np.float32
    NQT = T // 128
    NC = T // 16 - 1
    NCH = (NC + 127) // 128
    NCP = NCH * 128
    c = {}
    c128 = np.zeros((5, 128, 128), f)
    c128[C_ONES] = 1.0
    c128[C_ID] = np.eye(128, dtype=f)
    c128[C_TRI] = np.triu(np.ones((128, 128), f))
    bd = np.zeros((128, 128), f)
    bd[:64, :64] = 1.0
    bd[64:, 64:] = 1.0
    c128[C_BD64] = bd
    rm = np.zeros((128, 128), f)
    for hb in (0, 64):
        for m in range(64):
            if m < 32:
                rm[hb + m + 32, hb + m] = -1.0
            else:
                rm[hb + m - 32, hb + m] = 1.0
    c128[C_RM] = rm
    c["c128"] = c128
    inv = np.power(f(10000.0), -np.arange(32, dtype=f) * f(2.0) / f(64)).astype(f)
    pos = np.arange(T, dtype=f)
    ang = (pos[:, None] * inv[None, :]).astype(f)
    cs = np.cos(ang).astype(f).T
    sn = np.sin(ang).astype(f).T
    c["c_cos"] = np.ascontiguousarray(np.concatenate([cs, cs, cs, cs], axis=0))
    c["c_sin"] = np.ascontiguousarray(np.concatenate([sn, sn, sn, sn], axis=0))
    cend = (np.arange(NCP, dtype=f) * f(16) + f(31)).astype(f)
    angc = (cend[:, None] * inv[None, :]).astype(f)
    csc = np.cos(angc).astype(f).T
    snc = np.sin(angc).astype(f).T
    c["c_cosc"] = np.ascontiguousarray(np.concatenate([csc, csc], axis=0))
    c["c_sinc"] = np.ascontiguousarray(np.concatenate([snc, snc], axis=0))
    mimp = np.zeros((NCP, 128), f)
    for n in range(NC):
        mimp[n, n // 4] += 1.0
        mimp[n, (n + 1) // 4] += 1.0
    c["c_mimp"] = np.ascontiguousarray(mimp.reshape(NCH, 128, 128).transpose(1, 0, 2))
    g2 = np.zeros((64, T), f)
    for b in range(T // 64):
        g2[b % 64, b * 64:(b + 1) * 64] = 1.0
    c["c_g2"] = g2
    sb = np.zeros((NQT, 128, 128), f)
    blk = np.arange(128)
    for qt in range(NQT):
        for q in range(128):
            cur = (qt * 128 + q) // 64
            row = np.where(blk > cur, -1.0, 0.0)
            row[(blk == 0) | (blk == cur) | (blk == cur - 1)] = 1.0e4
            sb[qt, q] = row
    c["c_sbias"] = sb
    thr = np.zeros((128, NCH * NQT), f)
    p = np.arange(128)
    for j in range(NCH):
        for qt in range(NQT):
            thr[:, j * NQT + qt] = 16 * (128 * j + p) + 31 - 128 * qt
    c["c_thr"] = thr
    c["c_qidx"] = np.ascontiguousarray(np.tile(np.arange(128, dtype=f)[None, :], (128, 4)))
    kk = np.arange(128)[:, None]
    qq = np.arange(128)[None, :]
    c["c_negc"] = np.ascontiguousarray(np.tile(np.where(kk > qq, NEG, 0.0).astype(f), (1, 4)))
    c["c_nega"] = np.ascontiguousarray(np.tile(np.where(kk <= qq, NEG, 0.0).astype(f), (1, 4)))
    return c


_CACHE = {}


def kernel(**inputs):
    inp = {k: np.asarray(v) for k, v in inputs.items()}
    B, T, _ = inp["x"].shape
    L = inp["w_in"].shape[0]
    key = (T, L)
    if key not in _CACHE:
        _CACHE[key] = build(T, L)
    P = _CACHE[key]
    w = prep_weights(inp, L)
    w.update(consts(T))
    maps = []
    for b in range(B):
        m = {k: w[k] for k in P.inputs if k in w}
        m["xT"] = np.ascontiguousarray(inp["x"][b].T, dtype=np.float32)
        m["memT"] = np.ascontiguousarray(inp["mem"][b].T, dtype=np.float32)
        maps.append(m)
    res = run_bass_kernel_spmd(P.nc, maps, core_ids=list(range(B)))
    out = np.stack([np.asarray(res.results[b]["outT"]).T for b in range(B)])
    return np.ascontiguousarray(out, dtype=np.float32)
```

```python
import contextlib
import numpy as np
import concourse.bass as bass
import concourse.mybir as mybir
from concourse.bass_utils import run_bass_kernel_spmd

F32 = mybir.dt.float32
AF = mybir.ActivationFunctionType
ALU = mybir.AluOpType

D = 2048
DFF = 5632
MEM = 256
EPS = 1e-6
NEG = -30000.0
ENGS = ("tensor", "vector", "scalar", "gpsimd", "sync")

_sizes = (2048, 1024, 1024, 4, 4, 1024, 256, 256, 256, 256, 256, 256, 48, 1024, 2048, 2048, 2048)
_names = ("m_qk", "m_v", "m_o", "m_i", "m_f", "n_q", "n_kc", "n_vc", "n_ks", "n_vs", "n_kw", "n_vw", "n_g",
          "c_q", "g_a", "g_b", "g_c")
_off = {}
_o = 0
for _n, _s in zip(_names, _sizes):
    _off[_n] = (_o, _s)
    _o += _s
D_IN = _o
FM_SEGS = ("m_qk", "m_o", "n_q", "n_kc", "n_vc", "n_ks", "n_kw", "c_q", "g_a", "g_b", "g_c", "n_g")
TM_SEGS = ("m_v", "n_vs", "n_vw", "m_i", "m_f")
SIG_SEGS = ("m_o", "g_a", "g_b", "g_c", "n_g")
FM_ROW = {}
_r = 0
for _n in FM_SEGS:
    FM_ROW[_n] = _r
    _r += _off[_n][1]
N_FM = _r
N_FM_BLK = (N_FM + 127) // 128
TM_COL = {}
_r = 0
for _n in TM_SEGS:
    TM_COL[_n] = _r
    _r += _off[_n][1]
N_TM = _r
N_TM_BLK = (N_TM + 127) // 128
FM_SIG_BLK = set()
for _n in SIG_SEGS:
    for _b in range(FM_ROW[_n] // 128, (FM_ROW[_n] + _off[_n][1] + 127) // 128):
        FM_SIG_BLK.add(_b)


class Res:
    __slots__ = ("name", "w", "r")

    def __init__(self, name=""):
        self.name = name
        self.w = {}
        self.r = {}


class Sched:
    def __init__(self, nc, n_dma_sems=28, strict_same=True):
        self.nc = nc
        self.es = contextlib.ExitStack()
        self.streams = {e: [] for e in ENGS}
        self.sem = {}
        self.cnt = {}
        self.known = {e: {} for e in ENGS}
        self.semobj = {}
        for e in ENGS:
            s = self.es.enter_context(nc.semaphore("s_" + e))
            self.sem[e] = "s_" + e
            self.semobj["s_" + e] = s
            self.cnt[e] = 0
        self.dma_pool = []
        for i in range(n_dma_sems):
            nm = "d%d" % i
            self.semobj[nm] = self.es.enter_context(nc.semaphore(nm))
            self.dma_pool.append([nm, 0])
        self.dma_rr = 0
        self.strict_same = strict_same
        self.n_ops = 0

    def sb(self, name, shape, dtype=F32):
        return self.es.enter_context(self.nc.sbuf_tensor(name, list(shape), dtype))

    def ps(self, name, shape, dtype=F32):
        return self.es.enter_context(self.nc.psum_tensor(name, list(shape), dtype))

    def _waits(self, eng, reads, writes, acc):
        need = {}
        own = self.sem[eng]

        def add(d, skip_own=False):
            for k, v in d.items():
                if skip_own and k == own:
                    continue
                if v > need.get(k, 0):
                    need[k] = v

        for r in reads:
            add(r.w)
        for w in writes:
            add(w.w, skip_own=acc)
            add(w.r)
        kn = self.known[eng]
        for k, v in need.items():
            if k == own and (eng == "tensor" or not self.strict_same):
                continue
            if kn.get(k, 0) >= v:
                continue
            kn[k] = v
            self.streams[eng].append(("w", k, v))

    def op(self, eng, fn, reads=(), writes=(), inc=True, acc=False):
        self._waits(eng, reads, writes, acc)
        own = self.sem[eng]
        if inc:
            self.cnt[eng] += 1
            tok = self.cnt[eng]
        else:
            tok = self.cnt[eng] + 1
        self.streams[eng].append(("o", fn, inc))
        for r in reads:
            if r.r.get(own, 0) < tok:
                r.r[own] = tok
        for w in writes:
            if acc:
                if w.w.get(own, 0) < tok:
                    w.w[own] = tok
            else:
                w.w = {own: tok}
                w.r = {}
        self.n_ops += 1

    def dma(self, q, out, in_, reads=(), writes=(), acc=False, **kw):
        ent = self.dma_pool[self.dma_rr]
        self.dma_rr = (self.dma_rr + 1) % len(self.dma_pool)
        nm = ent[0]
        kn = self.known[q]
        if ent[1] > 0 and kn.get(nm, 0) < ent[1]:
            kn[nm] = ent[1]
            self.streams[q].append(("w", nm, ent[1]))
        self._waits(q, reads, writes, acc)
        ent[1] += 16
        tok = ent[1]
        self.streams[q].append(("d", out, in_, nm, kw))
        for r in reads:
            if r.r.get(nm, 0) < tok:
                r.r[nm] = tok
        for w in writes:
            if acc:
                if w.w.get(nm, 0) < tok:
                    w.w[nm] = tok
            else:
                w.w = {nm: tok}
                w.r = {}
        self.n_ops += 1

    def finish(self, final_res):
        nc = self.nc
        need = {}
        for r in final_res:
            for k, v in r.w.items():
                need[k] = max(need.get(k, 0), v)
        for k, v in need.items():
            self.streams["sync"].append(("w", k, v))
        semobj = self.semobj
        streams = self.streams
        sem = self.sem

        def replay(name, e):
            own = semobj[sem[name]]
            for it in streams[name]:
                if it[0] == "w":
                    e.wait_ge(semobj[it[1]], it[2])
                elif it[0] == "o":
                    ins = it[1](e)
                    if it[2]:
                        ins.then_inc(own, 1)
                else:
                    e.dma_start(out=it[1], in_=it[2], **it[4]).then_inc(semobj[it[3]], 16)

        with nc.Block() as block:
            @block.sync
            def _(e):
                replay("sync", e)

            @block.tensor
            def _(e):
                replay("tensor", e)

            @block.vector
            def _(e):
                replay("vector", e)

            @block.scalar
            def _(e):
                replay("scalar", e)

            @block.gpsimd
            def _(e):
                replay("gpsimd", e)
        self.es.close()


def mm(S, out, lhsT, rhs, start, stop, reads, writes, inc=None):
    S.op("tensor", lambda e: e.matmul(out, lhsT=lhsT, rhs=rhs, start=start, stop=stop), reads, writes,
         inc=(stop if inc is None else inc), acc=(not start))


def tr(S, out, in_, ident, reads, writes):
    S.op("tensor", lambda e: e.transpose(out=out, in_=in_, identity=ident), reads, writes)


def act(S, out, in_, func, reads, writes, bias=None, scale=None, acc=False):
    kw = {}
    if bias is not None:
        kw["bias"] = bias
    if scale is not None:
        kw["scale"] = scale
    S.op("scalar", lambda e: e.activation(out=out, in_=in_, func=func, **kw), reads, writes, acc=acc)


def tt(S, eng, out, in0, in1, op, reads, writes, acc=False):
    S.op(eng, lambda e: e.tensor_tensor(out=out, in0=in0, in1=in1, op=op), reads, writes, acc=acc)


def stt(S, eng, out, in0, scalar, in1, op0, op1, reads, writes, acc=False):
    S.op(eng, lambda e: e.scalar_tensor_tensor(out=out, in0=in0, scalar=scalar, in1=in1, op0=op0, op1=op1),
         reads, writes, acc=acc)


def ts(S, eng, out, in0, s1, s2, op0, op1, reads, writes, acc=False):
    if s2 is None:
        S.op(eng, lambda e: e.tensor_scalar(out=out, in0=in0, scalar1=s1, scalar2=None, op0=op0), reads, writes, acc=acc)
    else:
        S.op(eng, lambda e: e.tensor_scalar(out=out, in0=in0, scalar1=s1, scalar2=s2, op0=op0, op1=op1),
             reads, writes, acc=acc)


def cp(S, eng, out, in_, reads, writes, acc=False):
    if eng == "scalar":
        S.op(eng, lambda e: e.copy(out=out, in_=in_), reads, writes, acc=acc)
    else:
        S.op(eng, lambda e: e.tensor_copy(out=out, in_=in_), reads, writes, acc=acc)


class Ring:
    def __init__(self, S, name, n, shape, psum=False):
        self.bufs = [(S.ps if psum else S.sb)("%s%d" % (name, i), shape) for i in range(n)]
        self.res = [Res("%s%d" % (name, i)) for i in range(n)]
        self.i = 0

    def next(self):
        b, r = self.bufs[self.i], self.res[self.i]
        self.i = (self.i + 1) % len(self.bufs)
        return b, r


class WStream:
    def __init__(self, S, nbuf=5):
        self.S = S
        self.bufs = [S.sb("wb%d" % i, [128, 16, 128]) for i in range(nbuf)]
        self.res = [Res("wb%d" % i) for i in range(nbuf)]
        self.q = []
        self.issued = 0
        self.consumed = 0

    def push(self, ap, kc):
        self.q.append((ap, kc))

    def get(self):
        nb = len(self.bufs)
        while self.issued < len(self.q) and self.issued < self.consumed + nb - 1:
            ap, kc = self.q[self.issued]
            b = self.issued % nb
            self.S.dma("sync", self.bufs[b][:, 0:kc, :], ap, writes=[self.res[b]])
            self.issued += 1
        b = self.consumed % nb
        kc = self.q[self.consumed][1]
        self.consumed += 1
        return self.bufs[b], self.res[b], kc


class Arena:
    def __init__(self, S, name, n):
        self.t = S.sb(name, [128, n])
        self.n = n
        self.off = 0
        self.live = []
        self.prev = {}

    def reset(self):
        for r in self.live:
            for d in (r.w, r.r):
                for k, v in d.items():
                    if v > self.prev.get(k, 0):
                        self.prev[k] = v
        self.live = []
        self.off = 0

    def alloc(self, shape, name="", p0=0, share=None):
        n = int(np.prod(shape[1:]))
        off = self.off if share is None else share
        assert off + n <= self.n, (name, off, n, self.n)
        v = self.t[p0:p0 + shape[0], off:off + n]
        if len(shape) == 3:
            v = v.rearrange("p (a b) -> p a b", a=shape[1])
        elif len(shape) == 4:
            v = v.rearrange("p (a b c) -> p a b c", a=shape[1], b=shape[2])
        if share is None:
            self.off += n
        r = Res(name)
        r.r = dict(self.prev)
        self.live.append(r)
        return v, r


class ZProxy:
    def __init__(self, prog, T):
        self.segs = []
        for n in FM_SEGS:
            rows = ((_off[n][1] + 127) // 128) * 128
            t = prog.nc.dram_tensor("z_" + n, [rows, T], F32, kind="Internal").ap()
            self.segs.append((FM_ROW[n], rows, t))

    def __getitem__(self, key):
        rs, cs = key
        for base, rows, t in self.segs:
            if base <= rs.start < base + rows:
                assert rs.stop <= base + rows
                return t[rs.start - base:rs.stop - base, cs]
        raise KeyError(key)


C_ONES, C_ID, C_TRI, C_BD64, C_RM = range(5)
LN16 = float(np.log(16.0))


class Prog:
    def __init__(self, T, L, dbg=()):
        self.T, self.L = T, L
        self.NT = T // 512
        self.NQT = T // 128
        self.NC = T // 16 - 1
        self.NCH = (self.NC + 127) // 128
        self.NCP = self.NCH * 128
        nc = bass.Bass("TRN2", target_bir_lowering=False)
        self.nc = nc
        self.S = Sched(nc)
        self.dbg = set(dbg)
        self.inputs = {}
        self.dres = {}
        S = self.S
        NQT, NCH, NCP = self.NQT, self.NCH, self.NCP
        self.xT = self.din("xT", [D, T])
        self.memT = self.din("memT", [D, MEM])
        self.w_in = self.din("w_in", [L, N_FM_BLK + N_TM_BLK, 128, 16, 128])
        self.gmix = self.din("gmix", [L, 128, 16])
        self.gffn = self.din("gffn", [L, 128, 16])
        self.gmem = self.din("gmem", [L, 128, 16])
        self.w_up = self.din("w_up", [L, 3, 16, 128, 8, 128])
        self.w_out = self.din("w_out", [L, 16, 128, 16, 128])
        self.w_fg = self.din("w_fg", [L, 44, 128, 16, 128])
        self.w_fu = self.din("w_fu", [L, 44, 128, 16, 128])
        self.w_fd = self.din("w_fd", [L, 16, 128, 44, 128])
        self.w_mk = self.din("w_mk", [L, 8, 128, 16, 128])
        self.w_mv = self.din("w_mv", [L, 8, 128, 16, 128])
        self.g_cq = self.din("g_cq", [L, 128, 2])
        self.g_ck = self.din("g_ck", [L, 128, 2])
        self.m_cw = self.din("m_cw", [L, 128, 16, 4])
        self.m_b8 = self.din("m_b8", [L, 1, 8])
        self.m_gn = self.din("m_gn", [L, 128, 8])
        self.g_n4 = self.din("g_n4", [L, 128, 4])
        self.n_w1 = self.din("n_w1", [L, 2, 64, 32, 128])
        self.n_pe = self.din("n_pe", [L, 2, 64, 32])
        self.n_w2 = self.din("n_w2", [L, 2, 128, 64])
        self.c128 = self.din("c128", [5, 128, 128])
        self.c_cos = self.din("c_cos", [128, T])
        self.c_sin = self.din("c_sin", [128, T])
        self.c_cosc = self.din("c_cosc", [64, NCP])
        self.c_sinc = self.din("c_sinc", [64, NCP])
        self.c_mimp = self.din("c_mimp", [128, NCH, 128])
        self.c_g2 = self.din("c_g2", [64, T])
        self.c_sbias = self.din("c_sbias", [NQT, 128, 128])
        self.c_thr = self.din("c_thr", [128, NCH * NQT])
        self.c_qidx = self.din("c_qidx", [128, 512])
        self.c_negc = self.din("c_negc", [128, 512])
        self.c_nega = self.din("c_nega", [128, 512])
        self.zfm = ZProxy(self, T)
        self.dres["zfm"] = Res("zfm")
        self.ztm = self.dscr("ztm", [T, N_TM_BLK * 128])
        self.yT = [self.dscr("y%sT" % b, [1024, T], inp=("yin" in self.dbg)) for b in "abc"]
        self.qkc = self.dscr("qkc", [2048, T])
        self.nqr = self.dscr("nqr", [1024, T])
        self.nkr = self.dscr("nkr", [512, T])
        self.xbuf = [self.dscr("xs%d" % i, [D, T]) for i in range(2)]
        self.outT = nc.dram_tensor("outT", [D, T], F32, kind="ExternalOutput").ap()
        self.dres["outT"] = Res("outT")
        self.r_const = Res("const")
        self.cst = S.sb("cst", [128, 5, 128])
        S.dma("sync", self.cst[:], self.c128.rearrange("k p m -> p k m"), writes=[self.r_const])
        self.ones = self.cst[:, C_ONES, :]
        self.ident = self.cst[:, C_ID, :]
        self.tri = self.cst[:, C_TRI, :]
        self.bd64 = self.cst[:, C_BD64, :]
        self.rm = self.cst[:, C_RM, :]
        self.gm = S.sb("gm", [128, L, 16])
        self.gf = S.sb("gf", [128, L, 16])
        self.gme = S.sb("gme", [128, L, 16])
        self.gsm = S.sb("gsm", [128, L, 24])
        S.dma("sync", self.gm[:], self.gmix.rearrange("l p c -> p l c"), writes=[self.r_const], acc=True)
        S.dma("sync", self.gf[:], self.gffn.rearrange("l p c -> p l c"), writes=[self.r_const], acc=True)
        S.dma("sync", self.gme[:], self.gmem.rearrange("l p c -> p l c"), writes=[self.r_const], acc=True)
        S.dma("sync", self.gsm[:, :, 0:2], self.g_cq.rearrange("l p c -> p l c"), writes=[self.r_const], acc=True)
        S.dma("sync", self.gsm[:, :, 2:4], self.g_ck.rearrange("l p c -> p l c"), writes=[self.r_const], acc=True)
        S.dma("sync", self.gsm[:, :, 4:12], self.m_gn.rearrange("l p c -> p l c"), writes=[self.r_const], acc=True)
        S.dma("sync", self.gsm[:, :, 12:16], self.g_n4.rearrange("l p c -> p l c"), writes=[self.r_const], acc=True)
        for l in range(L):
            S.dma("sync", self.gsm[:, l, 16:24], self.m_b8[l].partition_broadcast(128), writes=[self.r_const], acc=True)
        self.cw = S.sb("cw", [128, L, 16, 4])
        S.dma("sync", self.cw[:], self.m_cw.rearrange("l p c j -> p l c j"), writes=[self.r_const], acc=True)
        self.ws = WStream(S, nbuf=4)
        self.psr = Ring(S, "pp", 4, [128, 512], psum=True)
        self.pa = [S.ps("pa%d" % i, [128, 512]) for i in range(4)]
        self.r_pa = [Res("pa%d" % i) for i in range(4)]
        self.obr = Ring(S, "ob", 4, [128, 512])
        self.sgr = Ring(S, "sg", 2, [128, 512])
        self.tmr = Ring(S, "tm", 2, [128, 512])
        self.AR0 = Arena(S, "ar0", 16 * 512)
        self.AR1 = Arena(S, "ar1", 16 * 512)
        self.AR2 = Arena(S, "ar2", 22 * 512)
        self.rs = S.sb("rs", [128, 512])
        self.r_rs = Res("rs")

    def din(self, name, shape):
        t = self.nc.dram_tensor(name, list(shape), F32, kind="ExternalInput").ap()
        self.inputs[name] = tuple(shape)
        return t

    def dscr(self, name, shape, inp=False):
        kind = "Internal"
        if inp:
            kind = "ExternalInput"
            self.inputs[name] = tuple(shape)
        elif name in self.dbg:
            kind = "ExternalOutput"
        t = self.nc.dram_tensor(name, list(shape), F32, kind=kind).ap()
        self.dres[name] = Res(name)
        return t

    def dense_bufs(self):
        for a in (self.AR0, self.AR1, self.AR2):
            a.reset()
        self.xt, self.r_xt = self.AR0.alloc([128, 16, 512], "xt")
        self.hb, self.r_hb = self.AR1.alloc([128, 16, 512], "hb")
        self.actb, self.r_actb = self.AR2.alloc([128, 22, 512], "actb")

    def rstd(self, out, r_out, ps, r_ps, n):
        S = self.S
        act(S, out, ps, AF.Ln, [r_ps], [r_out], bias=EPS, scale=1.0 / n)
        act(S, out, out, AF.Exp, [r_out], [r_out], scale=-0.5)

    def rmsnorm_tile(self, gsb, l):
        S = self.S
        xt, hb, rs = self.xt, self.hb, self.rs
        act(S, hb[:], xt[:], AF.Square, [self.r_xt], [self.r_hb])
        ps, rps = self.psr.next()
        for c in range(16):
            mm(S, ps[:], self.ones, hb[:, c, :], c == 0, c == 15, [self.r_const, self.r_hb], [rps])
        self.rstd(rs[:], self.r_rs, ps[:], rps, D)
        for c in range(16):
            if c % 3 != 2:
                stt(S, "vector", hb[:, c, :], xt[:, c, :], gsb[:, l, c:c + 1], rs[:], ALU.mult, ALU.mult,
                    [self.r_xt, self.r_rs, self.r_const], [self.r_hb], acc=(c > 0))
            else:
                ts(S, "gpsimd", hb[:, c, :], xt[:, c, :], gsb[:, l, c:c + 1], None, ALU.mult, None,
                   [self.r_xt, self.r_const], [self.r_hb], acc=True)
                tt(S, "gpsimd", hb[:, c, :], hb[:, c, :], rs[:], ALU.mult, [self.r_rs, self.r_hb], [self.r_hb], acc=True)

    def phase_A(self, l, xsrc, r_xsrc):
        S = self.S
        self.dense_bufs()
        nblk = N_FM_BLK + N_TM_BLK
        for tt_ in range(self.NT):
            for b in range(nblk):
                self.ws.push(self.w_in[l, b], 16)
        for tt_ in range(self.NT):
            t0 = tt_ * 512
            S.dma("sync", self.xt[:], xsrc[:, t0:t0 + 512].rearrange("(c p) t -> p c t", p=128),
                  reads=[r_xsrc], writes=[self.r_xt])
            self.rmsnorm_tile(self.gm, l)
            hb = self.hb
            for b in range(nblk):
                wb, rwb, kc = self.ws.get()
                ps, rps = self.psr.next()
                ob, rob = self.obr.next()
                if b < N_FM_BLK:
                    for c in range(16):
                        mm(S, ps[:], wb[:, c, :], hb[:, c, :], c == 0, c == 15, [rwb, self.r_hb], [rps])
                    if b in FM_SIG_BLK:
                        act(S, ob[:], ps[:], AF.Sigmoid, [rps], [rob])
                    elif b % 2 == 0:
                        cp(S, "vector", ob[:], ps[:], [rps], [rob])
                    else:
                        cp(S, "scalar", ob[:], ps[:], [rps], [rob])
                    S.dma("gpsimd", self.zfm[b * 128:(b + 1) * 128, t0:t0 + 512], ob[:], reads=[rob],
                          writes=[self.dres["zfm"]], acc=True)
                else:
                    bt = b - N_FM_BLK
                    for j in range(4):
                        for c in range(16):
                            mm(S, ps[:, j * 128:(j + 1) * 128], hb[:, c, j * 128:(j + 1) * 128], wb[:, c, :],
                               c == 0, c == 15, [rwb, self.r_hb], [rps], inc=(c == 15 and j == 3))
                    cp(S, "vector", ob[:], ps[:], [rps], [rob])
                    S.dma("gpsimd", self.ztm[t0:t0 + 512, bt * 128:(bt + 1) * 128].rearrange("(j p) c -> p j c", p=128),
                          ob[:].rearrange("p (j c) -> p j c", j=4), reads=[rob], writes=[self.dres["ztm"]], acc=True)

    def phase_E(self, l, xsrc, r_xsrc, xdst, r_xdst):
        S = self.S
        self.dense_bufs()
        xt, hb = self.xt, self.hb
        actb = self.actb
        yb = actb
        for tt_ in range(self.NT):
            for br in range(3):
                for mb in range(16):
                    self.ws.push(self.w_up[l, br, mb], 8)
            for mb in range(16):
                self.ws.push(self.w_out[l, mb], 16)
            for half in range(2):
                for fb in range(22):
                    self.ws.push(self.w_fg[l, half * 22 + fb], 16)
                    self.ws.push(self.w_fu[l, half * 22 + fb], 16)
                for mb in range(16):
                    self.ws.push(self.w_fd[l, mb, :, half * 22:half * 22 + 16, :], 16)
                    self.ws.push(self.w_fd[l, mb, :, half * 22 + 16:half * 22 + 22, :], 6)
        for tt_ in range(self.NT):
            t0 = tt_ * 512
            S.dma("sync", xt[:], xsrc[:, t0:t0 + 512].rearrange("(c p) t -> p c t", p=128),
                  reads=[r_xsrc], writes=[self.r_xt])
            gname = ("g_a", "g_b", "g_c")
            for br in range(3):
                S.dma("sync", yb[:, 0:8, :], self.yT[br][:, t0:t0 + 512].rearrange("(c p) t -> p c t", p=128),
                      reads=[self.dres["y%sT" % "abc"[br]]], writes=[self.r_actb])
                for mb in range(16):
                    wb, rwb, kc = self.ws.get()
                    ps, rps = self.psr.next()
                    sg, rsg = self.sgr.next()
                    r0 = FM_ROW[gname[br]] + mb * 128
                    S.dma("sync", sg[:], self.zfm[r0:r0 + 128, t0:t0 + 512], reads=[self.dres["zfm"]], writes=[rsg])
                    for c in range(8):
                        mm(S, ps[:], wb[:, c, :], yb[:, c, :], c == 0, c == 7, [rwb, self.r_actb], [rps])
                    if br == 0:
                        tt(S, "vector", hb[:, mb, :], ps[:], sg[:], ALU.mult, [rps, rsg], [self.r_hb], acc=(mb > 0))
                    else:
                        tm, rtm = self.tmr.next()
                        tt(S, "vector", tm[:], ps[:], sg[:], ALU.mult, [rps, rsg], [rtm])
                        tt(S, "gpsimd", hb[:, mb, :], hb[:, mb, :], tm[:], ALU.add, [rtm, self.r_hb], [self.r_hb], acc=True)
            for mb in range(16):
                wb, rwb, kc = self.ws.get()
                ps, rps = self.psr.next()
                for c in range(16):
                    mm(S, ps[:], wb[:, c, :], hb[:, c, :], c == 0, c == 15, [rwb, self.r_hb], [rps])
                tt(S, "vector", xt[:, mb, :], xt[:, mb, :], ps[:], ALU.add, [rps, self.r_xt], [self.r_xt], acc=True)
            self.rmsnorm_tile(self.gf, l)
            for half in range(2):
                for fb in range(22):
                    wg, rwg, _ = self.ws.get()
                    psg, rpsg = self.psr.next()
                    for c in range(16):
                        mm(S, psg[:], wg[:, c, :], hb[:, c, :], c == 0, c == 15, [rwg, self.r_hb], [rpsg])
                    wu, rwu, _ = self.ws.get()
                    psu, rpsu = self.psr.next()
                    for c in range(16):
                        mm(S, psu[:], wu[:, c, :], hb[:, c, :], c == 0, c == 15, [rwu, self.r_hb], [rpsu])
                    tm, rtm = self.tmr.next()
                    act(S, tm[:], psg[:], AF.Silu, [rpsg], [rtm])
                    tt(S, "vector", actb[:, fb, :], tm[:], psu[:], ALU.mult, [rtm, rpsu], [self.r_actb], acc=(fb > 0))
                for mb in range(16):
                    ps, rps = self.psr.next()
                    w1, rw1, _ = self.ws.get()
                    for c in range(16):
                        mm(S, ps[:], w1[:, c, :], actb[:, c, :], c == 0, False, [rw1, self.r_actb], [rps], inc=False)
                    w2, rw2, _ = self.ws.get()
                    for c in range(6):
                        mm(S, ps[:], w2[:, c, :], actb[:, 16 + c, :], False, c == 5, [rw2, self.r_actb], [rps])
                    tt(S, "vector", xt[:, mb, :], xt[:, mb, :], ps[:], ALU.add, [rps, self.r_xt], [self.r_xt], acc=True)
            S.dma("gpsimd", xdst[:, t0:t0 + 512].rearrange("(c p) t -> p c t", p=128), xt[:],
                  reads=[self.r_xt], writes=[r_xdst], acc=True)

    def phase_C(self, l):
        S = self.S
        for a in (self.AR1, self.AR2):
            a.reset()
        A1, A2 = self.AR1, self.AR2
        rc = self.r_const
        sqm, r_sqm = A1.alloc([128, 16, 256], "sqm")
        memn, r_memn = A2.alloc([128, 16, 256], "memn")
        kT, r_kT = A2.alloc([128, 8, 256], "kT")
        vv, r_vv = A2.alloc([128, 2, 1024], "vv")
        qb, r_qb = A2.alloc([128, 2, 512], "qb")
        sq, r_sq = A2.alloc([128, 2, 512], "sq")
        pt, r_pt = A2.alloc([128, 2, 512], "pt")
        rs, r_rs = self.rs, self.r_rs
        S.dma("sync", memn[:], self.memT.rearrange("(c p) m -> p c m", p=128), writes=[r_memn])
        for mb in range(8):
            self.ws.push(self.w_mk[l, mb], 16)
        for mb in range(8):
            self.ws.push(self.w_mv[l, mb], 16)
        act(S, sqm[:], memn[:], AF.Square, [r_memn], [r_sqm])
        ps, rps = self.psr.next()
        for c in range(16):
            mm(S, ps[:, 0:256], self.ones, sqm[:, c, :], c == 0, c == 15, [rc, r_sqm], [rps])
        self.rstd(rs[:, 0:256], r_rs, ps[:, 0:256], rps, D)
        for c in range(16):
            stt(S, "vector", memn[:, c, :], memn[:, c, :], self.gme[:, l, c:c + 1], rs[:, 0:256], ALU.mult, ALU.mult,
                [r_memn, r_rs, rc], [r_memn], acc=True)
        for mb in range(8):
            wb, rwb, _ = self.ws.get()
            ps, rps = self.psr.next()
            for c in range(16):
                mm(S, ps[:, 0:256], wb[:, c, :], memn[:, c, :], c == 0, c == 15, [rwb, r_memn], [rps])
            cp(S, "scalar", kT[:, mb, :], ps[:, 0:256], [rps], [r_kT], acc=(mb > 0))
        for hh in range(4):
            act(S, sq[:, :, 0:256], kT[:, 2 * hh:2 * hh + 2, :], AF.Square, [r_kT], [r_sq])
            ps, rps = self.psr.next()
            for j in range(2):
                mm(S, ps[:, 0:256], self.ones, sq[:, j, 0:256], j == 0, j == 1, [rc, r_sq], [rps])
            self.rstd(rs[:, 0:256], r_rs, ps[:, 0:256], rps, 256)
            for j in range(2):
                stt(S, "vector", kT[:, 2 * hh + j, :], kT[:, 2 * hh + j, :], self.gsm[:, l, 2 + j:3 + j], rs[:, 0:256],
                    ALU.mult, ALU.mult, [r_kT, r_rs, rc], [r_kT], acc=True)
        for mb in range(8):
            wb, rwb, _ = self.ws.get()
            ps, rps = self.psr.next()
            for mt in range(2):
                for c in range(16):
                    mm(S, ps[:, mt * 128:(mt + 1) * 128], memn[:, c, mt * 128:(mt + 1) * 128], wb[:, c, :],
                       c == 0, c == 15, [rwb, r_memn], [rps], inc=(c == 15 and mt == 1))
            cp(S, "vector", vv[:, :, mb * 128:(mb + 1) * 128], ps[:, 0:256].rearrange("p (m c) -> p m c", m=2),
               [rps], [r_vv], acc=(mb > 0))
        r0 = FM_ROW["c_q"]
        for tt_ in range(self.NT):
            t0 = tt_ * 512
            for hh in range(4):
                S.dma("sync", qb[:], self.zfm[r0 + hh * 256:r0 + (hh + 1) * 256, t0:t0 + 512].rearrange("(j p) t -> p j t", p=128),
                      reads=[self.dres["zfm"]], writes=[r_qb])
                act(S, sq[:], qb[:], AF.Square, [r_qb], [r_sq])
                ps, rps = self.psr.next()
                for j in range(2):
                    mm(S, ps[:], self.ones, sq[:, j, :], j == 0, j == 1, [rc, r_sq], [rps])
                self.rstd(rs[:], r_rs, ps[:], rps, 256)
                for j in range(2):
                    stt(S, "vector", qb[:, j, :], qb[:, j, :], self.gsm[:, l, j:j + 1], rs[:], ALU.mult, ALU.mult,
                        [r_qb, r_rs, rc], [r_qb], acc=True)
                for mt in range(2):
                    ps, rps = self.psr.next()
                    for j in range(2):
                        mm(S, ps[:], kT[:, 2 * hh + j, mt * 128:(mt + 1) * 128], qb[:, j, :], j == 0, j == 1, [r_kT, r_qb], [rps])
                    act(S, pt[:, mt, :], ps[:], AF.Exp, [rps], [r_pt], scale=1.0 / 16.0, acc=(mt > 0))
                psd, rpsd = self.psr.next()
                for mt in range(2):
                    mm(S, psd[:], self.ones, pt[:, mt, :], mt == 0, mt == 1, [rc, r_pt], [rpsd])
                tm, rtm = self.tmr.next()
                S.op("vector", lambda e, o=tm[:], i=psd[:]: e.reciprocal(out=o, in_=i), [rpsd], [rtm])
                for j in range(2):
                    ps, rps = self.psr.next()
                    for mt in range(2):
                        mm(S, ps[:], vv[:, mt, hh * 256 + j * 128:hh * 256 + (j + 1) * 128], pt[:, mt, :], mt == 0, mt == 1,
                           [r_vv, r_pt], [rps])
                    ob, rob = self.obr.next()
                    tt(S, "vector", ob[:], ps[:], tm[:], ALU.mult, [rps, rtm], [rob])
                    S.dma("gpsimd", self.yT[2][hh * 256 + j * 128:hh * 256 + (j + 1) * 128, t0:t0 + 512], ob[:],
                          reads=[rob], writes=[self.dres["ycT"]], acc=True)

    def phase_M(self, l):
        S = self.S
        T = self.T
        rc = self.r_const
        for a in (self.AR0, self.AR1, self.AR2):
            a.reset()
        A0, A1, A2 = self.AR0, self.AR1, self.AR2
        ub = [A0.alloc([128, 515], "u%d" % i) for i in range(3)]
        ab = [A0.alloc([128, 512], "a%d" % i) for i in range(3)]
        k = 0
        for tt_ in range(self.NT):
            t0 = tt_ * 512
            for c in range(16):
                (u, ru), (a, ra) = ub[k % 3], ab[k % 3]
                k += 1
                if tt_ == 0:
                    S.op("gpsimd", lambda e, o=u[:, 0:3]: e.memset(o, 0.0), [], [ru])
                    S.dma("sync", u[:, 3:515], self.zfm[c * 128:(c + 1) * 128, 0:512], reads=[self.dres["zfm"]], writes=[ru], acc=True)
                else:
                    S.dma("sync", u[:], self.zfm[c * 128:(c + 1) * 128, t0 - 3:t0 + 512], reads=[self.dres["zfm"]], writes=[ru])
                cwl = self.cw[:, l, c, :]
                ts(S, "vector", a[:], u[:, 0:512], cwl[:, 0:1], None, ALU.mult, None, [ru, rc], [ra])
                for j in range(1, 4):
                    stt(S, "vector", a[:], u[:, j:j + 512], cwl[:, j:j + 1], a[:], ALU.mult, ALU.add, [ru, ra, rc], [ra])
                act(S, a[:], a[:], AF.Silu, [ra], [ra])
                S.dma("gpsimd", self.qkc[c * 128:(c + 1) * 128, t0:t0 + 512], a[:], reads=[ra], writes=[self.dres["qkc"]], acc=True)
        for a in (A0, A1, A2):
            a.reset()
        qTb = [A0.alloc([128, 8, 128], "qT%d" % i) for i in range(2)]
        kTb = [A0.alloc([128, 8, 128], "kT%d" % i) for i in range(2)]
        sob = [A1.alloc([128, 8, 128], "so%d" % i) for i in range(2)]
        yab = [A1.alloc([128, 8, 128], "ya%d" % i) for i in range(2)]
        grb = [A1.alloc([128, 8], "gr%d" % i) for i in range(2)]
        Cst, r_C = A2.alloc([128, 4, 2, 384], "Cst")
        vxb = [A2.alloc([128, 4, 384], "vx%d" % i) for i in range(2)]
        tmp = []
        for i in range(4):
            d = {}
            for nm, shp in (("nb", [128, 128]), ("ebt", [128, 128]), ("Dt", [128, 128]), ("W0", [128, 128]), ("Wt", [128, 128]),
                            ("qtl", [128, 2, 128]), ("dm", [128, 128]), ("rd", [128, 128]), ("hh", [128, 2, 128]),
                            ("sqh", [128, 2, 128]), ("rr", [128, 128]), ("kt", [128, 256]), ("yt", [128, 128])):
                d[nm] = (A2, A2, A0, A1)[i].alloc(shp, nm + str(i))
            tmp.append(d)
        gts = []
        for i in range(2):
            d = {}
            for nm, w in (("gz", 8), ("e1", 4), ("nlf", 4), ("imb", 4), ("bD", 4), ("gl", 4), ("gcol", 4), ("dcol", 4)):
                d[nm] = A1.alloc([128, w], nm + str(i))
            gts.append(d)
        S.op("vector", lambda e: e.memset(Cst[:], 0.0), [], [r_C])
        for i in range(2):
            S.op("gpsimd", lambda e, o=vxb[i][0][:, :, 256:384]: e.memset(o, 1.0), [], [vxb[i][1]])
        r_mo = FM_ROW["m_o"]
        hk = 0
        for ch in range(T // 128):
            t0 = ch * 128
            (qT, rq), (kT, rk), (so, rso), (ya, rya), (gr, rgr) = qTb[ch % 2], kTb[ch % 2], sob[ch % 2], yab[ch % 2], grb[ch % 2]
            vx, rvx = vxb[ch % 2]
            g = gts[ch % 2]
            S.dma("sync", qT[:], self.qkc[0:1024, t0:t0 + 128].rearrange("(c p) t -> p c t", p=128), reads=[self.dres["qkc"]], writes=[rq])
            S.dma("sync", kT[:], self.qkc[1024:2048, t0:t0 + 128].rearrange("(c p) t -> p c t", p=128), reads=[self.dres["qkc"]], writes=[rk])
            S.dma("sync", so[:], self.zfm[r_mo:r_mo + 1024, t0:t0 + 128].rearrange("(c p) t -> p c t", p=128), reads=[self.dres["zfm"]], writes=[rso])
            S.dma("sync", vx[:, :, 0:256], self.ztm[t0:t0 + 128, 0:1024].rearrange("p (h d) -> p h d", h=4),
                  reads=[self.dres["ztm"]], writes=[rvx], acc=True)
            S.dma("sync", gr[:], self.ztm[t0:t0 + 128, TM_COL["m_i"]:TM_COL["m_i"] + 8], reads=[self.dres["ztm"]], writes=[rgr])
            gz, rgz = g["gz"]; e1, re1 = g["e1"]; nlf, rnlf = g["nlf"]; imb, rimb = g["imb"]; bD, rbD = g["bD"]
            gl, rgl = g["gl"]; gcol, rgcol = g["gcol"]; dcol, rdcol = g["dcol"]
            tt(S, "vector", gz[:], gr[:], self.gsm[:, l, 16:24], ALU.add, [rgr, rc], [rgz])
            act(S, e1[:], gz[:, 4:8], AF.Exp, [rgz], [re1], scale=-1.0)
            act(S, nlf[:], e1[:], AF.Ln, [re1], [rnlf], bias=1.0)
            psg, rpsg = self.psr.next()
            mm(S, psg[:, 0:4], self.tri, nlf[:], True, True, [rc, rnlf], [rpsg])
            mm(S, psg[:, 4:8], self.ones, nlf[:], True, True, [rc, rnlf], [rpsg])
            tt(S, "vector", imb[:], gz[:, 0:4], psg[:, 0:4], ALU.add, [rgz, rpsg], [rimb])
            ts(S, "vector", bD[:], imb[:], -LN16, None, ALU.add, None, [rimb], [rbD])
            tt(S, "vector", gl[:], imb[:], psg[:, 4:8], ALU.subtract, [rimb, rpsg], [rgl])
            act(S, gcol[:], gl[:], AF.Exp, [rgl], [rgcol])
            act(S, dcol[:], psg[:, 4:8], AF.Exp, [rpsg], [rdcol], scale=-1.0)
            pa, rpa = self.pa, self.r_pa
            HS = [dict() for _ in range(4)]

            def st1(h):
                tp = tmp[h]
                nb, rnb = tp["nb"]
                cp(S, "vector", nb[:], nlf[:, h:h + 1].to_broadcast([128, 128]), [rnlf], [rnb])

            def st2(h):
                tp = tmp[h]
                nb, rnb = tp["nb"]
                mm(S, pa[h][:, 0:128], nb[:], self.tri, True, True, [rnb, rc], [rpa[h]])

            def st3(h):
                tp = tmp[h]
                ebt, rebt = tp["ebt"]; Dt, rDt = tp["Dt"]
                act(S, ebt[:], pa[h][:, 0:128], AF.Exp, [rpa[h]], [rebt], scale=-1.0, bias=-LN16)
                act(S, Dt[:], pa[h][:, 0:128], AF.Exp, [rpa[h], rbD], [rDt], scale=-1.0, bias=bD[:, h:h + 1])

            def st4(h):
                for j in range(2):
                    mm(S, pa[h][:, 128:256], kT[:, 2 * h + j, :], qT[:, 2 * h + j, :], j == 0, j == 1, [rk, rq], [rpa[h]])

            def st5(h):
                tp = tmp[h]
                ebt, rebt = tp["ebt"]; Dt, rDt = tp["Dt"]; W0, rW0 = tp["W0"]; qtl, rqtl = tp["qtl"]
                tt(S, "gpsimd", W0[:], Dt[:], self.tri, ALU.mult, [rDt, rc], [rW0])
                for j in range(2):
                    tt(S, "gpsimd", qtl[:, j, :], qT[:, 2 * h + j, :], ebt[:], ALU.mult, [rq, rebt], [rqtl], acc=(j > 0))

            def st6(h):
                tp = tmp[h]
                W0, rW0 = tp["W0"]; Wt, rWt = tp["Wt"]
                tt(S, "vector", Wt[:], W0[:], pa[h][:, 128:256], ALU.mult, [rW0, rpa[h]], [rWt])

            def st7(h):
                tp = tmp[h]
                Wt, rWt = tp["Wt"]; qtl, rqtl = tp["qtl"]
                psN, rpsN = self.psr.next()
                HS[h]["psN"] = (psN, rpsN)
                for jv in range(3):
                    o = psN[:, jv * 128:(jv + 1) * 128]
                    mm(S, o, vx[:, h, jv * 128:(jv + 1) * 128], Wt[:], True, False, [rvx, rWt], [rpsN], inc=False)
                    for j in range(2):
                        mm(S, o, Cst[:, h, j, jv * 128:(jv + 1) * 128], qtl[:, j, :], False, j == 1, [r_C, rqtl], [rpsN],
                           inc=(j == 1 and jv == 2))

            def st8(h):
                tp = tmp[h]
                dm, rdm = tp["dm"]
                psN, rpsN = HS[h]["psN"]
                act(S, dm[:], psN[:, 256:384], AF.Abs, [rpsN], [rdm])

            def st9(h):
                tp = tmp[h]
                dm, rdm = tp["dm"]; rd, rrd = tp["rd"]; hh_, rhh = tp["hh"]
                psN, rpsN = HS[h]["psN"]
                ts(S, "vector", dm[:], dm[:], 1.0, None, ALU.max, None, [rdm], [rdm])
                S.op("vector", lambda e, o=rd[:], i=dm[:]: e.reciprocal(out=o, in_=i), [rdm], [rrd])
                for jv in range(2):
                    tt(S, "vector", hh_[:, jv, :], psN[:, jv * 128:(jv + 1) * 128], rd[:], ALU.mult, [rpsN, rrd], [rhh], acc=(jv > 0))

            def st10(h):
                tp = tmp[h]
                hh_, rhh = tp["hh"]; sqh, rsqh = tp["sqh"]
                act(S, sqh[:], hh_[:], AF.Square, [rhh], [rsqh])

            def st11(h):
                tp = tmp[h]
                sqh, rsqh = tp["sqh"]
                for jv in range(2):
                    mm(S, pa[h][:, 0:128], self.ones, sqh[:, jv, :], jv == 0, jv == 1, [rc, rsqh], [rpa[h]])
                for j in range(2):
                    S.op("tensor", lambda e, o=pa[h][:, 128 + j * 128:256 + j * 128], i=kT[:, 2 * h + j, :], idn=self.ident:
                         e.transpose(out=o, in_=i, identity=idn), [rk, rc], [rpa[h]], acc=True)

            def st12(h):
                tp = tmp[h]
                rr, rrr = tp["rr"]
                self.rstd(rr[:], rrr, pa[h][:, 0:128], rpa[h], 256)

            def st13(h):
                tp = tmp[h]
                rr, rrr = tp["rr"]; hh_, rhh = tp["hh"]; yt, ryt = tp["yt"]; kt, rkt = tp["kt"]
                ts(S, "vector", kt[:], pa[h][:, 128:384], gcol[:, h:h + 1], None, ALU.mult, None, [rpa[h], rgcol], [rkt])
                for jv in range(2):
                    stt(S, "vector", yt[:], hh_[:, jv, :], self.gsm[:, l, 4 + 2 * h + jv:5 + 2 * h + jv], rr[:], ALU.mult, ALU.mult,
                        [rhh, rrr, rc], [ryt])
                    tt(S, "gpsimd", ya[:, 2 * h + jv, :], yt[:], so[:, 2 * h + jv, :], ALU.mult, [ryt, rso], [rya],
                       acc=(h > 0 or jv > 0))

            def st14(h):
                tp = tmp[h]
                kt, rkt = tp["kt"]
                for j in range(2):
                    psU, rpsU = self.psr.next()
                    mm(S, psU[:, 0:384], kt[:, j * 128:(j + 1) * 128], vx[:, h, :], True, True, [rkt, rvx], [rpsU])
                    stt(S, "vector", Cst[:, h, j, :], Cst[:, h, j, :], dcol[:, h:h + 1], psU[:, 0:384], ALU.mult, ALU.add,
                        [r_C, rdcol, rpsU], [r_C], acc=True)

            for stg in (st1, st2, st3, st4, st5, st6, st7, st8, st9, st10, st11, st12, st13, st14):
                for h in range(4):
                    stg(h)
            S.dma("gpsimd", self.yT[0][:, t0:t0 + 128].rearrange("(c p) t -> p c t", p=128), ya[:], reads=[rya],
                  writes=[self.dres["yaT"]], acc=True)

    def phase_N(self, l):
        S = self.S
        T, NQT, NCH, NCP, NC = self.T, self.NQT, self.NCH, self.NCP, self.NC
        rc = self.r_const
        A0, A1, A2 = self.AR0, self.AR1, self.AR2
        if not hasattr(self, "AR3"):
            self.AR3 = Arena(S, "ar3", 7168)
        A3 = self.AR3
        for a in (A0, A1, A2, A3):
            a.reset()
        zres = self.dres["zfm"]
        cosb = [A0.alloc([128, 512], "cos%d" % i) for i in range(2)]
        sinb = [A0.alloc([128, 512], "sin%d" % i) for i in range(2)]
        xin = [A0.alloc([128, 512], "xin%d" % i) for i in range(3)]
        sqb = [A0.alloc([128, 512], "sqb%d" % i) for i in range(2)]
        xnb = [A0.alloc([128, 512], "xnb%d" % i) for i in range(2)]
        t1b = [A0.alloc([128, 512], "t1b%d" % i) for i in range(2)]
        rrb = [A0.alloc([128, 512], "rrb%d" % i) for i in range(2)]
        items = [(FM_ROW["n_q"] + c * 128, 12, self.nqr[c * 128:(c + 1) * 128], "nqr") for c in range(8)]
        items += [(FM_ROW["n_ks"] + c * 128, 14, self.nkr[c * 128:(c + 1) * 128], "nkr") for c in range(2)]
        items += [(FM_ROW["n_kw"] + c * 128, 15, self.nkr[256 + c * 128:256 + (c + 1) * 128], "nkr") for c in range(2)]
        k = 0
        for tt_ in range(self.NT):
            t0 = tt_ * 512
            (cs, rcs), (sn, rsn) = cosb[tt_ % 2], sinb[tt_ % 2]
            S.dma("sync", cs[:], self.c_cos[:, t0:t0 + 512], writes=[rcs])
            S.dma("sync", sn[:], self.c_sin[:, t0:t0 + 512], writes=[rsn])
            for (row, gi, dst, dname) in items:
                (x, rx) = xin[k % 3]
                (sq, rsq), (xn, rxn), (t1, rt1), (rr, rrr) = sqb[k % 2], xnb[k % 2], t1b[k % 2], rrb[k % 2]
                k += 1
                S.dma("sync", x[:], self.zfm[row:row + 128, t0:t0 + 512], reads=[zres], writes=[rx])
                act(S, sq[:], x[:], AF.Square, [rx], [rsq])
                ps, rps = self.psr.next()
                mm(S, ps[:], self.bd64, sq[:], True, True, [rc, rsq], [rps])
                self.rstd(rr[:], rrr, ps[:], rps, 64)
                stt(S, "vector", xn[:], x[:], self.gsm[:, l, gi:gi + 1], rr[:], ALU.mult, ALU.mult, [rx, rrr, rc], [rxn])
                ps2, rps2 = self.psr.next()
                mm(S, ps2[:], self.rm, xn[:], True, True, [rc, rxn], [rps2])
                tt(S, "gpsimd", t1[:], xn[:], cs[:], ALU.mult, [rxn, rcs], [rt1])
                ob, rob = self.obr.next()
                tt(S, "vector", ob[:], ps2[:], sn[:], ALU.mult, [rps2, rsn], [rob])
                tt(S, "gpsimd", ob[:], ob[:], t1[:], ALU.add, [rob, rt1], [rob])
                S.dma("gpsimd", dst[:, t0:t0 + 512], ob[:], reads=[rob], writes=[self.dres[dname]], acc=True)
        for a in (A0, A1):
            a.reset()
        kcT, r_kcT = A2.alloc([64, 4, NCP], "kcT")
        vcx, r_vcx = A2.alloc([128, 4, NCH, 128], "vcx")
        S.op("gpsimd", lambda e: e.memset(vcx[:], 1.0), [], [r_vcx])
        w1, r_w1 = A0.alloc([64, 2, 32, 128], "w1")
        u, r_u = A1.alloc([64, T], "u")
        pe, r_pe = A3.alloc([64, 2, 32], "pe")
        w2, r_w2 = A3.alloc([128, 2, 64], "w2")
        bcol, r_bcol = A3.alloc([128, 2], "bcol")
        cosc, r_cosc = A3.alloc([64, NCP], "cosc")
        sinc, r_sinc = A3.alloc([64, NCP], "sinc")
        sqc, r_sqc = A3.alloc([64, NCP], "sqc")
        xnc, r_xnc = A3.alloc([64, NCP], "xnc")
        t1c, r_t1c = A3.alloc([64, NCP], "t1c")
        rrc, r_rrc = A3.alloc([64, NCP], "rrc")
        hs, r_hs = A3.alloc([128, NCP], "hs")
        for kind in range(2):
            S.dma("sync", w1[:, kind], self.n_w1[l, kind], writes=[r_w1], acc=(kind > 0))
        S.dma("sync", pe[:], self.n_pe[l].rearrange("k d j -> d k j"), writes=[r_pe])
        S.dma("sync", w2[:], self.n_w2[l].rearrange("k p m -> p k m"), writes=[r_w2])
        S.dma("sync", cosc[:], self.c_cosc, writes=[r_cosc])
        S.dma("sync", sinc[:], self.c_sinc, writes=[r_sinc])
        S.op("vector", lambda e: e.memset(kcT[:], 0.0), [], [r_kcT])
        S.op("gpsimd", lambda e: e.memset(hs[:], 0.0), [], [r_hs])
        for kind in range(2):
            ps, rps = self.psr.next()
            for j in range(32):
                mm(S, ps[:, 0:1], w1[:, kind, j, :], pe[:, kind, j:j + 1], j == 0, j == 31, [r_w1, r_pe], [rps])
            cp(S, "vector", bcol[:, kind:kind + 1], ps[:, 0:1], [rps], [r_bcol], acc=(kind > 0))
        u3 = u.rearrange("p (n s) -> p n s", s=16)
        for g in range(4):
            for kind in range(2):
                row = FM_ROW["n_kc" if kind == 0 else "n_vc"] + g * 64
                S.dma("sync", u[:], self.zfm[row:row + 64, :], reads=[zres], writes=[r_u])
                ps, rps = self.psr.next()
                for j in range(32):
                    a, jj = (0, j) if j < 16 else (1, j - 16)
                    mm(S, ps[:, 0:NC], w1[:, kind, j, :], u3[:, a:a + NC, jj], j == 0, j == 31, [r_w1, r_u], [rps])
                act(S, hs[:, 0:NC], ps[:, 0:NC], AF.Silu, [rps, r_bcol], [r_hs], bias=bcol[:, kind:kind + 1], acc=True)
                if kind == 0:
                    ps2, rps2 = self.psr.next()
                    mm(S, ps2[0:64, 0:NC], w2[:, 0, :], hs[:, 0:NC], True, True, [r_w2, r_hs], [rps2])
                    act(S, sqc[:, 0:NC], ps2[0:64, 0:NC], AF.Square, [rps2], [r_sqc])
                    ps3, rps3 = self.psr.next()
                    mm(S, ps3[0:64, 0:NC], self.ones[0:64, 0:64], sqc[:, 0:NC], True, True, [rc, r_sqc], [rps3])
                    self.rstd(rrc[:, 0:NC], r_rrc, ps3[0:64, 0:NC], rps3, 64)
                    stt(S, "vector", xnc[:, 0:NC], ps2[0:64, 0:NC], self.gsm[0:64, l, 13:14], rrc[:, 0:NC], ALU.mult, ALU.mult,
                        [rps2, r_rrc, rc], [r_xnc])
                    ps4, rps4 = self.psr.next()
                    mm(S, ps4[0:64, 0:NC], self.rm[0:64, 0:64], xnc[:, 0:NC], True, True, [rc, r_xnc], [rps4])
                    tt(S, "gpsimd", t1c[:, 0:NC], xnc[:, 0:NC], cosc[:, 0:NC], ALU.mult, [r_xnc, r_cosc], [r_t1c])
                    tt(S, "vector", kcT[:, g, 0:NC], ps4[0:64, 0:NC], sinc[:, 0:NC], ALU.mult, [rps4, r_sinc], [r_kcT], acc=True)
                    tt(S, "gpsimd", kcT[:, g, 0:NC], kcT[:, g, 0:NC], t1c[:, 0:NC], ALU.add, [r_kcT, r_t1c], [r_kcT], acc=True)
                else:
                    for ch in range(NCH):
                        ps5, rps5 = self.psr.next()
                        mm(S, ps5[:, 0:64], hs[:, ch * 128:(ch + 1) * 128], w2[:, 1, :], True, True, [r_hs, r_w2], [rps5])
                        cp(S, "scalar", vcx[:, g, ch, 0:64], ps5[:, 0:64], [rps5], [r_vcx], acc=True)
        for a in (A0, A1, A3):
            a.reset()
        NHB = (NQT + 31) // 32
        kT2, r_kT2 = A0.alloc([128, T], "kT2")
        vsx, r_vsx = A1.alloc([128, NQT, 128], "vsx")
        Et, r_Et = A2.alloc([128, NCH, 512], "Et")
        Qsb = [[A2.alloc([128, 4, 128], "Qs%d_%d" % (i, hb)) for hb in range(NHB)] for i in range(2)]
        vwx, r_vwx = A2.alloc([128, 5, 128], "vwx")
        rcpb = [A2.alloc([128, 512], "rcp%d" % i) for i in range(3)]
        off0 = A3.off
        gtb = [A3.alloc([64, 12, 128], "gt%d" % i, p0=64) for i in range(2)]
        kwb, r_kwb = A3.alloc([64, 5, 128], "kwb", p0=0, share=off0)
        negc, r_negc = A3.alloc([128, 512], "negc")
        nega, r_nega = A3.alloc([128, 512], "nega")
        qidx, r_qidx = A3.alloc([128, 512], "qidx")
        thr, r_thr = A3.alloc([128, NCH * NQT], "thr")
        mimp, r_mimp = A3.alloc([128, NCH, 128], "mimp")
        sbb = [A3.alloc([128, 128], "sb%d" % i) for i in range(2)]
        chs = []
        for i in range(2):
            d = {}
            for nm, w in (("score", 128), ("sc2", 128), ("selA", 128), ("selB", 128), ("rdT", 4), ("m8", 16)):
                d[nm] = A3.alloc([128, w], nm + str(i))
            chs.append(d)
        S.dma("sync", negc[:], self.c_negc, writes=[r_negc])
        S.dma("sync", nega[:], self.c_nega, writes=[r_nega])
        S.dma("sync", qidx[:], self.c_qidx, writes=[r_qidx])
        S.dma("sync", thr[:], self.c_thr, writes=[r_thr])
        S.dma("sync", mimp[:], self.c_mimp, writes=[r_mimp])
        S.dma("sync", kT2[64:128, :], self.c_g2, writes=[r_kT2])
        S.op("gpsimd", lambda e: e.memset(vsx[:], 1.0), [], [r_vsx])
        S.op("gpsimd", lambda e: e.memset(vwx[:], 1.0), [], [r_vwx])
        ngrow = FM_ROW["n_g"]
        pa, rpa = self.pa, self.r_pa
        idhi = self.ident[64:128, 64:128]
        iters = [(g, qt) for g in range(4) for qt in range(NQT)]
        n_it = len(iters)
        st = {}

        def group_loads(g):
            S.dma("sync", kT2[0:64, :], self.nkr[g * 64:(g + 1) * 64, :], reads=[self.dres["nkr"]], writes=[r_kT2], acc=True)
            c0 = TM_COL["n_vs"] + g * 64
            for k8 in range(0, NQT, 8):
                S.dma("sync", vsx[:, k8:k8 + 8, 0:64], self.ztm[k8 * 128:(k8 + 8) * 128, c0:c0 + 64].rearrange("(k p) d -> p k d", p=128),
                      reads=[self.dres["ztm"]], writes=[r_vsx], acc=True)

        def load(i):
            g, qt = iters[i]
            t0 = qt * 128
            d = {"Qs": Qsb[i % 2], "sb": sbb[i % 2], "gt": gtb[i % 2], "ch": chs[i % 2], "yacc": self.tmr.next(),
                 "nhb": qt // 32 + 1}
            st[i] = d
            qsrc = self.nqr[g * 256:(g + 1) * 256, t0:t0 + 128].rearrange("(r d) q -> d r q", d=64)
            for hb in range(d["nhb"]):
                S.dma("sync", d["Qs"][hb][0][0:64], qsrc, reads=[self.dres["nqr"]], writes=[d["Qs"][hb][1]])
            gt, rgt = d["gt"]
            S.dma("sync", gt[:], self.zfm[ngrow + g * 12:ngrow + (g + 1) * 12, t0:t0 + 128].partition_broadcast(64),
                  reads=[zres], writes=[rgt])
            S.dma("sync", d["sb"][0][:], self.c_sbias[qt], writes=[d["sb"][1]])

        def finA(br, i, num, rnum):
            d = st[i]
            gt, rgt = d["gt"]
            gt4 = gt.rearrange("p (r j) q -> p r j q", j=3)
            rcp, rrcp = rcpb[br]
            hi = rcp[64:128, :]
            ts(S, "vector", hi, num[64:128, :], 1e-30, None, ALU.max, None, [rnum], [rrcp])
            S.op("vector", lambda e, o=hi: e.reciprocal(out=o, in_=o), [rrcp], [rrcp])
            tt(S, "gpsimd", hi.rearrange("p (r q) -> p r q", r=4), hi.rearrange("p (r q) -> p r q", r=4),
               gt4[:, :, br, :], ALU.mult, [rrcp, rgt], [rrcp])

        def finB(br, i, num, rnum, first):
            d = st[i]
            yacc, ryacc = d["yacc"]
            rcp, rrcp = rcpb[br]
            psf, rpsf = self.psr.next()
            mm(S, psf[0:64, :], idhi, rcp[64:128, :], True, True, [rc, rrcp], [rpsf])
            act(S, rcp[0:64, :], psf[0:64, :], AF.Copy, [rpsf], [rrcp], acc=True)
            if first:
                tt(S, "vector", yacc[0:64, :], num[0:64, :], rcp[0:64, :], ALU.mult, [rnum, rrcp], [ryacc])
            else:
                tt(S, "vector", rcp[0:64, :], num[0:64, :], rcp[0:64, :], ALU.mult, [rnum, rrcp], [rrcp])
                tt(S, "gpsimd", yacc[0:64, :], yacc[0:64, :], rcp[0:64, :], ALU.add, [ryacc, rrcp], [ryacc])

        def store(i):
            g, qt = iters[i]
            t0 = qt * 128
            yacc, ryacc = st[i]["yacc"]
            S.dma("gpsimd", self.yT[1][g * 256:(g + 1) * 256, t0:t0 + 128].rearrange("(r d) q -> d r q", d=64),
                  yacc[0:64, :].rearrange("p (r q) -> p r q", r=4), reads=[ryacc], writes=[self.dres["ybT"]], acc=True)
            del st[i]

        def cmpA(i):
            g, qt = iters[i]
            t0 = qt * 128
            d = st[i]
            Q0, rQ0 = d["Qs"][0]
            Qlo = Q0[0:64].rearrange("p r q -> p (r q)")
            sb, rsb = d["sb"]
            c = d["ch"]
            score, r_score = c["score"]; sc2, r_sc2 = c["sc2"]; selA, r_selA = c["selA"]; selB, r_selB = c["selB"]
            rdT, r_rdT = c["rdT"]; m8, r_m8 = c["m8"]
            jmax = min((8 * qt + 6) // 128, NCH - 1)
            for j in range(jmax + 1):
                ps, rps = self.psr.next()
                mm(S, ps[:], kcT[:, g, j * 128:(j + 1) * 128], Qlo, True, True, [r_kcT, rQ0], [rps])
                act(S, Et[:, j, :], ps[:], AF.Exp, [rps], [r_Et], scale=0.125, acc=(j > 0))
                if 16 * (128 * j + 127) + 31 > t0:
                    stt(S, "vector", Et[:, j, :], qidx[:], thr[:, j * NQT + qt:j * NQT + qt + 1], Et[:, j, :], ALU.is_ge, ALU.mult,
                        [r_qidx, r_thr, r_Et], [r_Et], acc=True)
            for j in range(jmax + 1):
                mm(S, pa[0][:], vcx[:, g, j, :], Et[:, j, :], j == 0, j == jmax, [r_vcx, r_Et], [rpa[0]])
            for r in range(4):
                for j in range(jmax + 1):
                    mm(S, pa[1][:, r * 128:(r + 1) * 128], Et[:, j, r * 128:(r + 1) * 128], mimp[:, j, :], j == 0, j == jmax,
                       [r_Et, r_mimp], [rpa[1]], inc=(j == jmax and r == 3))
            psd, rpsd = self.psr.next()
            for r in range(4):
                for j in range(jmax + 1):
                    mm(S, psd[:, r:r + 1], Et[:, j, r * 128:(r + 1) * 128], self.ones[:, 0:1], j == 0, j == jmax,
                       [r_Et, rc], [rpsd], inc=(j == jmax and r == 3))
            ts(S, "vector", rdT[:], psd[:, 0:4], 1e-30, None, ALU.max, None, [rpsd], [r_rdT])
            S.op("vector", lambda e, o=rdT[:]: e.reciprocal(out=o, in_=o), [r_rdT], [r_rdT])
            stt(S, "vector", score[:], pa[1][:, 0:128], rdT[:, 0:1], sb[:], ALU.mult, ALU.add, [rpa[1], r_rdT, rsb], [r_score])
            for r in range(1, 4):
                stt(S, "vector", score[:], pa[1][:, r * 128:(r + 1) * 128], rdT[:, r:r + 1], score[:], ALU.mult, ALU.add,
                    [rpa[1], r_rdT, r_score], [r_score])
            S.op("vector", lambda e, o=m8[:, 0:8], s_=score[:]: e.max(out=o, in_=s_), [r_score], [r_m8])
            S.op("vector", lambda e, o=sc2[:], m=m8[:, 0:8], s_=score[:]: e.match_replace(out=o, in_to_replace=m, in_values=s_, imm_value=-1e9),
                 [r_score, r_m8], [r_sc2])
            S.op("vector", lambda e, o=m8[:, 8:16], s_=sc2[:]: e.max(out=o, in_=s_), [r_sc2], [r_m8], acc=True)
            ts(S, "vector", selA[:], score[:], m8[:, 15:16], NEG, ALU.is_lt, ALU.mult, [r_score, r_m8], [r_selA])
            ts(S, "gpsimd", selB[:, 64:128], score[:, 0:64], m8[:, 15:16], NEG, ALU.is_lt, ALU.mult, [r_score, r_m8], [r_selB])
            ts(S, "gpsimd", selB[:, 0:64], score[:, 64:128], m8[:, 15:16], NEG, ALU.is_lt, ALU.mult, [r_score, r_m8], [r_selB], acc=True)
            finA(0, i, pa[0], rpa[0])

        def trans(i):
            d = st[i]
            c = d["ch"]
            selA, r_selA = c["selA"]; selB, r_selB = c["selB"]
            for hb in range(d["nhb"]):
                psT, rpsT = self.psr.next()
                src, rsrc_ = (selB, r_selB) if hb == 0 else (selA, r_selA)
                tr(S, psT[:, 0:128], src[:], self.ident, [rsrc_, rc], [rpsT])
                Qh, rQh = d["Qs"][hb]
                for r in range(4):
                    cp(S, "scalar" if r % 2 else "vector", Qh[64:128, r, :], psT[64:128, 0:128], [rpsT], [rQh], acc=True)

        def win(i):
            g, qt = iters[i]
            d = st[i]
            Q0, rQ0 = d["Qs"][0]
            Qlo = Q0[0:64].rearrange("p r q -> p (r q)")
            cw0 = TM_COL["n_vw"] + g * 64
            k0 = max(0, qt - 4)
            nk = qt - k0 + 1
            S.dma("sync", vwx[:, 0:nk, 0:64], self.ztm[k0 * 128:(qt + 1) * 128, cw0:cw0 + 64].rearrange("(k p) d -> p k d", p=128),
                  reads=[self.dres["ztm"]], writes=[r_vwx], acc=True)
            S.dma("sync", kwb[:, 0:nk, :], self.nkr[256 + g * 64:256 + (g + 1) * 64, k0 * 128:(qt + 1) * 128].rearrange("d (k p) -> d k p", p=128),
                  reads=[self.dres["nkr"]], writes=[r_kwb])
            for kt in range(k0, qt + 1):
                ps, rps = self.psr.next()
                last_c = (kt == qt)
                last_a = (kt == qt - 4)
                mm(S, ps[:], kwb[:, kt - k0, :], Qlo, True, not (last_c or last_a), [r_kwb, rQ0], [rps])
                if last_c:
                    mm(S, ps[:], self.ident, negc[:], False, True, [rc, r_negc], [rps])
                if last_a:
                    mm(S, ps[:], self.ident, nega[:], False, True, [rc, r_nega], [rps])
                pt, rpt = self.obr.next()
                act(S, pt[:], ps[:], AF.Exp, [rps], [rpt], scale=0.125)
                mm(S, pa[3][:], vwx[:, kt - k0, :], pt[:], kt == k0, kt == qt, [r_vwx, rpt], [rpa[3]])

        def sel(i):
            g, qt = iters[i]
            d = st[i]
            for kt in range(qt + 1):
                Qh, rQh = d["Qs"][kt // 32]
                ps, rps = self.psr.next()
                mm(S, ps[:], kT2[:, kt * 128:(kt + 1) * 128], Qh[:].rearrange("p r q -> p (r q)"), True, kt != qt, [r_kT2, rQh], [rps])
                if kt == qt:
                    mm(S, ps[:], self.ident, negc[:], False, True, [rc, r_negc], [rps])
                pt, rpt = self.obr.next()
                act(S, pt[:], ps[:], AF.Exp, [rps], [rpt], scale=0.125)
                mm(S, pa[2][:], vsx[:, kt, :], pt[:], kt == 0, kt == qt, [r_vsx, rpt], [rpa[2]])

        load(0)
        cmpA(0)
        finB(0, 0, pa[0], rpa[0], True)
        for i in range(n_it):
            g, qt = iters[i]
            if qt == 0:
                group_loads(g)
            win(i)
            finA(2, i, pa[3], rpa[3])
            if i > 0:
                finB(1, i - 1, pa[2], rpa[2], False)
                store(i - 1)
            trans(i)
            if i + 1 < n_it:
                load(i + 1)
                cmpA(i + 1)
            finB(2, i, pa[3], rpa[3], False)
            sel(i)
            if i + 1 < n_it:
                finB(0, i + 1, pa[0], rpa[0], True)
            finA(1, i, pa[2], rpa[2])
        finB(1, n_it - 1, pa[2], rpa[2], False)
        store(n_it - 1)


def build(T, L, dbg=(), phases="ACMNE"):
    P = Prog(T, L, dbg)
    S = P.S
    src, rsrc = P.xT, Res("xT")
    for l in range(L):
        if l == L - 1:
            dst, rdst = P.outT, P.dres["outT"]
        else:
            dst, rdst = P.xbuf[l % 2], P.dres["xs%d" % (l % 2)]
        if "A" in phases:
            P.phase_A(l, src, rsrc)
        if "C" in phases:
            P.phase_C(l)
        if "M" in phases:
            P.phase_M(l)
        if "N" in phases:
            P.phase_N(l)
        if "E" in phases:
            P.phase_E(l, src, rsrc, dst, rdst)
        src, rsrc = dst, rdst
    fin = [P.dres["outT"]] + [P.dres[k] for k in P.dbg if k in P.dres]
    S.finish(fin)
    return P


def tile_w(W):
    K, M = W.shape
    return np.ascontiguousarray(W.reshape(K // 128, 128, M // 128, 128).transpose(2, 1, 0, 3))


def colmajor(g, L, nchunk):
    return np.ascontiguousarray(g.reshape(L, nchunk, 128).transpose(0, 2, 1))


def prep_weights(inp, L):
    out = {}
    f = np.float32
    w_in = inp["w_in"]
    cols = []
    for n in FM_SEGS:
        o, s = _off[n]
        cols.append(w_in[:, :, o:o + s])
    cols.append(np.zeros((L, D, N_FM_BLK * 128 - N_FM), f))
    for n in TM_SEGS:
        o, s = _off[n]
        cols.append(w_in[:, :, o:o + s])
    cols.append(np.zeros((L, D, N_TM_BLK * 128 - N_TM), f))
    wr = np.concatenate(cols, axis=2)
    out["w_in"] = np.stack([tile_w(wr[l]) for l in range(L)])
    out["gmix"] = colmajor(inp["norm_mix_g"], L, 16)
    out["gffn"] = colmajor(inp["norm_ffn_g"], L, 16)
    out["gmem"] = colmajor(inp["norm_mem_g"], L, 16)
    out["w_up"] = np.stack([np.stack([tile_w(inp[k][l]) for k in ("w_up_a", "w_up_b", "w_up_c")]) for l in range(L)])
    out["w_out"] = np.stack([tile_w(inp["w_out"][l]) for l in range(L)])
    out["w_fg"] = np.stack([tile_w(inp["w_ffn_gate"][l]) for l in range(L)])
    out["w_fu"] = np.stack([tile_w(inp["w_ffn_up"][l]) for l in range(L)])
    out["w_fd"] = np.stack([tile_w(inp["w_ffn_down"][l]) for l in range(L)])
    out["w_mk"] = np.stack([tile_w(inp["w_mem_k"][l]) for l in range(L)])
    out["w_mv"] = np.stack([tile_w(inp["w_mem_v"][l]) for l in range(L)])
    out["g_cq"] = colmajor(inp["c_q_norm_g"], L, 2)
    out["g_ck"] = colmajor(inp["c_k_norm_g"], L, 2)
    out["m_cw"] = np.ascontiguousarray(inp["m_conv_w"].reshape(L, 4, 16, 128).transpose(0, 3, 2, 1))
    out["m_b8"] = np.concatenate([inp["m_i_bias"], inp["m_f_bias"]], axis=1).reshape(L, 1, 8).astype(f)
    out["m_gn"] = colmajor(inp["m_norm_g"].reshape(L, 1024), L, 8)
    g4 = np.stack([inp["n_q_norm_g"], inp["n_kc_norm_g"], inp["n_ks_norm_g"], inp["n_kw_norm_g"]], axis=2)
    out["g_n4"] = np.ascontiguousarray(np.concatenate([g4, g4], axis=1))
    out["n_w1"] = np.stack([np.stack([inp[k][l].reshape(32, 64, 128).transpose(1, 0, 2) for k in ("n_cmp_w1_k", "n_cmp_w1_v")])
                            for l in range(L)])
    out["n_pe"] = np.stack([np.stack([inp[k][l].T for k in ("n_cmp_pe_k", "n_cmp_pe_v")]) for l in range(L)])
    out["n_w2"] = np.stack([np.stack([inp[k][l] for k in ("n_cmp_w2_k", "n_cmp_w2_v")]) for l in range(L)])
    return {k: np.ascontiguousarray(v, dtype=f) for k, v in out.items()}


def consts(T):
    f = np.float32
    NQT = T // 128
    NC = T // 16 - 1
    NCH = (NC + 127) // 128
    NCP = NCH * 128
    c = {}
    c128 = np.zeros((5, 128, 128), f)
    c128[C_ONES] = 1.0
    c128[C_ID] = np.eye(128, dtype=f)
    c128[C_TRI] = np.triu(np.ones((128, 128), f))
    bd = np.zeros((128, 128), f)
    bd[:64, :64] = 1.0
    bd[64:, 64:] = 1.0
    c128[C_BD64] = bd
    rm = np.zeros((128, 128), f)
    for hb in (0, 64):
        for m in range(64):
            if m < 32:
                rm[hb + m + 32, hb + m] = -1.0
            else:
                rm[hb + m - 32, hb + m] = 1.0
    c128[C_RM] = rm
    c["c128"] = c128
    inv = np.power(f(10000.0), -np.arange(32, dtype=f) * f(2.0) / f(64)).astype(f)
    pos = np.arange(T, dtype=f)
    ang = (pos[:, None] * inv[None, :]).astype(f)
    cs = np.cos(ang).astype(f).T
    sn = np.sin(ang).astype(f).T
    c["c_cos"] = np.ascontiguousarray(np.concatenate([cs, cs, cs, cs], axis=0))
    c["c_sin"] = np.ascontiguousarray(np.concatenate([sn, sn, sn, sn], axis=0))
    cend = (np.arange(NCP, dtype=f) * f(16) + f(31)).astype(f)
    angc = (cend[:, None] * inv[None, :]).astype(f)
    csc = np.cos(angc).astype(f).T
    snc = np.sin(angc).astype(f).T
    c["c_cosc"] = np.ascontiguousarray(np.concatenate([csc, csc], axis=0))
    c["c_sinc"] = np.ascontiguousarray(np.concatenate([snc, snc], axis=0))
    mimp = np.zeros((NCP, 128), f)
    for n in range(NC):
        mimp[n, n // 4] += 1.0
        mimp[n, (n + 1) // 4] += 1.0
    c["c_mimp"] = np.ascontiguousarray(mimp.reshape(NCH, 128, 128).transpose(1, 0, 2))
    g2 = np.zeros((64, T), f)
    for b in range(T // 64):
        g2[b % 64, b * 64:(b + 1) * 64] = 1.0
    c["c_g2"] = g2
    sb = np.zeros((NQT, 128, 128), f)
    blk = np.arange(128)
    for qt in range(NQT):
        for q in range(128):
            cur = (qt * 128 + q) // 64
            row = np.where(blk > cur, -1.0, 0.0)
            row[(blk == 0) | (blk == cur) | (blk == cur - 1)] = 1.0e4
            sb[qt, q] = row
    c["c_sbias"] = sb
    thr = np.zeros((128, NCH * NQT), f)
    p = np.arange(128)
    for j in range(NCH):
        for qt in range(NQT):
            thr[:, j * NQT + qt] = 16 * (128 * j + p) + 31 - 128 * qt
    c["c_thr"] = thr
    c["c_qidx"] = np.ascontiguousarray(np.tile(np.arange(128, dtype=f)[None, :], (128, 4)))
    kk = np.arange(128)[:, None]
    qq = np.arange(128)[None, :]
    c["c_negc"] = np.ascontiguousarray(np.tile(np.where(kk > qq, NEG, 0.0).astype(f), (1, 4)))
    c["c_nega"] = np.ascontiguousarray(np.tile(np.where(kk <= qq, NEG, 0.0).astype(f), (1, 4)))
    return c


_CACHE = {}


def kernel(**inputs):
    inp = {k: np.asarray(v) for k, v in inputs.items()}
    B, T, _ = inp["x"].shape
    L = inp["w_in"].shape[0]
    key = (T, L)
    if key not in _CACHE:
        _CACHE[key] = build(T, L)
    P = _CACHE[key]
    w = prep_weights(inp, L)
    w.update(consts(T))
    maps = []
    for b in range(B):
        m = {k: w[k] for k in P.inputs if k in w}
        m["xT"] = np.ascontiguousarray(inp["x"][b].T, dtype=np.float32)
        m["memT"] = np.ascontiguousarray(inp["mem"][b].T, dtype=np.float32)
        maps.append(m)
    res = run_bass_kernel_spmd(P.nc, maps, core_ids=list(range(B)))
    out = np.stack([np.asarray(res.results[b]["outT"]).T for b in range(B)])
    return np.ascontiguousarray(out, dtype=np.float32)
```
